# Optimizing a Trainium2 kernel written in Bass

```python
import jax, jax.numpy as jnp
from jax import lax
import numpy as np


D_MODEL = 1024
BATCH = 8
SEQ = 2048
DEPTH = 2

MIX_WIDTH = 256
CONV_K = 3
GMLP_GROUPS = 4
GMLP_CHUNK = 128
HEAD_DIM = 64
SB_HEADS = MIX_WIDTH // HEAD_DIM
FOX_HEADS = MIX_WIDTH // HEAD_DIM
Q_BLOCK = 128
N_BRANCH = 4
FFN_HIDDEN = -(-8 * D_MODEL // (3 * 256)) * 256
EPS = 1e-6
IN_SPLIT_SIZES = (MIX_WIDTH, MIX_WIDTH, MIX_WIDTH, 2 * MIX_WIDTH, MIX_WIDTH, MIX_WIDTH, MIX_WIDTH,
                  MIX_WIDTH, MIX_WIDTH, MIX_WIDTH, FOX_HEADS, N_BRANCH * D_MODEL)
N_IN = sum(IN_SPLIT_SIZES)

kernel_name = 'hybrid_gated_conv_gmlp_stickbreak_fox'


def _rms_norm(x, g):
    xf = x.astype(jnp.float32)
    y = xf * lax.rsqrt(jnp.mean(xf * xf, axis=-1, keepdims=True) + EPS)
    return (y * g.astype(jnp.float32)).astype(x.dtype)


def _layer_norm(x, g, b):
    xf = x.astype(jnp.float32)
    mu = jnp.mean(xf, axis=-1, keepdims=True)
    var = jnp.mean(jnp.square(xf - mu), axis=-1, keepdims=True)
    y = (xf - mu) * lax.rsqrt(var + EPS)
    return (y * g.astype(jnp.float32) + b.astype(jnp.float32)).astype(x.dtype)


def _split_heads(t, n_heads):
    b, s, _ = t.shape
    return t.reshape(b, s, n_heads, HEAD_DIM).transpose(0, 2, 1, 3)


def _merge_heads(t):
    b, h, s, d = t.shape
    return t.transpose(0, 2, 1, 3).reshape(b, s, h * d)


def _short_conv_mixer(b_gate, c_gate, xt, w_conv):
    xc = c_gate * xt
    y = lax.conv_general_dilated(
        xc, w_conv[:, None, :].astype(xc.dtype), window_strides=(1,),
        padding=[(CONV_K - 1, 0)], dimension_numbers=('NWC', 'WIO', 'NWC'),
        feature_group_count=MIX_WIDTH)
    return b_gate * y


def _spatial_gating_mixer(uv, w_s, b_s, ln_g, ln_b):
    u, v = jnp.split(jax.nn.gelu(uv), 2, axis=-1)
    v = _layer_norm(v, ln_g, ln_b)
    bsz, s, _ = v.shape
    v = v.reshape(bsz, s // GMLP_CHUNK, GMLP_CHUNK, GMLP_GROUPS, MIX_WIDTH // GMLP_GROUPS)
    w_causal = jnp.tril(w_s).astype(v.dtype)
    mixed = jnp.einsum('gts,bcsge->bctge', w_causal, v) + b_s.T[None, None, :, :, None].astype(v.dtype)
    return u * mixed.reshape(bsz, s, MIX_WIDTH)


def _stick_breaking_attention(q, k, v):
    seq = q.shape[2]
    scale = HEAD_DIM ** -0.5
    outs = []
    for i in range(seq // Q_BLOCK):
        start, end = i * Q_BLOCK, (i + 1) * Q_BLOCK
        z = jnp.einsum('bhqd,bhkd->bhqk', q[:, :, start:end], k[:, :, :end]).astype(jnp.float32) * scale
        q_pos = start + jnp.arange(Q_BLOCK)[:, None]
        k_pos = jnp.arange(end)[None, :]
        strict = k_pos < q_pos
        log_1m_beta = jnp.where(strict, jax.nn.log_sigmoid(-z), 0.0)
        later = lax.cumsum(log_1m_beta, axis=3, reverse=True) - log_1m_beta
        w = jnp.where(strict, jnp.exp(jax.nn.log_sigmoid(z) + later), 0.0)
        outs.append(jnp.einsum('bhqk,bhkd->bhqd', w.astype(v.dtype), v[:, :, :end]))
    return jnp.concatenate(outs, axis=2)


def _forgetting_attention(q, k, v, log_f_cum):
    seq = q.shape[2]
    scale = HEAD_DIM ** -0.5
    neg = jnp.finfo(jnp.float32).min
    outs = []
    for i in range(seq // Q_BLOCK):
        start, end = i * Q_BLOCK, (i + 1) * Q_BLOCK
        logits = jnp.einsum('bhqd,bhkd->bhqk', q[:, :, start:end], k[:, :, :end]).astype(jnp.float32) * scale
        logits = logits + log_f_cum[:, :, start:end, None] - log_f_cum[:, :, None, :end]
        causal = jnp.arange(end)[None, :] <= (start + jnp.arange(Q_BLOCK)[:, None])
        p = jax.nn.softmax(jnp.where(causal, logits, neg), axis=-1)
        outs.append(jnp.einsum('bhqk,bhkd->bhqd', p.astype(v.dtype), v[:, :, :end]))
    return jnp.concatenate(outs, axis=2)


def _mixer_block(xn, w_in, w_conv, w_s, b_s, ln_g, ln_b, q_norm_g, k_norm_g, b_f, w_branch, w_out):
    bsz, s, _ = xn.shape
    proj = xn @ w_in
    idx, acc = [], 0
    for size in IN_SPLIT_SIZES[:-1]:
        acc += size
        idx.append(acc)
    (cb, cc, cx, uv, sb_q, sb_k, sb_v, fx_q, fx_k, fx_v, f_raw, gate_raw) = jnp.split(proj, idx, axis=-1)

    y_a = _short_conv_mixer(cb, cc, cx, w_conv)
    y_b = _spatial_gating_mixer(uv, w_s, b_s, ln_g, ln_b)
    y_c = _merge_heads(_stick_breaking_attention(
        _split_heads(sb_q, SB_HEADS), _split_heads(sb_k, SB_HEADS), _split_heads(sb_v, SB_HEADS)))
    fq = _rms_norm(_split_heads(fx_q, FOX_HEADS), q_norm_g)
    fk = _rms_norm(_split_heads(fx_k, FOX_HEADS), k_norm_g)
    log_f = jax.nn.log_sigmoid((f_raw + b_f).astype(jnp.float32)).transpose(0, 2, 1)
    y_d = _merge_heads(_forgetting_attention(fq, fk, _split_heads(fx_v, FOX_HEADS), lax.cumsum(log_f, axis=2)))

    ys = jnp.stack([y_a, y_b, y_c, y_d], axis=2)
    branches = jnp.einsum('bsnc,ncd->bsnd', ys, w_branch)
    gates = jax.nn.sigmoid(gate_raw.reshape(bsz, s, N_BRANCH, D_MODEL).astype(jnp.float32)).astype(xn.dtype)
    merged = jnp.sum(gates * branches, axis=2)
    return merged @ w_out


def _swiglu(x, w_in, w_out):
    g, u = jnp.split(x @ w_in, 2, axis=-1)
    return (jax.nn.silu(g) * u) @ w_out


def setup_inputs(seed: int = 0) -> dict:
    key = jax.random.key(seed)
    ks = jax.random.split(key, 16)
    f32 = jnp.float32
    nrm = lambda k, shape: jax.random.normal(k, shape, f32)
    return {
        'x': nrm(ks[0], (BATCH, SEQ, D_MODEL)),
        'norm_mix_g': 1.0 + 0.05 * nrm(ks[1], (DEPTH, D_MODEL)),
        'w_in': nrm(ks[2], (DEPTH, D_MODEL, N_IN)) * D_MODEL ** -0.5,
        'w_conv': nrm(ks[3], (DEPTH, CONV_K, MIX_WIDTH)) * CONV_K ** -0.5,
        'w_spatial': nrm(ks[4], (DEPTH, GMLP_GROUPS, GMLP_CHUNK, GMLP_CHUNK)) * GMLP_CHUNK ** -0.5,
        'b_spatial': 1.0 + 0.1 * nrm(ks[5], (DEPTH, GMLP_GROUPS, GMLP_CHUNK)),
        'gmlp_ln_g': 1.0 + 0.05 * nrm(ks[6], (DEPTH, MIX_WIDTH)),
        'gmlp_ln_b': 0.02 * nrm(ks[7], (DEPTH, MIX_WIDTH)),
        'fox_q_norm_g': 1.0 + 0.05 * nrm(ks[8], (DEPTH, HEAD_DIM)),
        'fox_k_norm_g': 1.0 + 0.05 * nrm(ks[9], (DEPTH, HEAD_DIM)),
        'fox_forget_b': 2.0 + 0.5 * nrm(ks[10], (DEPTH, FOX_HEADS)),
        'w_branch': nrm(ks[11], (DEPTH, N_BRANCH, MIX_WIDTH, D_MODEL)) * MIX_WIDTH ** -0.5,
        'w_out': nrm(ks[12], (DEPTH, D_MODEL, D_MODEL)) * D_MODEL ** -0.5,
        'norm_ffn_g': 1.0 + 0.05 * nrm(ks[13], (DEPTH, D_MODEL)),
        'w_ffn_in': nrm(ks[14], (DEPTH, D_MODEL, 2 * FFN_HIDDEN)) * D_MODEL ** -0.5,
        'w_ffn_out': nrm(ks[15], (DEPTH, FFN_HIDDEN, D_MODEL)) * FFN_HIDDEN ** -0.5,
    }


def reference(x, norm_mix_g, w_in, w_conv, w_spatial, b_spatial, gmlp_ln_g, gmlp_ln_b,
              fox_q_norm_g, fox_k_norm_g, fox_forget_b, w_branch, w_out, norm_ffn_g,
              w_ffn_in, w_ffn_out):
    h = x
    for l in range(DEPTH):
        xn = _rms_norm(h, norm_mix_g[l])
        h = h + _mixer_block(xn, w_in[l], w_conv[l], w_spatial[l], b_spatial[l], gmlp_ln_g[l],
                             gmlp_ln_b[l], fox_q_norm_g[l], fox_k_norm_g[l], fox_forget_b[l],
                             w_branch[l], w_out[l])
        h = h + _swiglu(_rms_norm(h, norm_ffn_g[l]), w_ffn_in[l], w_ffn_out[l])
    return h
```

```python
import numpy as np
from contextlib import ExitStack
from concourse.bass_utils import run_bass_kernel_spmd

import concourse.bass as bass
import concourse.mybir as mybir

F32 = mybir.dt.float32
BF16 = mybir.dt.bfloat16
AF = mybir.ActivationFunctionType
ALU = mybir.AluOpType
AX = mybir.AxisListType

ENGINES = ["pe", "act", "dve", "pool", "sp"]
SEM_MAX = 30000
DMA_SEMS_PER_Q = 8


def _esize(dt):
    return mybir.dt.size(dt)


def _region(ap):
    sp = str(ap.space)
    if "SB" not in sp and "PSUM" not in sp:
        return None
    pat = ap.ap
    es = _esize(ap.dtype)
    pstride = pat[0][0]
    npart = pat[0][1]
    off = ap.offset
    if pstride > 0:
        p0 = off // pstride
        f0 = off % pstride
    else:
        p0 = 0
        f0 = off
    ext = 1
    for st, cnt in pat[1:]:
        ext += abs(st) * (cnt - 1)
    if "PSUM" in sp:
        return (ap.tensor.name, 0, 128, 0, 1 << 30)
    return (ap.tensor.name, p0, p0 + npart, f0 * es, (f0 + ext) * es)


class Op:
    __slots__ = ("idx", "eng", "fn", "reads", "writes", "is_dma", "deps",
                 "signaled", "sem", "val", "name", "small")


class Prog:
    def __init__(self, nc, same_engine_sync=False):
        self.nc = nc
        self.ops = []
        self.recs = {}
        self.same_engine_sync = same_engine_sync

    def add(self, eng, fn, reads=(), writes=(), is_dma=False, name=""):
        op = Op()
        op.idx = len(self.ops)
        op.eng = eng
        op.fn = fn
        op.is_dma = is_dma
        op.name = name
        op.signaled = False
        op.sem = None
        op.val = 0
        op.small = False
        for a in writes:
            n_el = 1
            for st_, cnt_ in a.ap[1:]:
                n_el *= cnt_
            if n_el < 128:
                op.small = True
        rr = [r for r in (_region(a) for a in reads) if r is not None]
        ww = [r for r in (_region(a) for a in writes) if r is not None]
        ww = ww + [r for r in rr if r[4] == (1 << 30)]
        rr = [r for r in rr if r[4] != (1 << 30)]
        deps = set()
        for (tn, plo, phi, blo, bhi) in rr:
            for rec in self.recs.get(tn, ()):
                if rec[5] and rec[0] < phi and plo < rec[1] and rec[2] < bhi and blo < rec[3]:
                    deps.add(rec[4])
        for (tn, plo, phi, blo, bhi) in ww:
            for rec in self.recs.get(tn, ()):
                if rec[0] < phi and plo < rec[1] and rec[2] < bhi and blo < rec[3]:
                    deps.add(rec[4])
        for (tn, plo, phi, blo, bhi) in ww:
            lst = self.recs.setdefault(tn, [])
            lst[:] = [rec for rec in lst if not (plo <= rec[0] and rec[1] <= phi and blo <= rec[2] and rec[3] <= bhi)]
            lst.append([plo, phi, blo, bhi, op.idx, True])
        for (tn, plo, phi, blo, bhi) in rr:
            lst = self.recs.setdefault(tn, [])
            if not is_dma:
                lst[:] = [rec for rec in lst if not ((not rec[5]) and (rec[4] == op.idx or (
                                                     self.ops[rec[4]].eng == eng and not self.ops[rec[4]].is_dma))
                                                     and plo <= rec[0] and rec[1] <= phi
                                                     and blo <= rec[2] and rec[3] <= bhi)]
            lst.append([plo, phi, blo, bhi, op.idx, False])
        deps.discard(op.idx)
        need = []
        for d in deps:
            dop = self.ops[d]
            if dop.eng == eng and not dop.is_dma and not is_dma and not self.same_engine_sync \
                    and (eng == "pe" or (eng != "pool" and not dop.small)):
                continue
            need.append(d)
            dop.signaled = True
        op.deps = need
        if is_dma:
            op.signaled = True
        self.ops.append(op)
        return op

    def mm(self, out, lhsT, rhs, start=True, stop=True, **kw):
        reads = [lhsT, rhs] + ([] if start else [out])
        return self.add("pe", lambda e: e.matmul(out, lhsT, rhs, start=start, stop=stop, **kw),
                        reads, [out], name="mm")

    def transpose(self, out, in_, ident):
        return self.add("pe", lambda e: e.transpose(out, in_, ident), [in_, ident], [out], name="tr")

    def act(self, out, in_, func, bias=None, scale=None, accum_out=None):
        kw = {}
        reads = [in_]
        writes = [out]
        if bias is not None:
            kw["bias"] = bias
            if not isinstance(bias, (int, float)):
                reads.append(bias)
        if scale is not None:
            kw["scale"] = scale
            if not isinstance(scale, (int, float)):
                reads.append(scale)
        if accum_out is not None:
            kw["accum_out"] = accum_out
            writes.append(accum_out)
        return self.add("act", lambda e: e.activation(out, in_, func, **kw), reads, writes, name="act")

    def tt(self, out, in0, in1, op, eng="dve"):
        return self.add(eng, lambda e: e.tensor_tensor(out, in0, in1, op), [in0, in1], [out], name="tt")

    def ts(self, out, in0, s1, s2, op0, op1=None, eng="dve"):
        reads = [in0]
        for s in (s1, s2):
            if s is not None and not isinstance(s, (int, float)):
                reads.append(s)
        if op1 is None:
            return self.add(eng, lambda e: e.tensor_scalar(out, in0, s1, None, op0), reads, [out], name="ts")
        return self.add(eng, lambda e: e.tensor_scalar(out, in0, s1, s2, op0, op1), reads, [out], name="ts")

    def stt(self, out, in0, scalar, in1, op0, op1):
        reads = [in0, in1]
        if not isinstance(scalar, (int, float)):
            reads.append(scalar)
        return self.add("dve", lambda e: e.scalar_tensor_tensor(out, in0, scalar, in1, op0, op1),
                        reads, [out], name="stt")

    def copy(self, out, in_, eng="dve"):
        if eng == "act":
            return self.add("act", lambda e: e.copy(out, in_), [in_], [out], name="copy")
        return self.add(eng, lambda e: e.tensor_copy(out, in_), [in_], [out], name="copy")

    def memset(self, ap, val, eng="dve"):
        return self.add(eng, lambda e: e.memset(ap, val), [], [ap], name="memset")

    def dma(self, out, in_, eng="sp", **kw):
        return self.add(eng, lambda e: e.dma_start(out=out, in_=in_, **kw), [in_], [out],
                        is_dma=True, name="dma")

    def emit(self, stack):
        nc = self.nc
        counters = {e: 0 for e in ENGINES}
        eng_sems = {e: [] for e in ENGINES}
        dma_sems = {e: [] for e in ENGINES}
        dma_cnt = {e: 0 for e in ENGINES}
        dma_semval = {}
        dma_hist = {e: [] for e in ENGINES}
        for op in self.ops:
            if op.is_dma:
                q = op.eng
                j = dma_cnt[q] % DMA_SEMS_PER_Q
                if len(dma_sems[q]) <= j:
                    dma_sems[q].append(stack.enter_context(nc.semaphore(f"d_{q}_{j}")))
                sem = dma_sems[q][j]
                v = dma_semval.get((q, j), 0) + 16
                dma_semval[(q, j)] = v
                op.sem = sem
                op.val = v
                if dma_cnt[q] >= DMA_SEMS_PER_Q:
                    prev = dma_hist[q][dma_cnt[q] - DMA_SEMS_PER_Q]
                    if prev.idx not in op.deps:
                        op.deps.append(prev.idx)
                dma_hist[q].append(op)
                dma_cnt[q] += 1
            elif op.signaled:
                e = op.eng
                c = counters[e]
                k = c // SEM_MAX
                if len(eng_sems[e]) <= k:
                    eng_sems[e].append(stack.enter_context(nc.semaphore(f"c_{e}_{k}")))
                op.sem = eng_sems[e][k]
                op.val = c % SEM_MAX + 1
                counters[e] = c + 1
        self.n_sems = sum(len(v) for v in eng_sems.values()) + sum(len(v) for v in dma_sems.values())
        block = stack.enter_context(nc.Block())
        ops = self.ops

        def run(engname):
            def body(e):
                waited = {}
                last = None
                for op in ops:
                    if op.eng != engname:
                        continue
                    for d in sorted(op.deps):
                        dop = ops[d]
                        key = id(dop.sem)
                        if waited.get(key, 0) >= dop.val:
                            continue
                        e.wait_ge(dop.sem, dop.val)
                        waited[key] = dop.val
                    inst = op.fn(e)
                    if op.sem is not None:
                        inst.then_inc(op.sem, 16 if op.is_dma else 1)
                    last = op
                for j, sem in enumerate(dma_sems[engname]):
                    v = dma_semval.get((engname, j), 0)
                    if v:
                        e.wait_ge(sem, v)
            return body

        block.tensor(run("pe"))
        block.scalar(run("act"))
        block.vector(run("dve"))
        block.gpsimd(run("pool"))
        block.sync(run("sp"))


S = 2048
D = 1024
NT = 16
NIN = 6916
FF = 2816
DEPTH = 2
EPS = 1e-6
NEG = -30000.0
R_BYTES = 57 * 1024


class Carver:
    def __init__(self, t):
        self.t = t
        self.off = 0

    def reset(self):
        self.off = 0

    def take(self, shape, dt):
        n = 1
        for s_ in shape:
            n *= s_
        nb = n * mybir.dt.size(dt)
        nb_al = (nb + 31) // 32 * 32
        assert self.off + nb_al <= R_BYTES, (self.off, nb_al)
        a = self.t[:, self.off // 4:(self.off + nb_al) // 4]
        self.off += nb_al
        if dt != F32:
            a = a.bitcast(dt)
        a = a[:, 0:n]
        if len(shape) == 2:
            a = a.rearrange("p (a b) -> p a b", a=shape[0])
        elif len(shape) == 3:
            a = a.rearrange("p (a b c) -> p a b c", a=shape[0], b=shape[1])
        return a


def build_nc(depth=DEPTH, dbg=()):
    nc = bass.Bass("TRN2", target_bir_lowering=False)

    def din(name, shape):
        return nc.dram_tensor(name, shape, F32, kind="ExternalInput").ap()

    x_d = din("x", [S, D])
    gmix_d = din("gmix", [DEPTH, D])
    w_in_d = din("w_in", [DEPTH, D, NIN])
    wconvT_d = din("wconvT", [DEPTH, 256, 3])
    w_sp_d = din("w_sp", [DEPTH, 4, 128, 128])
    b_spT_d = din("b_spT", [DEPTH, 128, 4])
    lng_d = din("lng", [DEPTH, 256])
    lnb_d = din("lnb", [DEPTH, 256])
    gq_d = din("gq", [DEPTH, 64])
    gk_d = din("gk", [DEPTH, 64])
    bf_d = din("bfg", [DEPTH, 4])
    w_br_d = din("w_br", [DEPTH, 4, 256, D])
    w_out_d = din("w_out", [DEPTH, D, D])
    gffn_d = din("gffn", [DEPTH, D])
    w_fi_d = din("w_fi", [DEPTH, D, 2 * FF])
    w_fo_d = din("w_fo", [DEPTH, FF, D])
    out_d = nc.dram_tensor("out", [S, D], F32, kind="ExternalOutput").ap()
    dbg_d = {}
    for name in dbg:
        if name.startswith("cpos"):
            dbg_d[name] = nc.dram_tensor("dbg_" + name, [128, 64], F32, kind="ExternalOutput").ap()
        elif name.startswith("y") or name.startswith("merged") or name.startswith("xnT") or name.startswith("fq") or name.startswith("fk"):
            shp = [128, (8 if (name.startswith("merged") or name.startswith("xnT")) else 2) * S]
            dbg_d[name] = nc.dram_tensor("dbg_" + name, shp, F32, kind="ExternalOutput").ap()
        else:
            dbg_d[name] = nc.dram_tensor("dbg_" + name, [S, D], F32, kind="ExternalOutput").ap()

    with ExitStack() as st:
        def sb(name, shape, dt):
            return st.enter_context(nc.sbuf_tensor(name, shape, dt))

        banks = [st.enter_context(nc.psum_tensor(f"bank{i}", [128, 512], F32)) for i in range(8)]
        h = sb("h", [128, NT, D], F32)
        xnT = sb("xnT", [128, 8, S], BF16)
        ysT = sb("ysT", [128, 8, S], BF16)
        wslots = [sb(f"wslot{i}", [128, 8, 512], BF16) for i in range(2)]
        Rt = sb("R", [128, R_BYTES // 4], F32)
        R = Carver(Rt)
        ident = sb("ident", [128, 128], BF16)
        negones = sb("negones", [128, 128], BF16)
        ones_bf = sb("ones_bf", [128, 128], BF16)
        uneg = sb("uneg", [128, 128], BF16)
        tri_f = sb("tri_f", [128, 128], F32)
        ones_f = sb("ones_f", [128, 128], F32)
        zeros_bf = sb("zeros_bf", [128, 128], BF16)
        sbmask = sb("sbmask", [128, 128], BF16)
        foxmask = sb("foxmask", [128, 128], BF16)
        ss = sb("ss", [128, NT], F32)
        rs = sb("rs", [128, NT], F32)
        wc = sb("wc", [128, 2, 3], F32)
        bsp = sb("bsp", [128, 4], F32)
        lng_b = sb("lng_b", [128, 256], F32)
        lnb_b = sb("lnb_b", [128, 256], F32)
        gq_b = sb("gq_b", [128, 64], F32)
        gk_b = sb("gk_b", [128, 64], F32)
        bf_b = sb("bf_b", [128, 4], F32)
        small = sb("small", [128, 256], F32)

        P = Prog(nc)
        rot_state = {}

        def rot(name, lst):
            i = rot_state.get(name, 0)
            rot_state[name] = i + 1
            return lst[i % len(lst)]

        P.memset(ones_bf[:], 1.0, eng="pool")
        P.memset(negones[:], -1.0, eng="pool")
        P.memset(zeros_bf[:], 0.0, eng="pool")
        P.memset(ones_f[:], 1.0, eng="pool")

        def asel(out, in_, pattern, op, fill, base, cm):
            P.add("pool", lambda e: e.affine_select(out, in_, pattern, op, fill, base=base, channel_multiplier=cm),
                  [in_], [out], name="asel")

        asel(ident[:], ones_bf[:], [[-1, 128]], ALU.is_equal, 0.0, 0, 1)
        asel(uneg[:], negones[:], [[-1, 128]], ALU.is_ge, 0.0, 0, 1)
        asel(tri_f[:], ones_f[:], [[1, 128]], ALU.is_ge, 0.0, 0, -1)
        asel(sbmask[:], zeros_bf[:], [[1, 128]], ALU.is_gt, NEG, 0, -1)
        asel(foxmask[:], zeros_bf[:], [[1, 128]], ALU.is_ge, NEG, 0, -1)

        xr = x_d.rearrange("(n p) d -> p n d", p=128)
        for i0 in range(0, NT, 2):
            P.dma(h[:, i0:i0 + 2, :], xr[:, i0:i0 + 2, :], eng=("sp" if (i0 // 2) % 2 == 0 else "act"))

        chunks = []
        loaded = [0]
        slot_of = {}

        def wsrc(ap2d):
            return ap2d.rearrange("(k p) n -> p k n", p=128)

        def plan_layer(l):
            w = w_in_d[l]
            for j in range(2):
                chunks.append((("A", l, j), [(w[:, g * 256 + j * 128: g * 256 + j * 128 + 128], g * 128) for g in range(3)]))
            chunks.append((("B", l), [(w[:, 768:1280], 0)]))
            chunks.append((("Cqk", l), [(w[:, 1280:1792], 0)]))
            chunks.append((("Cv", l), [(w[:, 1792:2048], 0)]))
            chunks.append((("Dqk", l), [(w[:, 2048:2560], 0)]))
            chunks.append((("Dvf", l), [(w[:, 2560:2820], 0)]))
            for dc in range(8):
                chunks.append((("G", l, dc), [(w[:, 2820 + i * 1024 + dc * 128: 2820 + i * 1024 + dc * 128 + 128], i * 128) for i in range(4)]))
            for hf in range(2):
                chunks.append((("O", l, hf), [(w_out_d[l][:, hf * 512:(hf + 1) * 512], 0)]))
            wf = w_fi_d[l]
            for ps_ in range(3):
                fcs = FFN_PASSES[ps_]
                for q in range(0, len(fcs), 2):
                    srcs = []
                    for qq, fc in enumerate(fcs[q:q + 2]):
                        srcs.append((wf[:, fc * 128:(fc + 1) * 128], qq * 256))
                        srcs.append((wf[:, FF + fc * 128: FF + (fc + 1) * 128], qq * 256 + 128))
                    chunks.append((("F", l, ps_, q // 2), srcs))

        FFN_PASSES = [list(range(0, 8)), list(range(8, 15)), list(range(15, 22))]
        for l in range(depth):
            plan_layer(l)

        def ensure_loaded(upto):
            while loaded[0] <= min(upto, len(chunks) - 1):
                ci = loaded[0]
                key, srcs = chunks[ci]
                slot = wslots[ci % len(wslots)]
                for (src, off) in srcs:
                    n = src.shape[1]
                    P.dma(slot[:, :, off:off + n], wsrc(src), eng="pool")
                slot_of[key] = (ci, slot)
                loaded[0] += 1

        def wget(key, ahead=1):
            ci = None
            for i_, (k_, _) in enumerate(chunks):
                if k_ == key:
                    ci = i_
                    break
            assert ci is not None, key
            ensure_loaded(ci + ahead)
            cj, slot = slot_of[key]
            assert cj == ci
            return slot

        def dump(name, src_ap, kind):
            if name not in dbg_d:
                return
            R2 = dbgbuf
            if kind == "fm":
                n = src_ap.shape[1]
                for c in range(n):
                    for G in range(4):
                        P.copy(R2[:, 0:512], src_ap[:, c, G * 512:(G + 1) * 512], eng="dve")
                        P.dma(dbg_d[name][:, c * S + G * 512: c * S + (G + 1) * 512], R2[:, 0:512], eng="sp")
            else:
                for i in range(NT):
                    P.dma(dbg_d[name].rearrange("(n p) d -> p n d", p=128)[:, i, :], src_ap[:, i, :], eng="sp")

        dbgbuf = sb("dbgbuf", [128, 512], F32) if dbg else None

        def norm_begin(g_row):
            nb = {}
            nb["gb"] = R.take([D], F32)
            nb["sq"] = R.take([D], BF16)
            nb["xn"] = [R.take([D], BF16) for _ in range(2)]
            P.dma(nb["gb"], g_row.partition_broadcast(128), eng="sp")
            return nb

        def norm_stage(nb, si, i):
            r_ = rs[:, i:i + 1]
            if si == 0:
                P.act(nb["sq"], h[:, i, :], AF.Square, accum_out=ss[:, i:i + 1])
            elif si == 1:
                P.ts(r_, ss[:, i:i + 1], 1.0 / D, EPS, ALU.mult, ALU.add)
            elif si == 2:
                P.act(r_, r_, AF.Sqrt)
            elif si == 3:
                P.add("dve", lambda e, o=r_: e.reciprocal(o, o), [r_], [r_], name="recip")
            elif si == 4:
                P.stt(nb["xn"][i % 2], h[:, i, :], r_, nb["gb"], ALU.mult, ALU.mult)
            elif si == 5:
                xn = nb["xn"][i % 2]
                bk = rot("tr", [0, 1])
                nb[("bk", i)] = bk
                pb = banks[bk][:].bitcast(BF16)
                for k in range(8):
                    P.transpose(pb[:, k * 128:(k + 1) * 128], xn[:, k * 128:(k + 1) * 128], ident[:])
            elif si == 6:
                pb = banks[nb[("bk", i)]][:].bitcast(BF16)
                P.copy(xnT[:, :, i * 128:(i + 1) * 128], pb.rearrange("p (k t) -> p k t", k=8), eng="act")

        NST = 7

        def norm_push(nb, t):
            for si in range(NST):
                j = t - si
                if 0 <= j < NT:
                    norm_stage(nb, si, j)

        def norm_flush(nb):
            for t in range(NT, NT + NST - 1):
                norm_push(nb, t)

        def rmsnorm(g_row):
            R.reset()
            nb = norm_begin(g_row)
            for i in range(NT):
                norm_push(nb, i)
            norm_flush(nb)

        def mixer_A(l):
            R.reset()
            P.dma(wc[:], wconvT_d[l].rearrange("(j p) k -> p j k", p=128), eng="sp")
            xcb = R.take([S + 2], BF16)
            cxs = [R.take([512], F32) for _ in range(2)]
            cbs = [R.take([512], F32) for _ in range(2)]
            dg = R.take([2, 3, 128], BF16)
            for j in range(2):
                for k in range(3):
                    P.ts(dg[:, j, k, :], ident[:], wc[:, j, k:k + 1], None, ALU.mult)
            for j in range(2):
                slot = wget(("A", l, j))
                P.memset(xcb[:, 0:2], 0.0, eng="dve")
                for G in range(4):
                    bset = rot("convset", [[0, 1, 2], [3, 4, 5]])
                    ts_ = slice(G * 512, (G + 1) * 512)
                    for gi in range(3):
                        for k in range(8):
                            P.mm(banks[bset[gi]][:], slot[:, k, gi * 128:(gi + 1) * 128], xnT[:, k, ts_],
                                 start=(k == 0), stop=(k == 7))
                    a = (j * 4 + G) % 2
                    P.copy(cxs[a], banks[bset[2]][:], eng="act")
                    P.tt(xcb[:, 2 + G * 512: 2 + (G + 1) * 512], banks[bset[1]][:], cxs[a], ALU.mult)
                    P.copy(cbs[a], banks[bset[0]][:], eng="act")
                    by = rot("convy", [6, 7])
                    for k in range(3):
                        P.mm(banks[by][:], dg[:, j, k, :], xcb[:, G * 512 + k: G * 512 + k + 512],
                             start=(k == 0), stop=(k == 2))
                    P.tt(ysT[:, 0 + j, ts_], banks[by][:], cbs[a], ALU.mult)

        def pipeline(n_items, stages):
            ns_ = len(stages)
            for t_ in range(n_items + ns_ - 1):
                for si_ in range(ns_):
                    j_ = t_ - si_
                    if 0 <= j_ < n_items:
                        stages[si_](j_)

        def mixer_B(l):
            R.reset()
            P.dma(bsp[:], b_spT_d[l], eng="sp")
            P.dma(lng_b[:], lng_d[l].partition_broadcast(128), eng="sp")
            P.dma(lnb_b[:], lnb_d[l].partition_broadcast(128), eng="sp")
            wsf = R.take([4, 128], F32)
            wsb = R.take([4, 128], BF16)
            wsT = R.take([4, 128], BF16)
            P.dma(wsf, w_sp_d[l].rearrange("g t s -> t g s"), eng="sp")
            for g in range(4):
                asel(wsb[:, g, :], wsf[:, g, :], [[-1, 128]], ALU.is_ge, 0.0, 0, 1)
            bk = rot("tr", [0, 1])
            pb = banks[bk][:].bitcast(BF16)
            for g in range(4):
                P.transpose(pb[:, g * 128:(g + 1) * 128], wsb[:, g, :], ident[:])
            P.copy(wsT, pb[:, 0:512].rearrange("p (g t) -> p g t", g=4), eng="dve")
            NB = 6
            guv = [R.take([512], F32) for _ in range(NB)]
            vn = [R.take([256], F32) for _ in range(NB)]
            vnb = [R.take([256], BF16) for _ in range(NB)]
            ybt = [R.take([256], BF16) for _ in range(NB)]
            slot = wget(("B", l))
            stt_ = {}

            def sm(i, lo, hi):
                b = i % NB
                return small[:, b * 16 + lo: b * 16 + hi]

            def s0(i):
                buv = rot("uv", [0, 1])
                stt_[("uv", i)] = buv
                for k in range(8):
                    P.mm(banks[buv][:], xnT[:, k, i * 128:(i + 1) * 128], slot[:, k, :], start=(k == 0), stop=(k == 7))

            def s1(i):
                P.act(guv[i % NB], banks[stt_[("uv", i)]][:], AF.Gelu_apprx_tanh)

            def s2(i):
                v = guv[i % NB][:, 256:512]
                st6, mv, rstd, dd_, m2_ = sm(i, 0, 6), sm(i, 8, 10), sm(i, 10, 11), sm(i, 11, 12), sm(i, 12, 13)
                P.add("dve", lambda e, o=st6, i_=v: e.bn_stats(o, i_), [v], [st6], name="bnstats")
                P.tt(dd_, st6[:, 1:2], st6[:, 4:5], ALU.subtract)
                P.tt(mv[:, 0:1], st6[:, 1:2], st6[:, 4:5], ALU.add)
                P.tt(m2_, st6[:, 2:3], st6[:, 5:6], ALU.add)
                P.ts(mv[:, 0:1], mv[:, 0:1], 0.5, None, ALU.mult)
                P.ts(m2_, m2_, 1.0 / 256, EPS, ALU.mult, ALU.add)
                P.tt(dd_, dd_, dd_, ALU.mult)
                P.stt(rstd, dd_, 0.25, m2_, ALU.mult, ALU.add)

            def s3(i):
                rstd = sm(i, 10, 11)
                P.act(rstd, rstd, AF.Sqrt)

            def s4(i):
                b = i % NB
                v = guv[b][:, 256:512]
                mv, rstd = sm(i, 8, 10), sm(i, 10, 11)
                P.add("dve", lambda e, o=rstd: e.reciprocal(o, o), [rstd], [rstd], name="recip")
                P.ts(vn[b], v, mv[:, 0:1], rstd, ALU.subtract, ALU.mult)
                P.tt(vn[b], vn[b], lng_b[:], ALU.mult)
                P.tt(vnb[b], vn[b], lnb_b[:], ALU.add)

            def s5(i):
                b = i % NB
                bmx = rot("mx", [2, 3])
                stt_[("mx", i)] = bmx
                for g in range(4):
                    P.mm(banks[bmx][:, g * 64:(g + 1) * 64], wsT[:, g, :], vnb[b][:, g * 64:(g + 1) * 64],
                         start=True, stop=True)

            def s6(i):
                b = i % NB
                bmx = stt_[("mx", i)]
                for g in range(4):
                    P.stt(ybt[b][:, g * 64:(g + 1) * 64], banks[bmx][:, g * 64:(g + 1) * 64], bsp[:, g:g + 1],
                          guv[b][:, g * 64:(g + 1) * 64], ALU.add, ALU.mult)

            def s7(i):
                b = i % NB
                btr = rot("trb", [4, 5])
                stt_[("tr", i)] = btr
                pb2 = banks[btr][:].bitcast(BF16)
                for c in range(2):
                    P.transpose(pb2[:, c * 128:(c + 1) * 128], ybt[b][:, c * 128:(c + 1) * 128], ident[:])

            def s8(i):
                pb2 = banks[stt_[("tr", i)]][:].bitcast(BF16)
                P.copy(ysT[:, 2:4, i * 128:(i + 1) * 128], pb2[:, 0:256].rearrange("p (c t) -> p c t", c=2), eng="act")

            pipeline(NT, [s0, s1, s2, s3, s4, s5, s6, s7, s8])

        def mixer_C(l):
            R.reset()
            qpad = R.take([4, S], BF16)
            P.memset(qpad, 0.0, eng="pool")
            kT = R.take([2, S], BF16)
            vtm = R.take([NT, 256], BF16)
            e_t = [R.take([512], F32) for _ in range(3)]
            sp_t = [R.take([512], BF16) for _ in range(3)]
            w_t = [R.take([512], BF16) for _ in range(3)]
            Sb = [R.take([512], BF16) for _ in range(2)]
            slot = wget(("Cqk", l))
            for qk in range(2):
                for c in range(2):
                    for G in range(4):
                        bk = rot("proj", [0, 1, 2, 3, 4, 5])
                        ts_ = slice(G * 512, (G + 1) * 512)
                        for k in range(8):
                            P.mm(banks[bk][:], slot[:, k, qk * 256 + c * 128: qk * 256 + (c + 1) * 128], xnT[:, k, ts_],
                                 start=(k == 0), stop=(k == 7))
                        if qk == 0:
                            P.act(qpad[0:64, 2 * c, ts_], banks[bk][0:64, :], AF.Copy, scale=0.125)
                            P.act(qpad[64:128, 2 * c + 1, ts_], banks[bk][64:128, :], AF.Copy, scale=0.125)
                        else:
                            P.copy(kT[:, c, ts_], banks[bk][:], eng="dve")
            slot = wget(("Cv", l))
            for i in range(NT):
                bk = rot("proj", [0, 1, 2, 3, 4, 5])
                for k in range(8):
                    P.mm(banks[bk][:, 0:256], xnT[:, k, i * 128:(i + 1) * 128], slot[:, k, 0:256],
                         start=(k == 0), stop=(k == 7))
                P.copy(vtm[:, i, :], banks[bk][:, 0:256], eng=("act" if i % 2 else "dve"))
            for hp in range(2):
                for G in range(4):
                    nkb = 4 * G + 4
                    jobs = []
                    for kb in range(nkb - 1, -1, -1):
                        for hh in range(2):
                            jobs.append((hh, kb))
                    OB = [6, 7]
                    for hh in range(2):
                        P.memset(Sb[hh], 0.0, eng="dve")
                    state = {}

                    def stage(si, job, jn):
                        hh, kb = job
                        po = hh * 64
                        r = kb - 4 * G
                        c0 = 128 * r if r >= 0 else 0
                        cs = slice(c0, 512)
                        qs = slice(G * 512 + c0, (G + 1) * 512)
                        ksl = slice(kb * 128, (kb + 1) * 128)
                        first = (kb == nkb - 1)
                        last = (kb == 0)
                        if si == 0:
                            zb = rot("z", [0, 1, 2, 3, 4, 5])
                            state[(job, "zb")] = zb
                            P.mm(banks[zb][:, cs], kT[:, hp, ksl], qpad[:, hp * 2 + hh, qs],
                                 start=True, stop=False)
                            if r >= 0:
                                P.mm(banks[zb][:, c0:c0 + 128], ident[:], sbmask[:], start=False, stop=False)
                        elif si == 1:
                            zb = state[(job, "zb")]
                            a = jn % 3
                            P.act(e_t[a][:, cs], banks[zb][:, cs], AF.Exp)
                            P.act(sp_t[a][:, cs], e_t[a][:, cs], AF.Ln, bias=1.0)
                        elif si == 2:
                            a = jn % 3
                            zb = state[(job, "zb")]
                            P.mm(banks[zb][:, cs], uneg[:], sp_t[a][:, cs], start=False, stop=first,
                                 skip_group_check=True)
                            if not first:
                                P.mm(banks[zb][:, cs], negones[:], Sb[hh][:, cs], start=False, stop=True,
                                     skip_group_check=True)
                            if not last:
                                P.tt(Sb[hh][:, cs], Sb[hh][:, cs], sp_t[a][:, cs], ALU.add)
                        elif si == 3:
                            a = jn % 3
                            zb = state[(job, "zb")]
                            P.act(w_t[a][:, cs], banks[zb][:, cs], AF.Exp)
                        elif si == 4:
                            a = jn % 3
                            P.mm(banks[OB[hh]][:, cs], vtm[:, kb, hp * 128:(hp + 1) * 128], w_t[a][:, cs],
                                 start=first, stop=last)
                            if last:
                                P.copy(ysT[po:po + 64, 4 + hp, G * 512:(G + 1) * 512], banks[OB[hh]][po:po + 64, :],
                                       eng="dve")

                    nst = 5
                    for t in range(len(jobs) + nst - 1):
                        for si in range(nst):
                            jn = t - si
                            if 0 <= jn < len(jobs):
                                stage(si, jobs[jn], jn)

        def mixer_D(l):
            R.reset()
            P.dma(gq_b[:], gq_d[l].partition_broadcast(128), eng="sp")
            P.dma(gk_b[:], gk_d[l].partition_broadcast(128), eng="sp")
            P.dma(bf_b[:], bf_d[l].partition_broadcast(128), eng="sp")
            P.ts(gq_b[:], gq_b[:], 0.125, None, ALU.mult)
            qpad = R.take([4, S], BF16)
            kpad = R.take([4, S], BF16)
            P.memset(qpad, 0.0, eng="pool")
            P.memset(kpad, 0.0, eng="pool")
            qpv = qpad.rearrange("p (c two) s -> p c two s", two=2)
            kpv = kpad.rearrange("p (c two) s -> p c two s", two=2)
            vtm = R.take([NT, 2, 192], BF16)
            P.memset(vtm[:, :, :, 64:128], 1.0, eng="pool")
            LF = R.take([NT, 4], F32)
            cpos = R.take([NT, 4], F32)
            carry = R.take([NT, 4], F32)
            r1 = carry
            r2 = LF
            cbf = R.take([NT, 4], BF16)
            off_shared = R.off
            ND = 4
            sq = [R.take([512], BF16) for _ in range(2)]
            qkc = [R.take([512], F32) for _ in range(ND)]
            qkn = [R.take([512], BF16) for _ in range(2)]
            R.off = off_shared
            TEq = R.take([NT, 4, 8], BF16)
            TEk = R.take([NT, 4, 8], BF16)
            p_t = [R.take([512], BF16) for _ in range(3)]
            rec_one = R.take([512], F32)
            rec = [rec_one, rec_one]
            slot_qk = wget(("Dqk", l))
            slot_vf = wget(("Dvf", l), ahead=0)
            stt_ = {}

            def smf(i, lo, hi):
                b = i % ND
                return small[:, 128 + b * 16 + lo: 128 + b * 16 + hi]

            def s0(i):
                tsl = slice(i * 128, (i + 1) * 128)
                bq = rot("fq", [0, 1])
                bv = rot("fv", [2, 3])
                stt_[("bq", i)] = bq
                stt_[("bv", i)] = bv
                for k in range(8):
                    P.mm(banks[bq][:], xnT[:, k, tsl], slot_qk[:, k, :], start=(k == 0), stop=(k == 7))
                for k in range(8):
                    P.mm(banks[bv][:, 0:260], xnT[:, k, tsl], slot_vf[:, k, 0:260], start=(k == 0), stop=(k == 7))

            def s1(i):
                bq, bv = stt_[("bq", i)], stt_[("bv", i)]
                P.act(sq[i % 2], banks[bq][:], AF.Square)
                P.copy(qkc[i % ND], banks[bq][:], eng="act")
                vsrc = banks[bv][:, 0:256].rearrange("p (c two d) -> p c two d", c=2, two=2)
                P.copy(vtm[:, i, :, 0:64], vsrc[:, :, 0, :], eng="act")
                P.copy(vtm[:, i, :, 128:192], vsrc[:, :, 1, :], eng="act")
                P.tt(smf(i, 0, 4), banks[bv][:, 256:260], bf_b[:], ALU.add)

            def s2(i):
                ssq = smf(i, 8, 16)
                P.add("dve", lambda e, o=ssq, i_=sq[i % 2]: e.tensor_reduce(o, i_.rearrange("p (j d) -> p j d", j=8), AX.X, ALU.add),
                      [sq[i % 2]], [ssq], name="tred")
                P.ts(ssq, ssq, 1.0 / 64, EPS, ALU.mult, ALU.add)

            def s3(i):
                fb = smf(i, 0, 4)
                ssq = smf(i, 8, 16)
                P.act(fb, fb, AF.Exp, scale=-1.0)
                P.act(LF[:, i, :], fb, AF.Ln, bias=1.0)
                P.act(ssq, ssq, AF.Sqrt)

            def s4(i):
                ssq = smf(i, 8, 16)
                P.add("dve", lambda e, o=ssq: e.reciprocal(o, o), [ssq], [ssq], name="recip")
                for j in range(8):
                    gbt = gq_b if j < 4 else gk_b
                    P.stt(qkn[i % 2][:, j * 64:(j + 1) * 64], qkc[i % ND][:, j * 64:(j + 1) * 64], ssq[:, j:j + 1], gbt[:],
                          ALU.mult, ALU.mult)

            def s5(i):
                btr = rot("ftr", [4, 5])
                stt_[("tr", i)] = btr
                pb = banks[btr][:].bitcast(BF16)
                for c in range(4):
                    P.transpose(pb[:, c * 128:(c + 1) * 128], qkn[i % 2][:, c * 128:(c + 1) * 128], ident[:])

            def s6(i):
                tsl = slice(i * 128, (i + 1) * 128)
                pb = banks[stt_[("tr", i)]][:].bitcast(BF16)
                P.copy(qpv[0:64, :, 0, tsl], pb[0:64, 0:256].rearrange("p (c t) -> p c t", c=2), eng="act")
                P.copy(qpv[64:128, :, 1, tsl], pb[64:128, 0:256].rearrange("p (c t) -> p c t", c=2), eng="act")
                P.copy(kpv[0:64, :, 0, tsl], pb[0:64, 256:512].rearrange("p (c t) -> p c t", c=2), eng="dve")
                P.copy(kpv[64:128, :, 1, tsl], pb[64:128, 256:512].rearrange("p (c t) -> p c t", c=2), eng="dve")

            pipeline(NT, [s0, s1, s2, s3, s4, s5, s6])
            LF2 = LF.rearrange("p i h -> p (i h)")
            P.mm(banks[6][:, 0:64], tri_f[:], LF2, start=True, stop=True)
            P.mm(banks[7][:, 0:64], ones_f[:], LF2, start=True, stop=True)
            P.memset(carry[:, 0, :], 0.0, eng="dve")
            for i in range(1, NT):
                P.tt(carry[:, i, :], carry[:, i - 1, :], banks[7][:, (i - 1) * 4: i * 4], ALU.add)
            cp2 = cpos.rearrange("p i h -> p (i h)")
            P.tt(cp2, banks[6][:, 0:64], carry.rearrange("p i h -> p (i h)"), ALU.add)
            P.memset(TEq, 1.0, eng="dve")
            P.memset(TEk, 1.0, eng="dve")
            cur = cpos
            for t_, nxt in enumerate((r1, r2, None)):
                P.copy(cbf, cur, eng="dve")
                P.copy(TEk[:, :, :, 3 + t_], cbf, eng="dve")
                P.ts(TEq[:, :, :, t_], cbf, -1.0, None, ALU.mult)
                if nxt is not None:
                    P.tt(nxt, cur, cbf, ALU.subtract)
                    cur = nxt
            if f"cpos{l}" in dbg_d:
                P.dma(dbg_d[f"cpos{l}"], cp2, eng="sp")
            for (TE, XP) in ((TEq, qpad), (TEk, kpad)):
                for hd in range(4):
                    opo = 64 - (hd % 2) * 64
                    for G in range(4):
                        btr = rot("lb", [0, 1, 2, 3])
                        pb = banks[btr][:].bitcast(BF16)
                        for ii in range(4):
                            P.transpose(pb[0:8, ii * 128:(ii + 1) * 128], TE[:, G * 4 + ii, hd, :], ident[:])
                        P.copy(XP[opo:opo + 6, hd, G * 512:(G + 1) * 512], pb[0:6, 0:512], eng="dve")
            for hp in range(2):
                for G in range(4):
                    nkb = 4 * G + 4
                    jobs = []
                    for kb in range(nkb - 1, -1, -1):
                        for hh in range(2):
                            jobs.append((hh, kb))
                    OBn = [4, 5] if (hp * 4 + G) % 2 == 0 else [6, 7]
                    state = {}

                    def stage(si, job, jn):
                        hh, kb = job
                        po = hh * 64
                        r = kb - 4 * G
                        c0 = 128 * r if r >= 0 else 0
                        cs = slice(c0, 512)
                        qs = slice(G * 512 + c0, (G + 1) * 512)
                        ksl = slice(kb * 128, (kb + 1) * 128)
                        first = (kb == nkb - 1)
                        last = (kb == 0)
                        a = jn % 3
                        if si == 0:
                            lb = rot("lb", [0, 1, 2, 3])
                            state[(job, "lb")] = lb
                            P.mm(banks[lb][:, cs], kpad[:, hp * 2 + hh, ksl], qpad[:, hp * 2 + hh, qs],
                                 start=True, stop=(r < 0))
                            if r >= 0:
                                P.mm(banks[lb][:, c0:c0 + 128], ident[:], foxmask[:], start=False, stop=True)
                        elif si == 1:
                            lb = state[(job, "lb")]
                            P.act(p_t[a][:, cs], banks[lb][:, cs], AF.Exp)
                        elif si == 3:
                            opo = 64 - po
                            P.mm(banks[OBn[hh]][:, cs], vtm[:, kb, hp, hh * 64: hh * 64 + 128], p_t[a][:, cs],
                                 start=first, stop=last)
                            if last:
                                rc = rec[hh]
                                P.add("dve", lambda e, o=rc[opo:opo + 64, :], i_=banks[OBn[hh]][opo:opo + 64, :]: e.reciprocal(o, i_),
                                      [banks[OBn[hh]][opo:opo + 64, :]], [rc[opo:opo + 64, :]], name="recip")
                                P.tt(ysT[po:po + 64, 6 + hp, G * 512:(G + 1) * 512], banks[OBn[hh]][po:po + 64, :],
                                     rc[opo:opo + 64, :], ALU.mult)

                    nst = 4
                    for t in range(len(jobs) + nst - 1):
                        for si in range(nst):
                            jn = t - si
                            if 0 <= jn < len(jobs):
                                stage(si, jobs[jn], jn)

        def merge_out(l):
            R.reset()
            mT = R.take([8, S], BF16)
            off_after_mT = R.off
            wb = [R.take([4, 2, 128], BF16) for _ in range(2)]
            sg = [R.take([512], F32) for _ in range(2)]
            macc = [R.take([512], F32) for _ in range(2)]
            tmp = [R.take([512], F32) for _ in range(2)]
            n = 0
            for dc in range(8):
                w_b = wb[dc % 2]
                for i in range(4):
                    P.dma(w_b[:, i, :, :], w_br_d[l][i, :, dc * 128:(dc + 1) * 128].rearrange("(j p) c -> p j c", p=128),
                          eng="pool")
                slot = wget(("G", l, dc))
                for G in range(4):
                    ts_ = slice(G * 512, (G + 1) * 512)
                    mc = macc[(dc * 4 + G) % 2]
                    for i in range(4):
                        gbk = rot("gate", [0, 1, 2, 3])
                        bbk = rot("br", [4, 5, 6, 7])
                        for k in range(8):
                            P.mm(banks[gbk][:], slot[:, k, i * 128:(i + 1) * 128], xnT[:, k, ts_],
                                 start=(k == 0), stop=(k == 7))
                        for j in range(2):
                            P.mm(banks[bbk][:], w_b[:, i, j, :], ysT[:, i * 2 + j, ts_], start=(j == 0), stop=(j == 1))
                        s_ = sg[n % 2]
                        t_ = tmp[n % 2]
                        n += 1
                        P.act(s_, banks[gbk][:], AF.Sigmoid)
                        if i == 0:
                            P.tt(mc, s_, banks[bbk][:], ALU.mult)
                        else:
                            P.tt(t_, s_, banks[bbk][:], ALU.mult)
                            if i < 3:
                                P.tt(mc, mc, t_, ALU.add)
                            else:
                                P.tt(mT[:, dc, ts_], mc, t_, ALU.add)
            dump(f"merged{l}", mT, "fm")
            R.off = off_after_mT
            nb = norm_begin(gffn_d[l])
            for hf in range(2):
                slot = wget(("O", l, hf))
                for i in range(NT):
                    bk = rot("wo", [2, 3, 4, 5, 6, 7])
                    for k in range(8):
                        P.mm(banks[bk][:], mT[:, k, i * 128:(i + 1) * 128], slot[:, k, :], start=(k == 0), stop=(k == 7))
                    hs = h[:, i, hf * 512:(hf + 1) * 512]
                    P.tt(hs, hs, banks[bk][:], ALU.add)
                    if hf == 1:
                        norm_push(nb, i)
            norm_flush(nb)
            dump(f"hmix{l}", h[:], "tm")

        def ffn(l, is_last, next_g=None):
            R.reset()
            hidT = ysT
            WoF2 = [R.take([8, D], BF16) for _ in range(2)]
            sg = [R.take([512], F32) for _ in range(2)]
            n = 0
            for ps_ in range(3):
                fcs = FFN_PASSES[ps_]
                nf = len(fcs)
                WoF = WoF2[ps_ % 2]
                for hf in range(2):
                    P.dma(WoF[:, 0:nf, hf * 512:(hf + 1) * 512],
                          w_fo_d[l][fcs[0] * 128:(fcs[-1] + 1) * 128, hf * 512:(hf + 1) * 512].rearrange("(f p) c -> p f c", p=128),
                          eng="pool")
                for q in range(0, nf, 2):
                    slot = wget(("F", l, ps_, q // 2))
                    for qq, fc in enumerate(fcs[q:q + 2]):
                        fl = q + qq
                        for G in range(4):
                            ts_ = slice(G * 512, (G + 1) * 512)
                            gbk = rot("gate", [0, 1, 2, 3])
                            ubk = rot("br", [4, 5, 6, 7])
                            for k in range(8):
                                P.mm(banks[gbk][:], slot[:, k, qq * 256: qq * 256 + 128], xnT[:, k, ts_],
                                     start=(k == 0), stop=(k == 7))
                            for k in range(8):
                                P.mm(banks[ubk][:], slot[:, k, qq * 256 + 128: qq * 256 + 256], xnT[:, k, ts_],
                                     start=(k == 0), stop=(k == 7))
                            s_ = sg[n % 2]
                            n += 1
                            P.act(s_, banks[gbk][:], AF.Silu)
                            P.tt(hidT[:, fl, ts_], s_, banks[ubk][:], ALU.mult)
                nb = None
                if ps_ == 2 and next_g is not None:
                    nb = norm_begin(next_g)
                for i in range(NT):
                    for hf in range(2):
                        bk = rot("wo", [2, 3, 4, 5, 6, 7])
                        for fl in range(nf):
                            P.mm(banks[bk][:], hidT[:, fl, i * 128:(i + 1) * 128], WoF[:, fl, hf * 512:(hf + 1) * 512],
                                 start=(fl == 0), stop=(fl == nf - 1))
                        hs = h[:, i, hf * 512:(hf + 1) * 512]
                        P.tt(hs, hs, banks[bk][:], ALU.add)
                    if is_last and ps_ == 2:
                        P.dma(out_d.rearrange("(n p) d -> p n d", p=128)[:, i, :], h[:, i, :], eng="sp")
                    if nb is not None:
                        norm_push(nb, i)
                if nb is not None:
                    norm_flush(nb)

        stop_after = [s_ for s_ in dbg if s_.startswith("stop:")]
        stop_after = stop_after[0][5:] if stop_after else None

        def finish_early():
            for i in range(NT):
                P.dma(out_d.rearrange("(n p) d -> p n d", p=128)[:, i, :], h[:, i, :], eng="sp")

        done = False
        rmsnorm(gmix_d[0])
        for l in range(depth):
            dump(f"xnT{l}", xnT[:], "fm")
            mixer_A(l)
            dump(f"ya{l}", ysT[:, 0:2, :], "fm")
            mixer_B(l)
            dump(f"yb{l}", ysT[:, 2:4, :], "fm")
            mixer_C(l)
            dump(f"yc{l}", ysT[:, 4:6, :], "fm")
            mixer_D(l)
            dump(f"yd{l}", ysT[:, 6:8, :], "fm")
            merge_out(l)
            if stop_after == f"M{l}":
                finish_early(); done = True; break
            ffn(l, is_last=(l == depth - 1), next_g=(gmix_d[l + 1] if l + 1 < depth else None))
            dump(f"h{l}", h[:], "tm")
        if not done and depth < DEPTH:
            finish_early()
        P.emit(st)
        nc._prog_stats = (len(P.ops), P.n_sems)
    return nc


def make_in_maps(inputs):
    f = lambda a: np.ascontiguousarray(np.asarray(a, dtype=np.float32))
    shared = {
        "gmix": f(inputs["norm_mix_g"]),
        "w_in": f(inputs["w_in"]),
        "wconvT": f(np.transpose(np.asarray(inputs["w_conv"]), (0, 2, 1))),
        "w_sp": f(inputs["w_spatial"]),
        "b_spT": f(np.transpose(np.asarray(inputs["b_spatial"]), (0, 2, 1))),
        "lng": f(inputs["gmlp_ln_g"]),
        "lnb": f(inputs["gmlp_ln_b"]),
        "gq": f(inputs["fox_q_norm_g"]),
        "gk": f(inputs["fox_k_norm_g"]),
        "bfg": f(inputs["fox_forget_b"]),
        "w_br": f(inputs["w_branch"]),
        "w_out": f(inputs["w_out"]),
        "gffn": f(inputs["norm_ffn_g"]),
        "w_fi": f(inputs["w_ffn_in"]),
        "w_fo": f(inputs["w_ffn_out"]),
    }
    x = np.asarray(inputs["x"], dtype=np.float32)
    return [dict(shared, x=np.ascontiguousarray(x[b])) for b in range(8)]


_NC_CACHE = {}


def kernel(**inputs):
    if "nc" not in _NC_CACHE:
        _NC_CACHE["nc"] = build_nc()
    nc = _NC_CACHE["nc"]
    in_maps = make_in_maps(inputs)
    res = run_bass_kernel_spmd(nc, in_maps, core_ids=list(range(8)))
    out = np.stack([np.asarray(r["out"], dtype=np.float32) for r in res.results], axis=0)
    return out
```

```python
import numpy as np
from contextlib import ExitStack
from concourse.bass_utils import run_bass_kernel_spmd

import concourse.bass as bass
import concourse.mybir as mybir

F32 = mybir.dt.float32
BF16 = mybir.dt.bfloat16
AF = mybir.ActivationFunctionType
ALU = mybir.AluOpType
AX = mybir.AxisListType

ENGINES = ["pe", "act", "dve", "pool", "sp"]
SEM_MAX = 30000
DMA_SEMS_PER_Q = 8


def _esize(dt):
    return mybir.dt.size(dt)


def _region(ap):
    sp = str(ap.space)
    if "SB" not in sp and "PSUM" not in sp:
        return None
    pat = ap.ap
    es = _esize(ap.dtype)
    pstride = pat[0][0]
    npart = pat[0][1]
    off = ap.offset
    if pstride > 0:
        p0 = off // pstride
        f0 = off % pstride
    else:
        p0 = 0
        f0 = off
    ext = 1
    for st, cnt in pat[1:]:
        ext += abs(st) * (cnt - 1)
    if "PSUM" in sp:
        return (ap.tensor.name, 0, 128, 0, 1 << 30)
    return (ap.tensor.name, p0, p0 + npart, f0 * es, (f0 + ext) * es)


class Op:
    __slots__ = ("idx", "eng", "fn", "reads", "writes", "is_dma", "deps",
                 "signaled", "sem", "val", "name", "small")


class Prog:
    def __init__(self, nc, same_engine_sync=False):
        self.nc = nc
        self.ops = []
        self.recs = {}
        self.same_engine_sync = same_engine_sync

    def add(self, eng, fn, reads=(), writes=(), is_dma=False, name=""):
        op = Op()
        op.idx = len(self.ops)
        op.eng = eng
        op.fn = fn
        op.is_dma = is_dma
        op.name = name
        op.signaled = False
        op.sem = None
        op.val = 0
        op.small = False
        for a in writes:
            n_el = 1
            for st_, cnt_ in a.ap[1:]:
                n_el *= cnt_
            if n_el < 128:
                op.small = True
        rr = [r for r in (_region(a) for a in reads) if r is not None]
        ww = [r for r in (_region(a) for a in writes) if r is not None]
        ww = ww + [r for r in rr if r[4] == (1 << 30)]
        rr = [r for r in rr if r[4] != (1 << 30)]
        deps = set()
        for (tn, plo, phi, blo, bhi) in rr:
            for rec in self.recs.get(tn, ()):
                if rec[5] and rec[0] < phi and plo < rec[1] and rec[2] < bhi and blo < rec[3]:
                    deps.add(rec[4])
        for (tn, plo, phi, blo, bhi) in ww:
            for rec in self.recs.get(tn, ()):
                if rec[0] < phi and plo < rec[1] and rec[2] < bhi and blo < rec[3]:
                    deps.add(rec[4])
        for (tn, plo, phi, blo, bhi) in ww:
            lst = self.recs.setdefault(tn, [])
            lst[:] = [rec for rec in lst if not (plo <= rec[0] and rec[1] <= phi and blo <= rec[2] and rec[3] <= bhi)]
            lst.append([plo, phi, blo, bhi, op.idx, True])
        for (tn, plo, phi, blo, bhi) in rr:
            lst = self.recs.setdefault(tn, [])
            if not is_dma:
                lst[:] = [rec for rec in lst if not ((not rec[5]) and (rec[4] == op.idx or (
                                                     self.ops[rec[4]].eng == eng and not self.ops[rec[4]].is_dma))
                                                     and plo <= rec[0] and rec[1] <= phi
                                                     and blo <= rec[2] and rec[3] <= bhi)]
            lst.append([plo, phi, blo, bhi, op.idx, False])
        deps.discard(op.idx)
        need = []
        for d in deps:
            dop = self.ops[d]
            if dop.eng == eng and not dop.is_dma and not is_dma and not self.same_engine_sync \
                    and (eng == "pe" or (eng != "pool" and not dop.small)):
                continue
            need.append(d)
            dop.signaled = True
        op.deps = need
        if is_dma:
            op.signaled = True
        self.ops.append(op)
        return op

    def mm(self, out, lhsT, rhs, start=True, stop=True, **kw):
        reads = [lhsT, rhs] + ([] if start else [out])
        return self.add("pe", lambda e: e.matmul(out, lhsT, rhs, start=start, stop=stop, **kw),
                        reads, [out], name="mm")

    def transpose(self, out, in_, ident):
        return self.add("pe", lambda e: e.transpose(out, in_, ident), [in_, ident], [out], name="tr")

    def act(self, out, in_, func, bias=None, scale=None, accum_out=None):
        kw = {}
        reads = [in_]
        writes = [out]
        if bias is not None:
            kw["bias"] = bias
            if not isinstance(bias, (int, float)):
                reads.append(bias)
        if scale is not None:
            kw["scale"] = scale
            if not isinstance(scale, (int, float)):
                reads.append(scale)
        if accum_out is not None:
            kw["accum_out"] = accum_out
            writes.append(accum_out)
        return self.add("act", lambda e: e.activation(out, in_, func, **kw), reads, writes, name="act")

    def tt(self, out, in0, in1, op, eng="dve"):
        return self.add(eng, lambda e: e.tensor_tensor(out, in0, in1, op), [in0, in1], [out], name="tt")

    def ts(self, out, in0, s1, s2, op0, op1=None, eng="dve"):
        reads = [in0]
        for s in (s1, s2):
            if s is not None and not isinstance(s, (int, float)):
                reads.append(s)
        if op1 is None:
            return self.add(eng, lambda e: e.tensor_scalar(out, in0, s1, None, op0), reads, [out], name="ts")
        return self.add(eng, lambda e: e.tensor_scalar(out, in0, s1, s2, op0, op1), reads, [out], name="ts")

    def stt(self, out, in0, scalar, in1, op0, op1):
        reads = [in0, in1]
        if not isinstance(scalar, (int, float)):
            reads.append(scalar)
        return self.add("dve", lambda e: e.scalar_tensor_tensor(out, in0, scalar, in1, op0, op1),
                        reads, [out], name="stt")

    def copy(self, out, in_, eng="dve"):
        if eng == "act":
            return self.add("act", lambda e: e.copy(out, in_), [in_], [out], name="copy")
        return self.add(eng, lambda e: e.tensor_copy(out, in_), [in_], [out], name="copy")

    def memset(self, ap, val, eng="dve"):
        return self.add(eng, lambda e: e.memset(ap, val), [], [ap], name="memset")

    def dma(self, out, in_, eng="sp", **kw):
        return self.add(eng, lambda e: e.dma_start(out=out, in_=in_, **kw), [in_], [out],
                        is_dma=True, name="dma")

    def emit(self, stack):
        nc = self.nc
        counters = {e: 0 for e in ENGINES}
        eng_sems = {e: [] for e in ENGINES}
        dma_sems = {e: [] for e in ENGINES}
        dma_cnt = {e: 0 for e in ENGINES}
        dma_semval = {}
        dma_hist = {e: [] for e in ENGINES}
        for op in self.ops:
            if op.is_dma:
                q = op.eng
                j = dma_cnt[q] % DMA_SEMS_PER_Q
                if len(dma_sems[q]) <= j:
                    dma_sems[q].append(stack.enter_context(nc.semaphore(f"d_{q}_{j}")))
                sem = dma_sems[q][j]
                v = dma_semval.get((q, j), 0) + 16
                dma_semval[(q, j)] = v
                op.sem = sem
                op.val = v
                if dma_cnt[q] >= DMA_SEMS_PER_Q:
                    prev = dma_hist[q][dma_cnt[q] - DMA_SEMS_PER_Q]
                    if prev.idx not in op.deps:
                        op.deps.append(prev.idx)
                dma_hist[q].append(op)
                dma_cnt[q] += 1
            elif op.signaled:
                e = op.eng
                c = counters[e]
                k = c // SEM_MAX
                if len(eng_sems[e]) <= k:
                    eng_sems[e].append(stack.enter_context(nc.semaphore(f"c_{e}_{k}")))
                op.sem = eng_sems[e][k]
                op.val = c % SEM_MAX + 1
                counters[e] = c + 1
        self.n_sems = sum(len(v) for v in eng_sems.values()) + sum(len(v) for v in dma_sems.values())
        block = stack.enter_context(nc.Block())
        ops = self.ops

        def run(engname):
            def body(e):
                waited = {}
                last = None
                for op in ops:
                    if op.eng != engname:
                        continue
                    for d in sorted(op.deps):
                        dop = ops[d]
                        key = id(dop.sem)
                        if waited.get(key, 0) >= dop.val:
                            continue
                        e.wait_ge(dop.sem, dop.val)
                        waited[key] = dop.val
                    inst = op.fn(e)
                    if op.sem is not None:
                        inst.then_inc(op.sem, 16 if op.is_dma else 1)
                    last = op
                for j, sem in enumerate(dma_sems[engname]):
                    v = dma_semval.get((engname, j), 0)
                    if v:
                        e.wait_ge(sem, v)
            return body

        block.tensor(run("pe"))
        block.scalar(run("act"))
        block.vector(run("dve"))
        block.gpsimd(run("pool"))
        block.sync(run("sp"))


S = 2048
D = 1024
NT = 16
NIN = 6916
FF = 2816
DEPTH = 2
EPS = 1e-6
NEG = -30000.0
R_BYTES = 57 * 1024


class Carver:
    def __init__(self, t):
        self.t = t
        self.off = 0

    def reset(self):
        self.off = 0

    def take(self, shape, dt):
        n = 1
        for s_ in shape:
            n *= s_
        nb = n * mybir.dt.size(dt)
        nb_al = (nb + 31) // 32 * 32
        assert self.off + nb_al <= R_BYTES, (self.off, nb_al)
        a = self.t[:, self.off // 4:(self.off + nb_al) // 4]
        self.off += nb_al
        if dt != F32:
            a = a.bitcast(dt)
        a = a[:, 0:n]
        if len(shape) == 2:
            a = a.rearrange("p (a b) -> p a b", a=shape[0])
        elif len(shape) == 3:
            a = a.rearrange("p (a b c) -> p a b c", a=shape[0], b=shape[1])
        return a


def build_nc(depth=DEPTH, dbg=()):
    nc = bass.Bass("TRN2", target_bir_lowering=False)

    def din(name, shape):
        return nc.dram_tensor(name, shape, F32, kind="ExternalInput").ap()

    x_d = din("x", [S, D])
    gmix_d = din("gmix", [DEPTH, D])
    w_in_d = din("w_in", [DEPTH, D, NIN])
    wconvT_d = din("wconvT", [DEPTH, 256, 3])
    w_sp_d = din("w_sp", [DEPTH, 4, 128, 128])
    b_spT_d = din("b_spT", [DEPTH, 128, 4])
    lng_d = din("lng", [DEPTH, 256])
    lnb_d = din("lnb", [DEPTH, 256])
    gq_d = din("gq", [DEPTH, 64])
    gk_d = din("gk", [DEPTH, 64])
    bf_d = din("bfg", [DEPTH, 4])
    w_br_d = din("w_br", [DEPTH, 4, 256, D])
    w_out_d = din("w_out", [DEPTH, D, D])
    gffn_d = din("gffn", [DEPTH, D])
    w_fi_d = din("w_fi", [DEPTH, D, 2 * FF])
    w_fo_d = din("w_fo", [DEPTH, FF, D])
    out_d = nc.dram_tensor("out", [S, D], F32, kind="ExternalOutput").ap()
    dbg_d = {}
    for name in dbg:
        if name.startswith("cpos"):
            dbg_d[name] = nc.dram_tensor("dbg_" + name, [128, 64], F32, kind="ExternalOutput").ap()
        elif name.startswith("y") or name.startswith("merged") or name.startswith("xnT") or name.startswith("fq") or name.startswith("fk"):
            shp = [128, (8 if (name.startswith("merged") or name.startswith("xnT")) else 2) * S]
            dbg_d[name] = nc.dram_tensor("dbg_" + name, shp, F32, kind="ExternalOutput").ap()
        else:
            dbg_d[name] = nc.dram_tensor("dbg_" + name, [S, D], F32, kind="ExternalOutput").ap()

    with ExitStack() as st:
        def sb(name, shape, dt):
            return st.enter_context(nc.sbuf_tensor(name, shape, dt))

        banks = [st.enter_context(nc.psum_tensor(f"bank{i}", [128, 512], F32)) for i in range(8)]
        h = sb("h", [128, NT, D], F32)
        xnT = sb("xnT", [128, 8, S], BF16)
        ysT = sb("ysT", [128, 8, S], BF16)
        wslots = [sb(f"wslot{i}", [128, 8, 512], BF16) for i in range(2)]
        Rt = sb("R", [128, R_BYTES // 4], F32)
        R = Carver(Rt)
        ident = sb("ident", [128, 128], BF16)
        negones = sb("negones", [128, 128], BF16)
        ones_bf = sb("ones_bf", [128, 128], BF16)
        uneg = sb("uneg", [128, 128], BF16)
        tri_f = sb("tri_f", [128, 128], F32)
        ones_f = sb("ones_f", [128, 128], F32)
        zeros_bf = sb("zeros_bf", [128, 128], BF16)
        sbmask = sb("sbmask", [128, 128], BF16)
        foxmask = sb("foxmask", [128, 128], BF16)
        ss = sb("ss", [128, NT], F32)
        rs = sb("rs", [128, NT], F32)
        wc = sb("wc", [128, 2, 3], F32)
        bsp = sb("bsp", [128, 4], F32)
        lng_b = sb("lng_b", [128, 256], F32)
        lnb_b = sb("lnb_b", [128, 256], F32)
        gq_b = sb("gq_b", [128, 64], F32)
        gk_b = sb("gk_b", [128, 64], F32)
        bf_b = sb("bf_b", [128, 4], F32)
        small = sb("small", [128, 256], F32)

        P = Prog(nc)
        rot_state = {}

        def rot(name, lst):
            i = rot_state.get(name, 0)
            rot_state[name] = i + 1
            return lst[i % len(lst)]

        P.memset(ones_bf[:], 1.0, eng="pool")
        P.memset(negones[:], -1.0, eng="pool")
        P.memset(zeros_bf[:], 0.0, eng="pool")
        P.memset(ones_f[:], 1.0, eng="pool")

        def asel(out, in_, pattern, op, fill, base, cm):
            P.add("pool", lambda e: e.affine_select(out, in_, pattern, op, fill, base=base, channel_multiplier=cm),
                  [in_], [out], name="asel")

        asel(ident[:], ones_bf[:], [[-1, 128]], ALU.is_equal, 0.0, 0, 1)
        asel(uneg[:], negones[:], [[-1, 128]], ALU.is_ge, 0.0, 0, 1)
        asel(tri_f[:], ones_f[:], [[1, 128]], ALU.is_ge, 0.0, 0, -1)
        asel(sbmask[:], zeros_bf[:], [[1, 128]], ALU.is_gt, NEG, 0, -1)
        asel(foxmask[:], zeros_bf[:], [[1, 128]], ALU.is_ge, NEG, 0, -1)

        xr = x_d.rearrange("(n p) d -> p n d", p=128)
        for i0 in range(0, NT, 2):
            P.dma(h[:, i0:i0 + 2, :], xr[:, i0:i0 + 2, :], eng=("sp" if (i0 // 2) % 2 == 0 else "act"))

        chunks = []
        loaded = [0]
        slot_of = {}

        def wsrc(ap2d):
            return ap2d.rearrange("(k p) n -> p k n", p=128)

        def plan_layer(l):
            w = w_in_d[l]
            for j in range(2):
                chunks.append((("A", l, j), [(w[:, g * 256 + j * 128: g * 256 + j * 128 + 128], g * 128) for g in range(3)]))
            chunks.append((("B", l), [(w[:, 768:1280], 0)]))
            chunks.append((("Cqk", l), [(w[:, 1280:1792], 0)]))
            chunks.append((("Cv", l), [(w[:, 1792:2048], 0)]))
            chunks.append((("Dqk", l), [(w[:, 2048:2560], 0)]))
            chunks.append((("Dvf", l), [(w[:, 2560:2820], 0)]))
            for dc in range(8):
                chunks.append((("G", l, dc), [(w[:, 2820 + i * 1024 + dc * 128: 2820 + i * 1024 + dc * 128 + 128], i * 128) for i in range(4)]))
            for hf in range(2):
                chunks.append((("O", l, hf), [(w_out_d[l][:, hf * 512:(hf + 1) * 512], 0)]))
            wf = w_fi_d[l]
            for ps_ in range(3):
                fcs = FFN_PASSES[ps_]
                for q in range(0, len(fcs), 2):
                    srcs = []
                    for qq, fc in enumerate(fcs[q:q + 2]):
                        srcs.append((wf[:, fc * 128:(fc + 1) * 128], qq * 256))
                        srcs.append((wf[:, FF + fc * 128: FF + (fc + 1) * 128], qq * 256 + 128))
                    chunks.append((("F", l, ps_, q // 2), srcs))

        FFN_PASSES = [list(range(0, 8)), list(range(8, 15)), list(range(15, 22))]
        for l in range(depth):
            plan_layer(l)

        def ensure_loaded(upto):
            while loaded[0] <= min(upto, len(chunks) - 1):
                ci = loaded[0]
                key, srcs = chunks[ci]
                slot = wslots[ci % len(wslots)]
                for (src, off) in srcs:
                    n = src.shape[1]
                    P.dma(slot[:, :, off:off + n], wsrc(src), eng="pool")
                slot_of[key] = (ci, slot)
                loaded[0] += 1

        def wget(key, ahead=1):
            ci = None
            for i_, (k_, _) in enumerate(chunks):
                if k_ == key:
                    ci = i_
                    break
            assert ci is not None, key
            ensure_loaded(ci + ahead)
            cj, slot = slot_of[key]
            assert cj == ci
            return slot

        def dump(name, src_ap, kind):
            if name not in dbg_d:
                return
            R2 = dbgbuf
            if kind == "fm":
                n = src_ap.shape[1]
                for c in range(n):
                    for G in range(4):
                        P.copy(R2[:, 0:512], src_ap[:, c, G * 512:(G + 1) * 512], eng="dve")
                        P.dma(dbg_d[name][:, c * S + G * 512: c * S + (G + 1) * 512], R2[:, 0:512], eng="sp")
            else:
                for i in range(NT):
                    P.dma(dbg_d[name].rearrange("(n p) d -> p n d", p=128)[:, i, :], src_ap[:, i, :], eng="sp")

        dbgbuf = sb("dbgbuf", [128, 512], F32) if dbg else None

        def norm_begin(g_row):
            nb = {}
            nb["gb"] = R.take([D], F32)
            nb["sq"] = R.take([D], BF16)
            nb["xn"] = [R.take([D], BF16) for _ in range(2)]
            P.dma(nb["gb"], g_row.partition_broadcast(128), eng="sp")
            return nb

        def norm_stage(nb, si, i):
            r_ = rs[:, i:i + 1]
            if si == 0:
                P.act(nb["sq"], h[:, i, :], AF.Square, accum_out=ss[:, i:i + 1])
            elif si == 1:
                P.ts(r_, ss[:, i:i + 1], 1.0 / D, EPS, ALU.mult, ALU.add)
            elif si == 2:
                P.act(r_, r_, AF.Sqrt)
            elif si == 3:
                P.add("dve", lambda e, o=r_: e.reciprocal(o, o), [r_], [r_], name="recip")
            elif si == 4:
                P.stt(nb["xn"][i % 2], h[:, i, :], r_, nb["gb"], ALU.mult, ALU.mult)
            elif si == 5:
                xn = nb["xn"][i % 2]
                bk = rot("tr", [0, 1])
                nb[("bk", i)] = bk
                pb = banks[bk][:].bitcast(BF16)
                for k in range(8):
                    P.transpose(pb[:, k * 128:(k + 1) * 128], xn[:, k * 128:(k + 1) * 128], ident[:])
            elif si == 6:
                pb = banks[nb[("bk", i)]][:].bitcast(BF16)
                P.copy(xnT[:, :, i * 128:(i + 1) * 128], pb.rearrange("p (k t) -> p k t", k=8), eng="act")

        NST = 7

        def norm_push(nb, t):
            for si in range(NST):
                j = t - si
                if 0 <= j < NT:
                    norm_stage(nb, si, j)

        def norm_flush(nb):
            for t in range(NT, NT + NST - 1):
                norm_push(nb, t)

        def rmsnorm(g_row):
            R.reset()
            nb = norm_begin(g_row)
            for i in range(NT):
                norm_push(nb, i)
            norm_flush(nb)

        def mixer_A(l):
            R.reset()
            P.dma(wc[:], wconvT_d[l].rearrange("(j p) k -> p j k", p=128), eng="sp")
            xcbs = [R.take([S + 2], BF16) for _ in range(2)]
            cxs = [R.take([512], F32) for _ in range(2)]
            cbs = [R.take([512], F32) for _ in range(2)]
            dg = R.take([2, 3, 128], BF16)
            for j in range(2):
                for k in range(3):
                    P.ts(dg[:, j, k, :], ident[:], wc[:, j, k:k + 1], None, ALU.mult)
            for xb_ in xcbs:
                P.memset(xb_[:, 0:2], 0.0, eng="dve")
            units = [(j, G) for j in range(2) for G in range(4)]
            ust = {}

            def front(n):
                j, G = units[n]
                if G == 0:
                    ust[("slot", j)] = wget(("A", l, j))
                slot = ust[("slot", j)]
                bset = rot("convset", [[0, 1, 2], [3, 4, 5]])
                ts_ = slice(G * 512, (G + 1) * 512)
                for gi in range(3):
                    for k in range(8):
                        P.mm(banks[bset[gi]][:], slot[:, k, gi * 128:(gi + 1) * 128], xnT[:, k, ts_],
                             start=(k == 0), stop=(k == 7))
                a = n % 2
                P.copy(cxs[a], banks[bset[2]][:], eng="act")
                P.tt(xcbs[j][:, 2 + G * 512: 2 + (G + 1) * 512], banks[bset[1]][:], cxs[a], ALU.mult)
                P.copy(cbs[a], banks[bset[0]][:], eng="act")

            def back(n):
                j, G = units[n]
                a = n % 2
                ts_ = slice(G * 512, (G + 1) * 512)
                by = rot("convy", [6, 7])
                for k in range(3):
                    P.mm(banks[by][:], dg[:, j, k, :], xcbs[j][:, G * 512 + k: G * 512 + k + 512],
                         start=(k == 0), stop=(k == 2))
                P.tt(ysT[:, 0 + j, ts_], banks[by][:], cbs[a], ALU.mult)

            for n in range(len(units) + 1):
                if n < len(units):
                    front(n)
                if n >= 1:
                    back(n - 1)

        def pipeline(n_items, stages):
            ns_ = len(stages)
            for t_ in range(n_items + ns_ - 1):
                for si_ in range(ns_):
                    j_ = t_ - si_
                    if 0 <= j_ < n_items:
                        stages[si_](j_)

        def mixer_B(l):
            R.reset()
            P.dma(bsp[:], b_spT_d[l], eng="sp")
            P.dma(lng_b[:], lng_d[l].partition_broadcast(128), eng="sp")
            P.dma(lnb_b[:], lnb_d[l].partition_broadcast(128), eng="sp")
            wsf = R.take([4, 128], F32)
            wsb = R.take([4, 128], BF16)
            wsT = R.take([4, 128], BF16)
            P.dma(wsf, w_sp_d[l].rearrange("g t s -> t g s"), eng="sp")
            for g in range(4):
                asel(wsb[:, g, :], wsf[:, g, :], [[-1, 128]], ALU.is_ge, 0.0, 0, 1)
            bk = rot("tr", [0, 1])
            pb = banks[bk][:].bitcast(BF16)
            for g in range(4):
                P.transpose(pb[:, g * 128:(g + 1) * 128], wsb[:, g, :], ident[:])
            P.copy(wsT, pb[:, 0:512].rearrange("p (g t) -> p g t", g=4), eng="dve")
            NB = 6
            guv = [R.take([512], F32) for _ in range(NB)]
            vn = [R.take([256], F32) for _ in range(NB)]
            vnb = [R.take([256], BF16) for _ in range(NB)]
            ybt = [R.take([256], BF16) for _ in range(NB)]
            slot = wget(("B", l))
            stt_ = {}

            def sm(i, lo, hi):
                b = i % NB
                return small[:, b * 16 + lo: b * 16 + hi]

            def s0(i):
                buv = rot("uv", [0, 1])
                stt_[("uv", i)] = buv
                for k in range(8):
                    P.mm(banks[buv][:], xnT[:, k, i * 128:(i + 1) * 128], slot[:, k, :], start=(k == 0), stop=(k == 7))

            def s1(i):
                P.act(guv[i % NB], banks[stt_[("uv", i)]][:], AF.Gelu_apprx_tanh)

            def s2(i):
                v = guv[i % NB][:, 256:512]
                st6, mv, rstd, dd_, m2_ = sm(i, 0, 6), sm(i, 8, 10), sm(i, 10, 11), sm(i, 11, 12), sm(i, 12, 13)
                P.add("dve", lambda e, o=st6, i_=v: e.bn_stats(o, i_), [v], [st6], name="bnstats")
                P.tt(dd_, st6[:, 1:2], st6[:, 4:5], ALU.subtract)
                P.tt(mv[:, 0:1], st6[:, 1:2], st6[:, 4:5], ALU.add)
                P.tt(m2_, st6[:, 2:3], st6[:, 5:6], ALU.add)
                P.ts(mv[:, 0:1], mv[:, 0:1], 0.5, None, ALU.mult)
                P.ts(m2_, m2_, 1.0 / 256, EPS, ALU.mult, ALU.add)
                P.tt(dd_, dd_, dd_, ALU.mult)
                P.stt(rstd, dd_, 0.25, m2_, ALU.mult, ALU.add)

            def s3(i):
                rstd = sm(i, 10, 11)
                P.act(rstd, rstd, AF.Sqrt)

            def s4(i):
                b = i % NB
                v = guv[b][:, 256:512]
                mv, rstd = sm(i, 8, 10), sm(i, 10, 11)
                P.add("dve", lambda e, o=rstd: e.reciprocal(o, o), [rstd], [rstd], name="recip")
                P.ts(vn[b], v, mv[:, 0:1], rstd, ALU.subtract, ALU.mult)
                P.tt(vn[b], vn[b], lng_b[:], ALU.mult)
                P.tt(vnb[b], vn[b], lnb_b[:], ALU.add)

            def s5(i):
                b = i % NB
                bmx = rot("mx", [2, 3])
                stt_[("mx", i)] = bmx
                for g in range(4):
                    P.mm(banks[bmx][:, g * 64:(g + 1) * 64], wsT[:, g, :], vnb[b][:, g * 64:(g + 1) * 64],
                         start=True, stop=True)

            def s6(i):
                b = i % NB
                bmx = stt_[("mx", i)]
                for g in range(4):
                    P.stt(ybt[b][:, g * 64:(g + 1) * 64], banks[bmx][:, g * 64:(g + 1) * 64], bsp[:, g:g + 1],
                          guv[b][:, g * 64:(g + 1) * 64], ALU.add, ALU.mult)

            def s7(i):
                b = i % NB
                btr = rot("trb", [4, 5])
                stt_[("tr", i)] = btr
                pb2 = banks[btr][:].bitcast(BF16)
                for c in range(2):
                    P.transpose(pb2[:, c * 128:(c + 1) * 128], ybt[b][:, c * 128:(c + 1) * 128], ident[:])

            def s8(i):
                pb2 = banks[stt_[("tr", i)]][:].bitcast(BF16)
                P.copy(ysT[:, 2:4, i * 128:(i + 1) * 128], pb2[:, 0:256].rearrange("p (c t) -> p c t", c=2), eng="act")

            pipeline(NT, [s0, s1, s2, s3, s4, s5, s6, s7, s8])

        def mixer_C(l):
            R.reset()
            qpad = R.take([4, S], BF16)
            P.memset(qpad, 0.0, eng="pool")
            kT = R.take([2, S], BF16)
            vtm = R.take([NT, 256], BF16)
            e_t = [R.take([512], F32) for _ in range(3)]
            sp_t = [R.take([512], BF16) for _ in range(3)]
            w_t = [R.take([512], BF16) for _ in range(3)]
            Sb = [R.take([512], BF16) for _ in range(2)]
            slot = wget(("Cqk", l))
            for qk in range(2):
                for c in range(2):
                    for G in range(4):
                        bk = rot("proj", [0, 1, 2, 3, 4, 5])
                        ts_ = slice(G * 512, (G + 1) * 512)
                        for k in range(8):
                            P.mm(banks[bk][:], slot[:, k, qk * 256 + c * 128: qk * 256 + (c + 1) * 128], xnT[:, k, ts_],
                                 start=(k == 0), stop=(k == 7))
                        if qk == 0:
                            P.act(qpad[0:64, 2 * c, ts_], banks[bk][0:64, :], AF.Copy, scale=0.125)
                            P.act(qpad[64:128, 2 * c + 1, ts_], banks[bk][64:128, :], AF.Copy, scale=0.125)
                        else:
                            P.copy(kT[:, c, ts_], banks[bk][:], eng="dve")
            slot = wget(("Cv", l))
            for i in range(NT):
                bk = rot("proj", [0, 1, 2, 3, 4, 5])
                for k in range(8):
                    P.mm(banks[bk][:, 0:256], xnT[:, k, i * 128:(i + 1) * 128], slot[:, k, 0:256],
                         start=(k == 0), stop=(k == 7))
                P.copy(vtm[:, i, :], banks[bk][:, 0:256], eng=("act" if i % 2 else "dve"))
            for hp in range(2):
                for G in range(4):
                    nkb = 4 * G + 4
                    jobs = []
                    for kb in range(nkb - 1, -1, -1):
                        for hh in range(2):
                            jobs.append((hh, kb))
                    OB = [6, 7]
                    for hh in range(2):
                        P.memset(Sb[hh], 0.0, eng="dve")
                    state = {}

                    def stage(si, job, jn):
                        hh, kb = job
                        po = hh * 64
                        r = kb - 4 * G
                        c0 = 128 * r if r >= 0 else 0
                        cs = slice(c0, 512)
                        qs = slice(G * 512 + c0, (G + 1) * 512)
                        ksl = slice(kb * 128, (kb + 1) * 128)
                        first = (kb == nkb - 1)
                        last = (kb == 0)
                        if si == 0:
                            zb = rot("z", [0, 1, 2, 3, 4, 5])
                            state[(job, "zb")] = zb
                            P.mm(banks[zb][:, cs], kT[:, hp, ksl], qpad[:, hp * 2 + hh, qs],
                                 start=True, stop=False)
                            if r >= 0:
                                P.mm(banks[zb][:, c0:c0 + 128], ident[:], sbmask[:], start=False, stop=False)
                        elif si == 1:
                            zb = state[(job, "zb")]
                            a = jn % 3
                            P.act(e_t[a][:, cs], banks[zb][:, cs], AF.Exp)
                            P.act(sp_t[a][:, cs], e_t[a][:, cs], AF.Ln, bias=1.0)
                        elif si == 2:
                            a = jn % 3
                            zb = state[(job, "zb")]
                            P.mm(banks[zb][:, cs], uneg[:], sp_t[a][:, cs], start=False, stop=first,
                                 skip_group_check=True)
                            if not first:
                                P.mm(banks[zb][:, cs], negones[:], Sb[hh][:, cs], start=False, stop=True,
                                     skip_group_check=True)
                            if not last:
                                P.tt(Sb[hh][:, cs], Sb[hh][:, cs], sp_t[a][:, cs], ALU.add)
                        elif si == 3:
                            a = jn % 3
                            zb = state[(job, "zb")]
                            P.act(w_t[a][:, cs], banks[zb][:, cs], AF.Exp)
                        elif si == 4:
                            a = jn % 3
                            P.mm(banks[OB[hh]][:, cs], vtm[:, kb, hp * 128:(hp + 1) * 128], w_t[a][:, cs],
                                 start=first, stop=last)
                            if last:
                                P.copy(ysT[po:po + 64, 4 + hp, G * 512:(G + 1) * 512], banks[OB[hh]][po:po + 64, :],
                                       eng="dve")

                    nst = 5
                    for t in range(len(jobs) + nst - 1):
                        for si in range(nst):
                            jn = t - si
                            if 0 <= jn < len(jobs):
                                stage(si, jobs[jn], jn)

        def mixer_D(l):
            R.reset()
            P.dma(gq_b[:], gq_d[l].partition_broadcast(128), eng="sp")
            P.dma(gk_b[:], gk_d[l].partition_broadcast(128), eng="sp")
            P.dma(bf_b[:], bf_d[l].partition_broadcast(128), eng="sp")
            P.ts(gq_b[:], gq_b[:], 0.125, None, ALU.mult)
            qpad = R.take([4, S], BF16)
            kpad = R.take([4, S], BF16)
            P.memset(qpad, 0.0, eng="pool")
            P.memset(kpad, 0.0, eng="pool")
            qpv = qpad.rearrange("p (c two) s -> p c two s", two=2)
            kpv = kpad.rearrange("p (c two) s -> p c two s", two=2)
            vtm = R.take([NT, 2, 192], BF16)
            P.memset(vtm[:, :, :, 64:128], 1.0, eng="pool")
            LF = R.take([NT, 4], F32)
            cpos = R.take([NT, 4], F32)
            carry = R.take([NT, 4], F32)
            r1 = carry
            r2 = LF
            cbf = R.take([NT, 4], BF16)
            off_shared = R.off
            ND = 4
            sq = [R.take([512], BF16) for _ in range(2)]
            qkc = [R.take([512], F32) for _ in range(ND)]
            qkn = [R.take([512], BF16) for _ in range(2)]
            R.off = off_shared
            TEq = R.take([NT, 4, 8], BF16)
            TEk = R.take([NT, 4, 8], BF16)
            p_t = [R.take([512], BF16) for _ in range(3)]
            rec_one = R.take([512], F32)
            rec = [rec_one, rec_one]
            slot_qk = wget(("Dqk", l))
            slot_vf = wget(("Dvf", l), ahead=0)
            stt_ = {}

            def smf(i, lo, hi):
                b = i % ND
                return small[:, 128 + b * 16 + lo: 128 + b * 16 + hi]

            def s0(i):
                tsl = slice(i * 128, (i + 1) * 128)
                bq = rot("fq", [0, 1])
                bv = rot("fv", [2, 3])
                stt_[("bq", i)] = bq
                stt_[("bv", i)] = bv
                for k in range(8):
                    P.mm(banks[bq][:], xnT[:, k, tsl], slot_qk[:, k, :], start=(k == 0), stop=(k == 7))
                for k in range(8):
                    P.mm(banks[bv][:, 0:260], xnT[:, k, tsl], slot_vf[:, k, 0:260], start=(k == 0), stop=(k == 7))

            def s1(i):
                bq, bv = stt_[("bq", i)], stt_[("bv", i)]
                P.act(sq[i % 2], banks[bq][:], AF.Square)
                P.copy(qkc[i % ND], banks[bq][:], eng="act")
                vsrc = banks[bv][:, 0:256].rearrange("p (c two d) -> p c two d", c=2, two=2)
                P.copy(vtm[:, i, :, 0:64], vsrc[:, :, 0, :], eng="act")
                P.copy(vtm[:, i, :, 128:192], vsrc[:, :, 1, :], eng="act")
                P.tt(smf(i, 0, 4), banks[bv][:, 256:260], bf_b[:], ALU.add)

            def s2(i):
                ssq = smf(i, 8, 16)
                P.add("dve", lambda e, o=ssq, i_=sq[i % 2]: e.tensor_reduce(o, i_.rearrange("p (j d) -> p j d", j=8), AX.X, ALU.add),
                      [sq[i % 2]], [ssq], name="tred")
                P.ts(ssq, ssq, 1.0 / 64, EPS, ALU.mult, ALU.add)

            def s3(i):
                fb = smf(i, 0, 4)
                ssq = smf(i, 8, 16)
                P.act(fb, fb, AF.Exp, scale=-1.0)
                P.act(LF[:, i, :], fb, AF.Ln, bias=1.0)
                P.act(ssq, ssq, AF.Sqrt)

            def s4(i):
                ssq = smf(i, 8, 16)
                P.add("dve", lambda e, o=ssq: e.reciprocal(o, o), [ssq], [ssq], name="recip")
                for j in range(8):
                    gbt = gq_b if j < 4 else gk_b
                    P.stt(qkn[i % 2][:, j * 64:(j + 1) * 64], qkc[i % ND][:, j * 64:(j + 1) * 64], ssq[:, j:j + 1], gbt[:],
                          ALU.mult, ALU.mult)

            def s5(i):
                btr = rot("ftr", [4, 5])
                stt_[("tr", i)] = btr
                pb = banks[btr][:].bitcast(BF16)
                for c in range(4):
                    P.transpose(pb[:, c * 128:(c + 1) * 128], qkn[i % 2][:, c * 128:(c + 1) * 128], ident[:])

            def s6(i):
                tsl = slice(i * 128, (i + 1) * 128)
                pb = banks[stt_[("tr", i)]][:].bitcast(BF16)
                P.copy(qpv[0:64, :, 0, tsl], pb[0:64, 0:256].rearrange("p (c t) -> p c t", c=2), eng="act")
                P.copy(qpv[64:128, :, 1, tsl], pb[64:128, 0:256].rearrange("p (c t) -> p c t", c=2), eng="act")
                P.copy(kpv[0:64, :, 0, tsl], pb[0:64, 256:512].rearrange("p (c t) -> p c t", c=2), eng="dve")
                P.copy(kpv[64:128, :, 1, tsl], pb[64:128, 256:512].rearrange("p (c t) -> p c t", c=2), eng="dve")

            pipeline(NT, [s0, s1, s2, s3, s4, s5, s6])
            LF2 = LF.rearrange("p i h -> p (i h)")
            P.mm(banks[6][:, 0:64], tri_f[:], LF2, start=True, stop=True)
            P.mm(banks[7][:, 0:64], ones_f[:], LF2, start=True, stop=True)
            P.memset(carry[:, 0, :], 0.0, eng="dve")
            for i in range(1, NT):
                P.tt(carry[:, i, :], carry[:, i - 1, :], banks[7][:, (i - 1) * 4: i * 4], ALU.add)
            cp2 = cpos.rearrange("p i h -> p (i h)")
            P.tt(cp2, banks[6][:, 0:64], carry.rearrange("p i h -> p (i h)"), ALU.add)
            P.memset(TEq, 1.0, eng="dve")
            P.memset(TEk, 1.0, eng="dve")
            cur = cpos
            for t_, nxt in enumerate((r1, r2, None)):
                P.copy(cbf, cur, eng="dve")
                P.copy(TEk[:, :, :, 3 + t_], cbf, eng="dve")
                P.ts(TEq[:, :, :, t_], cbf, -1.0, None, ALU.mult)
                if nxt is not None:
                    P.tt(nxt, cur, cbf, ALU.subtract)
                    cur = nxt
            if f"cpos{l}" in dbg_d:
                P.dma(dbg_d[f"cpos{l}"], cp2, eng="sp")
            for (TE, XP) in ((TEq, qpad), (TEk, kpad)):
                for hd in range(4):
                    opo = 64 - (hd % 2) * 64
                    for G in range(4):
                        btr = rot("lb", [0, 1, 2, 3])
                        pb = banks[btr][:].bitcast(BF16)
                        for ii in range(4):
                            P.transpose(pb[0:8, ii * 128:(ii + 1) * 128], TE[:, G * 4 + ii, hd, :], ident[:])
                        P.copy(XP[opo:opo + 6, hd, G * 512:(G + 1) * 512], pb[0:6, 0:512], eng="dve")
            for hp in range(2):
                for G in range(4):
                    nkb = 4 * G + 4
                    jobs = []
                    for kb in range(nkb - 1, -1, -1):
                        for hh in range(2):
                            jobs.append((hh, kb))
                    OBn = [4, 5] if (hp * 4 + G) % 2 == 0 else [6, 7]
                    state = {}

                    def stage(si, job, jn):
                        hh, kb = job
                        po = hh * 64
                        r = kb - 4 * G
                        c0 = 128 * r if r >= 0 else 0
                        cs = slice(c0, 512)
                        qs = slice(G * 512 + c0, (G + 1) * 512)
                        ksl = slice(kb * 128, (kb + 1) * 128)
                        first = (kb == nkb - 1)
                        last = (kb == 0)
                        a = jn % 3
                        if si == 0:
                            lb = rot("lb", [0, 1, 2, 3])
                            state[(job, "lb")] = lb
                            P.mm(banks[lb][:, cs], kpad[:, hp * 2 + hh, ksl], qpad[:, hp * 2 + hh, qs],
                                 start=True, stop=(r < 0))
                            if r >= 0:
                                P.mm(banks[lb][:, c0:c0 + 128], ident[:], foxmask[:], start=False, stop=True)
                        elif si == 1:
                            lb = state[(job, "lb")]
                            P.act(p_t[a][:, cs], banks[lb][:, cs], AF.Exp)
                        elif si == 3:
                            opo = 64 - po
                            P.mm(banks[OBn[hh]][:, cs], vtm[:, kb, hp, hh * 64: hh * 64 + 128], p_t[a][:, cs],
                                 start=first, stop=last)
                            if last:
                                rc = rec[hh]
                                P.add("dve", lambda e, o=rc[opo:opo + 64, :], i_=banks[OBn[hh]][opo:opo + 64, :]: e.reciprocal(o, i_),
                                      [banks[OBn[hh]][opo:opo + 64, :]], [rc[opo:opo + 64, :]], name="recip")
                                P.tt(ysT[po:po + 64, 6 + hp, G * 512:(G + 1) * 512], banks[OBn[hh]][po:po + 64, :],
                                     rc[opo:opo + 64, :], ALU.mult)

                    nst = 4
                    for t in range(len(jobs) + nst - 1):
                        for si in range(nst):
                            jn = t - si
                            if 0 <= jn < len(jobs):
                                stage(si, jobs[jn], jn)

        def merge_out(l):
            R.reset()
            mT = R.take([8, S], BF16)
            off_after_mT = R.off
            wb = [R.take([4, 2, 128], BF16) for _ in range(2)]
            sg = [R.take([512], F32) for _ in range(2)]
            macc = [R.take([512], F32) for _ in range(2)]
            tmp = [R.take([512], F32) for _ in range(2)]
            n = 0
            for dc in range(8):
                w_b = wb[dc % 2]
                for i in range(4):
                    P.dma(w_b[:, i, :, :], w_br_d[l][i, :, dc * 128:(dc + 1) * 128].rearrange("(j p) c -> p j c", p=128),
                          eng="pool")
                slot = wget(("G", l, dc))
                for G in range(4):
                    ts_ = slice(G * 512, (G + 1) * 512)
                    mc = macc[(dc * 4 + G) % 2]
                    for i in range(4):
                        gbk = rot("gate", [0, 1, 2, 3])
                        bbk = rot("br", [4, 5, 6, 7])
                        for k in range(8):
                            P.mm(banks[gbk][:], slot[:, k, i * 128:(i + 1) * 128], xnT[:, k, ts_],
                                 start=(k == 0), stop=(k == 7))
                        for j in range(2):
                            P.mm(banks[bbk][:], w_b[:, i, j, :], ysT[:, i * 2 + j, ts_], start=(j == 0), stop=(j == 1))
                        s_ = sg[n % 2]
                        t_ = tmp[n % 2]
                        n += 1
                        P.act(s_, banks[gbk][:], AF.Sigmoid)
                        if i == 0:
                            P.tt(mc, s_, banks[bbk][:], ALU.mult)
                        else:
                            P.tt(t_, s_, banks[bbk][:], ALU.mult)
                            if i < 3:
                                P.tt(mc, mc, t_, ALU.add)
                            else:
                                P.tt(mT[:, dc, ts_], mc, t_, ALU.add)
            dump(f"merged{l}", mT, "fm")
            R.off = off_after_mT
            nb = norm_begin(gffn_d[l])
            for hf in range(2):
                slot = wget(("O", l, hf))
                for i in range(NT):
                    bk = rot("wo", [2, 3, 4, 5, 6, 7])
                    for k in range(8):
                        P.mm(banks[bk][:], mT[:, k, i * 128:(i + 1) * 128], slot[:, k, :], start=(k == 0), stop=(k == 7))
                    hs = h[:, i, hf * 512:(hf + 1) * 512]
                    P.tt(hs, hs, banks[bk][:], ALU.add)
                    if hf == 1:
                        norm_push(nb, i)
            norm_flush(nb)
            dump(f"hmix{l}", h[:], "tm")

        def ffn(l, is_last, next_g=None):
            R.reset()
            hidT = ysT
            WoF2 = [R.take([8, D], BF16) for _ in range(2)]
            sg = [R.take([512], F32) for _ in range(2)]
            n = 0
            for ps_ in range(3):
                fcs = FFN_PASSES[ps_]
                nf = len(fcs)
                WoF = WoF2[ps_ % 2]
                for hf in range(2):
                    P.dma(WoF[:, 0:nf, hf * 512:(hf + 1) * 512],
                          w_fo_d[l][fcs[0] * 128:(fcs[-1] + 1) * 128, hf * 512:(hf + 1) * 512].rearrange("(f p) c -> p f c", p=128),
                          eng="pool")
                for q in range(0, nf, 2):
                    slot = wget(("F", l, ps_, q // 2))
                    for qq, fc in enumerate(fcs[q:q + 2]):
                        fl = q + qq
                        for G in range(4):
                            ts_ = slice(G * 512, (G + 1) * 512)
                            gbk = rot("gate", [0, 1, 2, 3])
                            ubk = rot("br", [4, 5, 6, 7])
                            for k in range(8):
                                P.mm(banks[gbk][:], slot[:, k, qq * 256: qq * 256 + 128], xnT[:, k, ts_],
                                     start=(k == 0), stop=(k == 7))
                            for k in range(8):
                                P.mm(banks[ubk][:], slot[:, k, qq * 256 + 128: qq * 256 + 256], xnT[:, k, ts_],
                                     start=(k == 0), stop=(k == 7))
                            s_ = sg[n % 2]
                            n += 1
                            P.act(s_, banks[gbk][:], AF.Silu)
                            P.tt(hidT[:, fl, ts_], s_, banks[ubk][:], ALU.mult)
                nb = None
                if ps_ == 2 and next_g is not None:
                    nb = norm_begin(next_g)
                for i in range(NT):
                    for hf in range(2):
                        bk = rot("wo", [2, 3, 4, 5, 6, 7])
                        for fl in range(nf):
                            P.mm(banks[bk][:], hidT[:, fl, i * 128:(i + 1) * 128], WoF[:, fl, hf * 512:(hf + 1) * 512],
                                 start=(fl == 0), stop=(fl == nf - 1))
                        hs = h[:, i, hf * 512:(hf + 1) * 512]
                        P.tt(hs, hs, banks[bk][:], ALU.add)
                    if is_last and ps_ == 2:
                        P.dma(out_d.rearrange("(n p) d -> p n d", p=128)[:, i, :], h[:, i, :], eng="sp")
                    if nb is not None:
                        norm_push(nb, i)
                if nb is not None:
                    norm_flush(nb)

        stop_after = [s_ for s_ in dbg if s_.startswith("stop:")]
        stop_after = stop_after[0][5:] if stop_after else None

        def finish_early():
            for i in range(NT):
                P.dma(out_d.rearrange("(n p) d -> p n d", p=128)[:, i, :], h[:, i, :], eng="sp")

        done = False
        rmsnorm(gmix_d[0])
        for l in range(depth):
            dump(f"xnT{l}", xnT[:], "fm")
            mixer_A(l)
            dump(f"ya{l}", ysT[:, 0:2, :], "fm")
            mixer_B(l)
            dump(f"yb{l}", ysT[:, 2:4, :], "fm")
            mixer_C(l)
            dump(f"yc{l}", ysT[:, 4:6, :], "fm")
            mixer_D(l)
            dump(f"yd{l}", ysT[:, 6:8, :], "fm")
            merge_out(l)
            if stop_after == f"M{l}":
                finish_early(); done = True; break
            ffn(l, is_last=(l == depth - 1), next_g=(gmix_d[l + 1] if l + 1 < depth else None))
            dump(f"h{l}", h[:], "tm")
        if not done and depth < DEPTH:
            finish_early()
        P.emit(st)
        nc._prog_stats = (len(P.ops), P.n_sems)
    return nc


def make_in_maps(inputs):
    f = lambda a: np.ascontiguousarray(np.asarray(a, dtype=np.float32))
    shared = {
        "gmix": f(inputs["norm_mix_g"]),
        "w_in": f(inputs["w_in"]),
        "wconvT": f(np.transpose(np.asarray(inputs["w_conv"]), (0, 2, 1))),
        "w_sp": f(inputs["w_spatial"]),
        "b_spT": f(np.transpose(np.asarray(inputs["b_spatial"]), (0, 2, 1))),
        "lng": f(inputs["gmlp_ln_g"]),
        "lnb": f(inputs["gmlp_ln_b"]),
        "gq": f(inputs["fox_q_norm_g"]),
        "gk": f(inputs["fox_k_norm_g"]),
        "bfg": f(inputs["fox_forget_b"]),
        "w_br": f(inputs["w_branch"]),
        "w_out": f(inputs["w_out"]),
        "gffn": f(inputs["norm_ffn_g"]),
        "w_fi": f(inputs["w_ffn_in"]),
        "w_fo": f(inputs["w_ffn_out"]),
    }
    x = np.asarray(inputs["x"], dtype=np.float32)
    return [dict(shared, x=np.ascontiguousarray(x[b])) for b in range(8)]


_NC_CACHE = {}


def kernel(**inputs):
    if "nc" not in _NC_CACHE:
        _NC_CACHE["nc"] = build_nc()
    nc = _NC_CACHE["nc"]
    in_maps = make_in_maps(inputs)
    res = run_bass_kernel_spmd(nc, in_maps, core_ids=list(range(8)))
    out = np.stack([np.asarray(r["out"], dtype=np.float32) for r in res.results], axis=0)
    return out
```

```python
import numpy as np
from contextlib import ExitStack
from concourse.bass_utils import run_bass_kernel_spmd

import concourse.bass as bass
import concourse.mybir as mybir

F32 = mybir.dt.float32
BF16 = mybir.dt.bfloat16
AF = mybir.ActivationFunctionType
ALU = mybir.AluOpType
AX = mybir.AxisListType

ENGINES = ["pe", "act", "dve", "pool", "sp"]
SEM_MAX = 30000
DMA_SEMS_PER_Q = 8


def _esize(dt):
    return mybir.dt.size(dt)


def _region(ap):
    sp = str(ap.space)
    if "SB" not in sp and "PSUM" not in sp:
        return None
    pat = ap.ap
    es = _esize(ap.dtype)
    pstride = pat[0][0]
    npart = pat[0][1]
    off = ap.offset
    if pstride > 0:
        p0 = off // pstride
        f0 = off % pstride
    else:
        p0 = 0
        f0 = off
    ext = 1
    for st, cnt in pat[1:]:
        ext += abs(st) * (cnt - 1)
    if "PSUM" in sp:
        return (ap.tensor.name, 0, 128, 0, 1 << 30)
    return (ap.tensor.name, p0, p0 + npart, f0 * es, (f0 + ext) * es)


class Op:
    __slots__ = ("idx", "eng", "fn", "reads", "writes", "is_dma", "deps",
                 "signaled", "sem", "val", "name", "small")


class Prog:
    def __init__(self, nc, same_engine_sync=False):
        self.nc = nc
        self.ops = []
        self.recs = {}
        self.same_engine_sync = same_engine_sync

    def add(self, eng, fn, reads=(), writes=(), is_dma=False, name=""):
        op = Op()
        op.idx = len(self.ops)
        op.eng = eng
        op.fn = fn
        op.is_dma = is_dma
        op.name = name
        op.signaled = False
        op.sem = None
        op.val = 0
        op.small = False
        for a in writes:
            n_el = 1
            for st_, cnt_ in a.ap[1:]:
                n_el *= cnt_
            if n_el < 128:
                op.small = True
        rr = [r for r in (_region(a) for a in reads) if r is not None]
        ww = [r for r in (_region(a) for a in writes) if r is not None]
        ww = ww + [r for r in rr if r[4] == (1 << 30)]
        rr = [r for r in rr if r[4] != (1 << 30)]
        deps = set()
        for (tn, plo, phi, blo, bhi) in rr:
            for rec in self.recs.get(tn, ()):
                if rec[5] and rec[0] < phi and plo < rec[1] and rec[2] < bhi and blo < rec[3]:
                    deps.add(rec[4])
        for (tn, plo, phi, blo, bhi) in ww:
            for rec in self.recs.get(tn, ()):
                if rec[0] < phi and plo < rec[1] and rec[2] < bhi and blo < rec[3]:
                    deps.add(rec[4])
        for (tn, plo, phi, blo, bhi) in ww:
            lst = self.recs.setdefault(tn, [])
            lst[:] = [rec for rec in lst if not (plo <= rec[0] and rec[1] <= phi and blo <= rec[2] and rec[3] <= bhi)]
            lst.append([plo, phi, blo, bhi, op.idx, True])
        for (tn, plo, phi, blo, bhi) in rr:
            lst = self.recs.setdefault(tn, [])
            if not is_dma:
                lst[:] = [rec for rec in lst if not ((not rec[5]) and (rec[4] == op.idx or (
                                                     self.ops[rec[4]].eng == eng and not self.ops[rec[4]].is_dma))
                                                     and plo <= rec[0] and rec[1] <= phi
                                                     and blo <= rec[2] and rec[3] <= bhi)]
            lst.append([plo, phi, blo, bhi, op.idx, False])
        deps.discard(op.idx)
        need = []
        for d in deps:
            dop = self.ops[d]
            if dop.eng == eng and not dop.is_dma and not is_dma and not self.same_engine_sync \
                    and (eng == "pe" or (eng != "pool" and not dop.small)):
                continue
            need.append(d)
            dop.signaled = True
        op.deps = need
        if is_dma:
            op.signaled = True
        self.ops.append(op)
        return op

    def mm(self, out, lhsT, rhs, start=True, stop=True, **kw):
        reads = [lhsT, rhs] + ([] if start else [out])
        return self.add("pe", lambda e: e.matmul(out, lhsT, rhs, start=start, stop=stop, **kw),
                        reads, [out], name="mm")

    def transpose(self, out, in_, ident):
        return self.add("pe", lambda e: e.transpose(out, in_, ident), [in_, ident], [out], name="tr")

    def act(self, out, in_, func, bias=None, scale=None, accum_out=None):
        kw = {}
        reads = [in_]
        writes = [out]
        if bias is not None:
            kw["bias"] = bias
            if not isinstance(bias, (int, float)):
                reads.append(bias)
        if scale is not None:
            kw["scale"] = scale
            if not isinstance(scale, (int, float)):
                reads.append(scale)
        if accum_out is not None:
            kw["accum_out"] = accum_out
            writes.append(accum_out)
        return self.add("act", lambda e: e.activation(out, in_, func, **kw), reads, writes, name="act")

    def tt(self, out, in0, in1, op, eng="dve"):
        return self.add(eng, lambda e: e.tensor_tensor(out, in0, in1, op), [in0, in1], [out], name="tt")

    def ts(self, out, in0, s1, s2, op0, op1=None, eng="dve"):
        reads = [in0]
        for s in (s1, s2):
            if s is not None and not isinstance(s, (int, float)):
                reads.append(s)
        if op1 is None:
            return self.add(eng, lambda e: e.tensor_scalar(out, in0, s1, None, op0), reads, [out], name="ts")
        return self.add(eng, lambda e: e.tensor_scalar(out, in0, s1, s2, op0, op1), reads, [out], name="ts")

    def stt(self, out, in0, scalar, in1, op0, op1):
        reads = [in0, in1]
        if not isinstance(scalar, (int, float)):
            reads.append(scalar)
        return self.add("dve", lambda e: e.scalar_tensor_tensor(out, in0, scalar, in1, op0, op1),
                        reads, [out], name="stt")

    def copy(self, out, in_, eng="dve"):
        if eng == "act":
            return self.add("act", lambda e: e.copy(out, in_), [in_], [out], name="copy")
        return self.add(eng, lambda e: e.tensor_copy(out, in_), [in_], [out], name="copy")

    def memset(self, ap, val, eng="dve"):
        return self.add(eng, lambda e: e.memset(ap, val), [], [ap], name="memset")

    def dma(self, out, in_, eng="sp", **kw):
        return self.add(eng, lambda e: e.dma_start(out=out, in_=in_, **kw), [in_], [out],
                        is_dma=True, name="dma")

    def emit(self, stack):
        nc = self.nc
        counters = {e: 0 for e in ENGINES}
        eng_sems = {e: [] for e in ENGINES}
        dma_sems = {e: [] for e in ENGINES}
        dma_cnt = {e: 0 for e in ENGINES}
        dma_semval = {}
        dma_hist = {e: [] for e in ENGINES}
        for op in self.ops:
            if op.is_dma:
                q = op.eng
                j = dma_cnt[q] % DMA_SEMS_PER_Q
                if len(dma_sems[q]) <= j:
                    dma_sems[q].append(stack.enter_context(nc.semaphore(f"d_{q}_{j}")))
                sem = dma_sems[q][j]
                v = dma_semval.get((q, j), 0) + 16
                dma_semval[(q, j)] = v
                op.sem = sem
                op.val = v
                if dma_cnt[q] >= DMA_SEMS_PER_Q:
                    prev = dma_hist[q][dma_cnt[q] - DMA_SEMS_PER_Q]
                    if prev.idx not in op.deps:
                        op.deps.append(prev.idx)
                dma_hist[q].append(op)
                dma_cnt[q] += 1
            elif op.signaled:
                e = op.eng
                c = counters[e]
                k = c // SEM_MAX
                if len(eng_sems[e]) <= k:
                    eng_sems[e].append(stack.enter_context(nc.semaphore(f"c_{e}_{k}")))
                op.sem = eng_sems[e][k]
                op.val = c % SEM_MAX + 1
                counters[e] = c + 1
        self.n_sems = sum(len(v) for v in eng_sems.values()) + sum(len(v) for v in dma_sems.values())
        block = stack.enter_context(nc.Block())
        ops = self.ops

        def run(engname):
            def body(e):
                waited = {}
                last = None
                for op in ops:
                    if op.eng != engname:
                        continue
                    for d in sorted(op.deps):
                        dop = ops[d]
                        key = id(dop.sem)
                        if waited.get(key, 0) >= dop.val:
                            continue
                        e.wait_ge(dop.sem, dop.val)
                        waited[key] = dop.val
                    inst = op.fn(e)
                    if op.sem is not None:
                        inst.then_inc(op.sem, 16 if op.is_dma else 1)
                    last = op
                for j, sem in enumerate(dma_sems[engname]):
                    v = dma_semval.get((engname, j), 0)
                    if v:
                        e.wait_ge(sem, v)
            return body

        block.tensor(run("pe"))
        block.scalar(run("act"))
        block.vector(run("dve"))
        block.gpsimd(run("pool"))
        block.sync(run("sp"))


S = 2048
D = 1024
NT = 16
NIN = 6916
FF = 2816
DEPTH = 2
EPS = 1e-6
NEG = -30000.0
R_BYTES = 57 * 1024


class Carver:
    def __init__(self, t):
        self.t = t
        self.off = 0

    def reset(self):
        self.off = 0

    def take(self, shape, dt):
        n = 1
        for s_ in shape:
            n *= s_
        nb = n * mybir.dt.size(dt)
        nb_al = (nb + 31) // 32 * 32
        assert self.off + nb_al <= R_BYTES, (self.off, nb_al)
        a = self.t[:, self.off // 4:(self.off + nb_al) // 4]
        self.off += nb_al
        if dt != F32:
            a = a.bitcast(dt)
        a = a[:, 0:n]
        if len(shape) == 2:
            a = a.rearrange("p (a b) -> p a b", a=shape[0])
        elif len(shape) == 3:
            a = a.rearrange("p (a b c) -> p a b c", a=shape[0], b=shape[1])
        return a


def build_nc(depth=DEPTH, dbg=()):
    nc = bass.Bass("TRN2", target_bir_lowering=False)

    def din(name, shape):
        return nc.dram_tensor(name, shape, F32, kind="ExternalInput").ap()

    x_d = din("x", [S, D])
    gmix_d = din("gmix", [DEPTH, D])
    w_in_d = din("w_in", [DEPTH, D, NIN])
    wconvT_d = din("wconvT", [DEPTH, 256, 3])
    w_sp_d = din("w_sp", [DEPTH, 4, 128, 128])
    b_spT_d = din("b_spT", [DEPTH, 128, 4])
    lng_d = din("lng", [DEPTH, 256])
    lnb_d = din("lnb", [DEPTH, 256])
    gq_d = din("gq", [DEPTH, 64])
    gk_d = din("gk", [DEPTH, 64])
    bf_d = din("bfg", [DEPTH, 4])
    w_br_d = din("w_br", [DEPTH, 4, 256, D])
    w_out_d = din("w_out", [DEPTH, D, D])
    gffn_d = din("gffn", [DEPTH, D])
    w_fi_d = din("w_fi", [DEPTH, D, 2 * FF])
    w_fo_d = din("w_fo", [DEPTH, FF, D])
    out_d = nc.dram_tensor("out", [S, D], F32, kind="ExternalOutput").ap()
    dbg_d = {}
    for name in dbg:
        if name.startswith("cpos"):
            dbg_d[name] = nc.dram_tensor("dbg_" + name, [128, 64], F32, kind="ExternalOutput").ap()
        elif name.startswith("y") or name.startswith("merged") or name.startswith("xnT") or name.startswith("fq") or name.startswith("fk"):
            shp = [128, (8 if (name.startswith("merged") or name.startswith("xnT")) else 2) * S]
            dbg_d[name] = nc.dram_tensor("dbg_" + name, shp, F32, kind="ExternalOutput").ap()
        else:
            dbg_d[name] = nc.dram_tensor("dbg_" + name, [S, D], F32, kind="ExternalOutput").ap()

    with ExitStack() as st:
        def sb(name, shape, dt):
            return st.enter_context(nc.sbuf_tensor(name, shape, dt))

        banks = [st.enter_context(nc.psum_tensor(f"bank{i}", [128, 512], F32)) for i in range(8)]
        h = sb("h", [128, NT, D], F32)
        xnT = sb("xnT", [128, 8, S], BF16)
        ysT = sb("ysT", [128, 8, S], BF16)
        wslots = [sb(f"wslot{i}", [128, 8, 512], BF16) for i in range(2)]
        Rt = sb("R", [128, R_BYTES // 4], F32)
        R = Carver(Rt)
        ident = sb("ident", [128, 128], BF16)
        negones = sb("negones", [128, 128], BF16)
        ones_bf = sb("ones_bf", [128, 128], BF16)
        uneg = sb("uneg", [128, 128], BF16)
        tri_f = sb("tri_f", [128, 128], F32)
        ones_f = sb("ones_f", [128, 128], F32)
        zeros_bf = sb("zeros_bf", [128, 128], BF16)
        sbmask = sb("sbmask", [128, 128], BF16)
        foxmask = sb("foxmask", [128, 128], BF16)
        ss = sb("ss", [128, NT], F32)
        rs = sb("rs", [128, NT], F32)
        wc = sb("wc", [128, 2, 3], F32)
        bsp = sb("bsp", [128, 4], F32)
        lng_b = sb("lng_b", [128, 256], F32)
        lnb_b = sb("lnb_b", [128, 256], F32)
        gq_b = sb("gq_b", [128, 64], F32)
        gk_b = sb("gk_b", [128, 64], F32)
        bf_b = sb("bf_b", [128, 4], F32)
        small = sb("small", [128, 256], F32)

        P = Prog(nc)
        rot_state = {}

        def rot(name, lst):
            i = rot_state.get(name, 0)
            rot_state[name] = i + 1
            return lst[i % len(lst)]

        P.memset(ones_bf[:], 1.0, eng="pool")
        P.memset(negones[:], -1.0, eng="pool")
        P.memset(zeros_bf[:], 0.0, eng="pool")
        P.memset(ones_f[:], 1.0, eng="pool")

        def asel(out, in_, pattern, op, fill, base, cm):
            P.add("pool", lambda e: e.affine_select(out, in_, pattern, op, fill, base=base, channel_multiplier=cm),
                  [in_], [out], name="asel")

        asel(ident[:], ones_bf[:], [[-1, 128]], ALU.is_equal, 0.0, 0, 1)
        asel(uneg[:], negones[:], [[-1, 128]], ALU.is_ge, 0.0, 0, 1)
        asel(tri_f[:], ones_f[:], [[1, 128]], ALU.is_ge, 0.0, 0, -1)
        asel(sbmask[:], zeros_bf[:], [[1, 128]], ALU.is_gt, NEG, 0, -1)
        asel(foxmask[:], zeros_bf[:], [[1, 128]], ALU.is_ge, NEG, 0, -1)

        xr = x_d.rearrange("(n p) d -> p n d", p=128)
        for i0 in range(0, NT, 2):
            P.dma(h[:, i0:i0 + 2, :], xr[:, i0:i0 + 2, :], eng=("sp" if (i0 // 2) % 2 == 0 else "act"))

        chunks = []
        loaded = [0]
        slot_of = {}

        def wsrc(ap2d):
            return ap2d.rearrange("(k p) n -> p k n", p=128)

        def plan_layer(l):
            w = w_in_d[l]
            for j in range(2):
                chunks.append((("A", l, j), [(w[:, g * 256 + j * 128: g * 256 + j * 128 + 128], g * 128) for g in range(3)]))
            chunks.append((("B", l), [(w[:, 768:1280], 0)]))
            chunks.append((("Cqk", l), [(w[:, 1280:1792], 0)]))
            chunks.append((("Cv", l), [(w[:, 1792:2048], 0)]))
            chunks.append((("Dqk", l), [(w[:, 2048:2560], 0)]))
            chunks.append((("Dvf", l), [(w[:, 2560:2820], 0)]))
            for dc in range(8):
                chunks.append((("G", l, dc), [(w[:, 2820 + i * 1024 + dc * 128: 2820 + i * 1024 + dc * 128 + 128], i * 128) for i in range(4)]))
            for hf in range(2):
                chunks.append((("O", l, hf), [(w_out_d[l][:, hf * 512:(hf + 1) * 512], 0)]))
            wf = w_fi_d[l]
            for ps_ in range(3):
                fcs = FFN_PASSES[ps_]
                for q in range(0, len(fcs), 2):
                    srcs = []
                    for qq, fc in enumerate(fcs[q:q + 2]):
                        srcs.append((wf[:, fc * 128:(fc + 1) * 128], qq * 256))
                        srcs.append((wf[:, FF + fc * 128: FF + (fc + 1) * 128], qq * 256 + 128))
                    chunks.append((("F", l, ps_, q // 2), srcs))

        FFN_PASSES = [list(range(0, 8)), list(range(8, 15)), list(range(15, 22))]
        for l in range(depth):
            plan_layer(l)

        def ensure_loaded(upto):
            while loaded[0] <= min(upto, len(chunks) - 1):
                ci = loaded[0]
                key, srcs = chunks[ci]
                slot = wslots[ci % len(wslots)]
                for (src, off) in srcs:
                    n = src.shape[1]
                    P.dma(slot[:, :, off:off + n], wsrc(src), eng="pool")
                slot_of[key] = (ci, slot)
                loaded[0] += 1

        def wget(key, ahead=1):
            ci = None
            for i_, (k_, _) in enumerate(chunks):
                if k_ == key:
                    ci = i_
                    break
            assert ci is not None, key
            ensure_loaded(ci + ahead)
            cj, slot = slot_of[key]
            assert cj == ci
            return slot

        def dump(name, src_ap, kind):
            if name not in dbg_d:
                return
            R2 = dbgbuf
            if kind == "fm":
                n = src_ap.shape[1]
                for c in range(n):
                    for G in range(4):
                        P.copy(R2[:, 0:512], src_ap[:, c, G * 512:(G + 1) * 512], eng="dve")
                        P.dma(dbg_d[name][:, c * S + G * 512: c * S + (G + 1) * 512], R2[:, 0:512], eng="sp")
            else:
                for i in range(NT):
                    P.dma(dbg_d[name].rearrange("(n p) d -> p n d", p=128)[:, i, :], src_ap[:, i, :], eng="sp")

        dbgbuf = sb("dbgbuf", [128, 512], F32) if dbg else None

        def norm_begin(g_row):
            nb = {}
            nb["gb"] = R.take([D], F32)
            nb["sq"] = R.take([D], BF16)
            nb["xn"] = [R.take([D], BF16) for _ in range(2)]
            P.dma(nb["gb"], g_row.partition_broadcast(128), eng="sp")
            return nb

        def norm_stage(nb, si, i):
            r_ = rs[:, i:i + 1]
            if si == 0:
                P.act(nb["sq"], h[:, i, :], AF.Square, accum_out=ss[:, i:i + 1])
            elif si == 1:
                P.ts(r_, ss[:, i:i + 1], 1.0 / D, EPS, ALU.mult, ALU.add)
            elif si == 2:
                P.act(r_, r_, AF.Sqrt)
            elif si == 3:
                P.add("dve", lambda e, o=r_: e.reciprocal(o, o), [r_], [r_], name="recip")
            elif si == 4:
                P.stt(nb["xn"][i % 2], h[:, i, :], r_, nb["gb"], ALU.mult, ALU.mult)
            elif si == 5:
                xn = nb["xn"][i % 2]
                bk = rot("tr", [0, 1])
                nb[("bk", i)] = bk
                pb = banks[bk][:].bitcast(BF16)
                for k in range(8):
                    P.transpose(pb[:, k * 128:(k + 1) * 128], xn[:, k * 128:(k + 1) * 128], ident[:])
            elif si == 6:
                pb = banks[nb[("bk", i)]][:].bitcast(BF16)
                P.copy(xnT[:, :, i * 128:(i + 1) * 128], pb.rearrange("p (k t) -> p k t", k=8), eng="act")

        NST = 7

        def norm_push(nb, t):
            for si in range(NST):
                j = t - si
                if 0 <= j < NT:
                    norm_stage(nb, si, j)

        def norm_flush(nb):
            for t in range(NT, NT + NST - 1):
                norm_push(nb, t)

        def rmsnorm(g_row):
            R.reset()
            nb = norm_begin(g_row)
            for i in range(NT):
                norm_push(nb, i)
            norm_flush(nb)

        def mixer_A(l):
            R.reset()
            P.dma(wc[:], wconvT_d[l].rearrange("(j p) k -> p j k", p=128), eng="sp")
            xcbs = [R.take([S + 2], BF16) for _ in range(2)]
            cxs = [R.take([512], F32) for _ in range(2)]
            cbs = [R.take([512], F32) for _ in range(2)]
            dg = R.take([2, 3, 128], BF16)
            for j in range(2):
                for k in range(3):
                    P.ts(dg[:, j, k, :], ident[:], wc[:, j, k:k + 1], None, ALU.mult)
            for xb_ in xcbs:
                P.memset(xb_[:, 0:2], 0.0, eng="dve")
            units = [(j, G) for j in range(2) for G in range(4)]
            ust = {}

            def front(n):
                j, G = units[n]
                if G == 0:
                    ust[("slot", j)] = wget(("A", l, j))
                slot = ust[("slot", j)]
                bset = rot("convset", [[0, 1, 2], [3, 4, 5]])
                ts_ = slice(G * 512, (G + 1) * 512)
                for gi in range(3):
                    for k in range(8):
                        P.mm(banks[bset[gi]][:], slot[:, k, gi * 128:(gi + 1) * 128], xnT[:, k, ts_],
                             start=(k == 0), stop=(k == 7))
                a = n % 2
                P.copy(cxs[a], banks[bset[2]][:], eng="act")
                P.tt(xcbs[j][:, 2 + G * 512: 2 + (G + 1) * 512], banks[bset[1]][:], cxs[a], ALU.mult)
                P.copy(cbs[a], banks[bset[0]][:], eng="act")

            def back(n):
                j, G = units[n]
                a = n % 2
                ts_ = slice(G * 512, (G + 1) * 512)
                by = rot("convy", [6, 7])
                for k in range(3):
                    P.mm(banks[by][:], dg[:, j, k, :], xcbs[j][:, G * 512 + k: G * 512 + k + 512],
                         start=(k == 0), stop=(k == 2))
                P.tt(ysT[:, 0 + j, ts_], banks[by][:], cbs[a], ALU.mult)

            for n in range(len(units) + 1):
                if n < len(units):
                    front(n)
                if n >= 1:
                    back(n - 1)

        def pipeline(n_items, stages):
            ns_ = len(stages)
            for t_ in range(n_items + ns_ - 1):
                for si_ in range(ns_):
                    j_ = t_ - si_
                    if 0 <= j_ < n_items:
                        stages[si_](j_)

        def mixer_B(l):
            R.reset()
            P.dma(bsp[:], b_spT_d[l], eng="sp")
            P.dma(lng_b[:], lng_d[l].partition_broadcast(128), eng="sp")
            P.dma(lnb_b[:], lnb_d[l].partition_broadcast(128), eng="sp")
            wsf = R.take([4, 128], F32)
            wsb = R.take([4, 128], BF16)
            wsT = R.take([4, 128], BF16)
            P.dma(wsf, w_sp_d[l].rearrange("g t s -> t g s"), eng="sp")
            for g in range(4):
                asel(wsb[:, g, :], wsf[:, g, :], [[-1, 128]], ALU.is_ge, 0.0, 0, 1)
            bk = rot("tr", [0, 1])
            pb = banks[bk][:].bitcast(BF16)
            for g in range(4):
                P.transpose(pb[:, g * 128:(g + 1) * 128], wsb[:, g, :], ident[:])
            P.copy(wsT, pb[:, 0:512].rearrange("p (g t) -> p g t", g=4), eng="dve")
            NB = 6
            guv = [R.take([512], F32) for _ in range(NB)]
            vn = [R.take([256], F32) for _ in range(NB)]
            vnb = [R.take([256], BF16) for _ in range(NB)]
            ybt = [R.take([256], BF16) for _ in range(NB)]
            slot = wget(("B", l))
            stt_ = {}

            def sm(i, lo, hi):
                b = i % NB
                return small[:, b * 16 + lo: b * 16 + hi]

            def s0(i):
                buv = rot("uv", [0, 1])
                stt_[("uv", i)] = buv
                for k in range(8):
                    P.mm(banks[buv][:], xnT[:, k, i * 128:(i + 1) * 128], slot[:, k, :], start=(k == 0), stop=(k == 7))

            def s1(i):
                P.act(guv[i % NB], banks[stt_[("uv", i)]][:], AF.Gelu_apprx_tanh)

            def s2(i):
                v = guv[i % NB][:, 256:512]
                st6, mv, rstd, dd_, m2_ = sm(i, 0, 6), sm(i, 8, 10), sm(i, 10, 11), sm(i, 11, 12), sm(i, 12, 13)
                P.add("dve", lambda e, o=st6, i_=v: e.bn_stats(o, i_), [v], [st6], name="bnstats")
                P.tt(dd_, st6[:, 1:2], st6[:, 4:5], ALU.subtract)
                P.tt(mv[:, 0:1], st6[:, 1:2], st6[:, 4:5], ALU.add)
                P.tt(m2_, st6[:, 2:3], st6[:, 5:6], ALU.add)
                P.ts(mv[:, 0:1], mv[:, 0:1], 0.5, None, ALU.mult)
                P.ts(m2_, m2_, 1.0 / 256, EPS, ALU.mult, ALU.add)
                P.tt(dd_, dd_, dd_, ALU.mult)
                P.stt(rstd, dd_, 0.25, m2_, ALU.mult, ALU.add)

            def s3(i):
                rstd = sm(i, 10, 11)
                P.act(rstd, rstd, AF.Sqrt)

            def s4(i):
                b = i % NB
                v = guv[b][:, 256:512]
                mv, rstd = sm(i, 8, 10), sm(i, 10, 11)
                P.add("dve", lambda e, o=rstd: e.reciprocal(o, o), [rstd], [rstd], name="recip")
                P.ts(vn[b], v, mv[:, 0:1], rstd, ALU.subtract, ALU.mult)
                P.tt(vn[b], vn[b], lng_b[:], ALU.mult)
                P.tt(vnb[b], vn[b], lnb_b[:], ALU.add)

            def s5(i):
                b = i % NB
                bmx = rot("mx", [2, 3])
                stt_[("mx", i)] = bmx
                for g in range(4):
                    P.mm(banks[bmx][:, g * 64:(g + 1) * 64], wsT[:, g, :], vnb[b][:, g * 64:(g + 1) * 64],
                         start=True, stop=True)

            def s6(i):
                b = i % NB
                bmx = stt_[("mx", i)]
                for g in range(4):
                    P.stt(ybt[b][:, g * 64:(g + 1) * 64], banks[bmx][:, g * 64:(g + 1) * 64], bsp[:, g:g + 1],
                          guv[b][:, g * 64:(g + 1) * 64], ALU.add, ALU.mult)

            def s7(i):
                b = i % NB
                btr = rot("trb", [4, 5])
                stt_[("tr", i)] = btr
                pb2 = banks[btr][:].bitcast(BF16)
                for c in range(2):
                    P.transpose(pb2[:, c * 128:(c + 1) * 128], ybt[b][:, c * 128:(c + 1) * 128], ident[:])

            def s8(i):
                pb2 = banks[stt_[("tr", i)]][:].bitcast(BF16)
                P.copy(ysT[:, 2:4, i * 128:(i + 1) * 128], pb2[:, 0:256].rearrange("p (c t) -> p c t", c=2), eng="act")

            pipeline(NT, [s0, s1, s2, s3, s4, s5, s6, s7, s8])

        def mixer_C(l):
            R.reset()
            qpad = R.take([4, S], BF16)
            P.memset(qpad, 0.0, eng="pool")
            kT = R.take([2, S], BF16)
            vtm = R.take([NT, 256], BF16)
            e_t = [R.take([512], F32) for _ in range(3)]
            sp_t = [R.take([512], BF16) for _ in range(3)]
            w_t = [R.take([512], BF16) for _ in range(3)]
            Sb = [R.take([512], BF16) for _ in range(2)]
            slot = wget(("Cqk", l))
            for qk in range(2):
                for c in range(2):
                    for G in range(4):
                        bk = rot("proj", [0, 1, 2, 3, 4, 5])
                        ts_ = slice(G * 512, (G + 1) * 512)
                        for k in range(8):
                            P.mm(banks[bk][:], slot[:, k, qk * 256 + c * 128: qk * 256 + (c + 1) * 128], xnT[:, k, ts_],
                                 start=(k == 0), stop=(k == 7))
                        if qk == 0:
                            P.act(qpad[0:64, 2 * c, ts_], banks[bk][0:64, :], AF.Copy, scale=0.125)
                            P.act(qpad[64:128, 2 * c + 1, ts_], banks[bk][64:128, :], AF.Copy, scale=0.125)
                        else:
                            P.copy(kT[:, c, ts_], banks[bk][:], eng="dve")
            slot = wget(("Cv", l))
            for i in range(NT):
                bk = rot("proj", [0, 1, 2, 3, 4, 5])
                for k in range(8):
                    P.mm(banks[bk][:, 0:256], xnT[:, k, i * 128:(i + 1) * 128], slot[:, k, 0:256],
                         start=(k == 0), stop=(k == 7))
                P.copy(vtm[:, i, :], banks[bk][:, 0:256], eng=("act" if i % 2 else "dve"))
            for hp in range(2):
                for G in range(4):
                    nkb = 4 * G + 4
                    jobs = []
                    for kb in range(nkb - 1, -1, -1):
                        for hh in range(2):
                            jobs.append((hh, kb))
                    OB = [6, 7]
                    for hh in range(2):
                        P.memset(Sb[hh], 0.0, eng="dve")
                    state = {}

                    def stage(si, job, jn):
                        hh, kb = job
                        po = hh * 64
                        r = kb - 4 * G
                        c0 = 128 * r if r >= 0 else 0
                        cs = slice(c0, 512)
                        qs = slice(G * 512 + c0, (G + 1) * 512)
                        ksl = slice(kb * 128, (kb + 1) * 128)
                        first = (kb == nkb - 1)
                        last = (kb == 0)
                        if si == 0:
                            zb = rot("z", [0, 1, 2, 3, 4, 5])
                            state[(job, "zb")] = zb
                            P.mm(banks[zb][:, cs], kT[:, hp, ksl], qpad[:, hp * 2 + hh, qs],
                                 start=True, stop=False)
                            if r >= 0:
                                P.mm(banks[zb][:, c0:c0 + 128], ident[:], sbmask[:], start=False, stop=False)
                        elif si == 1:
                            zb = state[(job, "zb")]
                            a = jn % 3
                            P.act(e_t[a][:, cs], banks[zb][:, cs], AF.Exp)
                            P.act(sp_t[a][:, cs], e_t[a][:, cs], AF.Ln, bias=1.0)
                        elif si == 2:
                            a = jn % 3
                            zb = state[(job, "zb")]
                            P.mm(banks[zb][:, cs], uneg[:], sp_t[a][:, cs], start=False, stop=first,
                                 skip_group_check=True)
                            if not first:
                                P.mm(banks[zb][:, cs], negones[:], Sb[hh][:, cs], start=False, stop=True,
                                     skip_group_check=True)
                            if not last:
                                P.tt(Sb[hh][:, cs], Sb[hh][:, cs], sp_t[a][:, cs], ALU.add)
                        elif si == 3:
                            a = jn % 3
                            zb = state[(job, "zb")]
                            P.act(w_t[a][:, cs], banks[zb][:, cs], AF.Exp)
                        elif si == 4:
                            a = jn % 3
                            P.mm(banks[OB[hh]][:, cs], vtm[:, kb, hp * 128:(hp + 1) * 128], w_t[a][:, cs],
                                 start=first, stop=last)
                            if last:
                                P.copy(ysT[po:po + 64, 4 + hp, G * 512:(G + 1) * 512], banks[OB[hh]][po:po + 64, :],
                                       eng="dve")

                    nst = 5
                    for t in range(len(jobs) + nst - 1):
                        for si in range(nst):
                            jn = t - si
                            if 0 <= jn < len(jobs):
                                stage(si, jobs[jn], jn)

        def mixer_D(l):
            R.reset()
            P.dma(gq_b[:], gq_d[l].partition_broadcast(128), eng="sp")
            P.dma(gk_b[:], gk_d[l].partition_broadcast(128), eng="sp")
            P.dma(bf_b[:], bf_d[l].partition_broadcast(128), eng="sp")
            P.ts(gq_b[:], gq_b[:], 0.125, None, ALU.mult)
            qpad = R.take([4, S], BF16)
            kpad = R.take([4, S], BF16)
            P.memset(qpad, 0.0, eng="pool")
            P.memset(kpad, 0.0, eng="pool")
            qpv = qpad.rearrange("p (c two) s -> p c two s", two=2)
            kpv = kpad.rearrange("p (c two) s -> p c two s", two=2)
            vtm = R.take([NT, 2, 192], BF16)
            P.memset(vtm[:, :, :, 64:128], 1.0, eng="pool")
            LF = R.take([NT, 4], F32)
            cpos = R.take([NT, 4], F32)
            carry = R.take([NT, 4], F32)
            r1 = carry
            r2 = LF
            cbf = R.take([NT, 4], BF16)
            off_shared = R.off
            ND = 4
            sq = [R.take([512], BF16) for _ in range(2)]
            qkc = [R.take([512], F32) for _ in range(ND)]
            qkn = [R.take([512], BF16) for _ in range(2)]
            R.off = off_shared
            TEq = R.take([NT, 4, 8], BF16)
            TEk = R.take([NT, 4, 8], BF16)
            p_t = [R.take([512], BF16) for _ in range(3)]
            rec_one = R.take([512], F32)
            rec = [rec_one, rec_one]
            slot_qk = wget(("Dqk", l))
            slot_vf = wget(("Dvf", l), ahead=0)
            stt_ = {}

            def smf(i, lo, hi):
                b = i % ND
                return small[:, 128 + b * 16 + lo: 128 + b * 16 + hi]

            def s0(i):
                tsl = slice(i * 128, (i + 1) * 128)
                bq = rot("fq", [0, 1])
                bv = rot("fv", [2, 3])
                stt_[("bq", i)] = bq
                stt_[("bv", i)] = bv
                for k in range(8):
                    P.mm(banks[bq][:], xnT[:, k, tsl], slot_qk[:, k, :], start=(k == 0), stop=(k == 7))
                for k in range(8):
                    P.mm(banks[bv][:, 0:260], xnT[:, k, tsl], slot_vf[:, k, 0:260], start=(k == 0), stop=(k == 7))

            def s1(i):
                bq, bv = stt_[("bq", i)], stt_[("bv", i)]
                P.act(sq[i % 2], banks[bq][:], AF.Square)
                P.copy(qkc[i % ND], banks[bq][:], eng="act")
                vsrc = banks[bv][:, 0:256].rearrange("p (c two d) -> p c two d", c=2, two=2)
                P.copy(vtm[:, i, :, 0:64], vsrc[:, :, 0, :], eng="act")
                P.copy(vtm[:, i, :, 128:192], vsrc[:, :, 1, :], eng="act")
                P.tt(smf(i, 0, 4), banks[bv][:, 256:260], bf_b[:], ALU.add)

            def s2(i):
                ssq = smf(i, 8, 16)
                P.add("dve", lambda e, o=ssq, i_=sq[i % 2]: e.tensor_reduce(o, i_.rearrange("p (j d) -> p j d", j=8), AX.X, ALU.add),
                      [sq[i % 2]], [ssq], name="tred")
                P.ts(ssq, ssq, 1.0 / 64, EPS, ALU.mult, ALU.add)

            def s3(i):
                fb = smf(i, 0, 4)
                ssq = smf(i, 8, 16)
                P.act(fb, fb, AF.Exp, scale=-1.0)
                P.act(LF[:, i, :], fb, AF.Ln, bias=1.0)
                P.act(ssq, ssq, AF.Sqrt)

            def s4(i):
                ssq = smf(i, 8, 16)
                P.add("dve", lambda e, o=ssq: e.reciprocal(o, o), [ssq], [ssq], name="recip")
                for j in range(8):
                    gbt = gq_b if j < 4 else gk_b
                    P.stt(qkn[i % 2][:, j * 64:(j + 1) * 64], qkc[i % ND][:, j * 64:(j + 1) * 64], ssq[:, j:j + 1], gbt[:],
                          ALU.mult, ALU.mult)

            def s5(i):
                btr = rot("ftr", [4, 5])
                stt_[("tr", i)] = btr
                pb = banks[btr][:].bitcast(BF16)
                for c in range(4):
                    P.transpose(pb[:, c * 128:(c + 1) * 128], qkn[i % 2][:, c * 128:(c + 1) * 128], ident[:])

            def s6(i):
                tsl = slice(i * 128, (i + 1) * 128)
                pb = banks[stt_[("tr", i)]][:].bitcast(BF16)
                P.copy(qpv[0:64, :, 0, tsl], pb[0:64, 0:256].rearrange("p (c t) -> p c t", c=2), eng="act")
                P.copy(qpv[64:128, :, 1, tsl], pb[64:128, 0:256].rearrange("p (c t) -> p c t", c=2), eng="act")
                P.copy(kpv[0:64, :, 0, tsl], pb[0:64, 256:512].rearrange("p (c t) -> p c t", c=2), eng="dve")
                P.copy(kpv[64:128, :, 1, tsl], pb[64:128, 256:512].rearrange("p (c t) -> p c t", c=2), eng="dve")

            pipeline(NT, [s0, s1, s2, s3, s4, s5, s6])
            wget(("G", l, 0), ahead=1)
            LF2 = LF.rearrange("p i h -> p (i h)")
            P.mm(banks[6][:, 0:64], tri_f[:], LF2, start=True, stop=True)
            P.mm(banks[7][:, 0:64], ones_f[:], LF2, start=True, stop=True)
            P.memset(carry[:, 0, :], 0.0, eng="dve")
            for i in range(1, NT):
                P.tt(carry[:, i, :], carry[:, i - 1, :], banks[7][:, (i - 1) * 4: i * 4], ALU.add)
            cp2 = cpos.rearrange("p i h -> p (i h)")
            P.tt(cp2, banks[6][:, 0:64], carry.rearrange("p i h -> p (i h)"), ALU.add)
            P.memset(TEq, 1.0, eng="dve")
            P.memset(TEk, 1.0, eng="dve")
            cur = cpos
            for t_, nxt in enumerate((r1, r2, None)):
                P.copy(cbf, cur, eng="dve")
                P.copy(TEk[:, :, :, 3 + t_], cbf, eng="dve")
                P.ts(TEq[:, :, :, t_], cbf, -1.0, None, ALU.mult)
                if nxt is not None:
                    P.tt(nxt, cur, cbf, ALU.subtract)
                    cur = nxt
            if f"cpos{l}" in dbg_d:
                P.dma(dbg_d[f"cpos{l}"], cp2, eng="sp")
            for (TE, XP) in ((TEq, qpad), (TEk, kpad)):
                for hd in range(4):
                    opo = 64 - (hd % 2) * 64
                    for G in range(4):
                        btr = rot("lb", [0, 1, 2, 3])
                        pb = banks[btr][:].bitcast(BF16)
                        for ii in range(4):
                            P.transpose(pb[0:8, ii * 128:(ii + 1) * 128], TE[:, G * 4 + ii, hd, :], ident[:])
                        P.copy(XP[opo:opo + 6, hd, G * 512:(G + 1) * 512], pb[0:6, 0:512], eng="dve")
            for hp in range(2):
                for G in range(4):
                    nkb = 4 * G + 4
                    jobs = []
                    for kb in range(nkb - 1, -1, -1):
                        for hh in range(2):
                            jobs.append((hh, kb))
                    OBn = [4, 5] if (hp * 4 + G) % 2 == 0 else [6, 7]
                    state = {}

                    def stage(si, job, jn):
                        hh, kb = job
                        po = hh * 64
                        r = kb - 4 * G
                        c0 = 128 * r if r >= 0 else 0
                        cs = slice(c0, 512)
                        qs = slice(G * 512 + c0, (G + 1) * 512)
                        ksl = slice(kb * 128, (kb + 1) * 128)
                        first = (kb == nkb - 1)
                        last = (kb == 0)
                        a = jn % 3
                        if si == 0:
                            lb = rot("lb", [0, 1, 2, 3])
                            state[(job, "lb")] = lb
                            P.mm(banks[lb][:, cs], kpad[:, hp * 2 + hh, ksl], qpad[:, hp * 2 + hh, qs],
                                 start=True, stop=(r < 0))
                            if r >= 0:
                                P.mm(banks[lb][:, c0:c0 + 128], ident[:], foxmask[:], start=False, stop=True)
                        elif si == 1:
                            lb = state[(job, "lb")]
                            P.act(p_t[a][:, cs], banks[lb][:, cs], AF.Exp)
                        elif si == 3:
                            opo = 64 - po
                            P.mm(banks[OBn[hh]][:, cs], vtm[:, kb, hp, hh * 64: hh * 64 + 128], p_t[a][:, cs],
                                 start=first, stop=last)
                            if last:
                                rc = rec[hh]
                                P.add("dve", lambda e, o=rc[opo:opo + 64, :], i_=banks[OBn[hh]][opo:opo + 64, :]: e.reciprocal(o, i_),
                                      [banks[OBn[hh]][opo:opo + 64, :]], [rc[opo:opo + 64, :]], name="recip")
                                P.tt(ysT[po:po + 64, 6 + hp, G * 512:(G + 1) * 512], banks[OBn[hh]][po:po + 64, :],
                                     rc[opo:opo + 64, :], ALU.mult)

                    nst = 4
                    for t in range(len(jobs) + nst - 1):
                        for si in range(nst):
                            jn = t - si
                            if 0 <= jn < len(jobs):
                                stage(si, jobs[jn], jn)

        def merge_out(l):
            R.reset()
            mT = R.take([8, S], BF16)
            off_after_mT = R.off
            wb = [R.take([4, 2, 128], BF16) for _ in range(2)]
            sg = [R.take([512], F32) for _ in range(2)]
            macc = [R.take([512], F32) for _ in range(2)]
            tmp = [R.take([512], F32) for _ in range(2)]
            n = 0
            for dc in range(8):
                w_b = wb[dc % 2]
                for i in range(4):
                    P.dma(w_b[:, i, :, :], w_br_d[l][i, :, dc * 128:(dc + 1) * 128].rearrange("(j p) c -> p j c", p=128),
                          eng="pool")
                slot = wget(("G", l, dc))
                for G in range(4):
                    ts_ = slice(G * 512, (G + 1) * 512)
                    mc = macc[(dc * 4 + G) % 2]
                    for i in range(4):
                        gbk = rot("gate", [0, 1, 2, 3])
                        bbk = rot("br", [4, 5, 6, 7])
                        for k in range(8):
                            P.mm(banks[gbk][:], slot[:, k, i * 128:(i + 1) * 128], xnT[:, k, ts_],
                                 start=(k == 0), stop=(k == 7))
                        for j in range(2):
                            P.mm(banks[bbk][:], w_b[:, i, j, :], ysT[:, i * 2 + j, ts_], start=(j == 0), stop=(j == 1))
                        s_ = sg[n % 2]
                        t_ = tmp[n % 2]
                        n += 1
                        P.act(s_, banks[gbk][:], AF.Sigmoid)
                        if i == 0:
                            P.tt(mc, s_, banks[bbk][:], ALU.mult)
                        else:
                            P.tt(t_, s_, banks[bbk][:], ALU.mult)
                            if i < 3:
                                P.tt(mc, mc, t_, ALU.add)
                            else:
                                P.tt(mT[:, dc, ts_], mc, t_, ALU.add)
            dump(f"merged{l}", mT, "fm")
            R.off = off_after_mT
            nb = norm_begin(gffn_d[l])
            for hf in range(2):
                slot = wget(("O", l, hf))
                for i in range(NT):
                    bk = rot("wo", [2, 3, 4, 5, 6, 7])
                    for k in range(8):
                        P.mm(banks[bk][:], mT[:, k, i * 128:(i + 1) * 128], slot[:, k, :], start=(k == 0), stop=(k == 7))
                    hs = h[:, i, hf * 512:(hf + 1) * 512]
                    P.tt(hs, hs, banks[bk][:], ALU.add)
                    if hf == 1:
                        norm_push(nb, i)
            norm_flush(nb)
            dump(f"hmix{l}", h[:], "tm")

        def ffn(l, is_last, next_g=None):
            R.reset()
            hidT = ysT
            WoF2 = [R.take([8, D], BF16) for _ in range(2)]
            sg = [R.take([512], F32) for _ in range(2)]
            n = 0
            for ps_ in range(3):
                fcs = FFN_PASSES[ps_]
                nf = len(fcs)
                WoF = WoF2[ps_ % 2]
                for hf in range(2):
                    P.dma(WoF[:, 0:nf, hf * 512:(hf + 1) * 512],
                          w_fo_d[l][fcs[0] * 128:(fcs[-1] + 1) * 128, hf * 512:(hf + 1) * 512].rearrange("(f p) c -> p f c", p=128),
                          eng="pool")
                for q in range(0, nf, 2):
                    slot = wget(("F", l, ps_, q // 2))
                    for qq, fc in enumerate(fcs[q:q + 2]):
                        fl = q + qq
                        for G in range(4):
                            ts_ = slice(G * 512, (G + 1) * 512)
                            gbk = rot("gate", [0, 1, 2, 3])
                            ubk = rot("br", [4, 5, 6, 7])
                            for k in range(8):
                                P.mm(banks[gbk][:], slot[:, k, qq * 256: qq * 256 + 128], xnT[:, k, ts_],
                                     start=(k == 0), stop=(k == 7))
                            for k in range(8):
                                P.mm(banks[ubk][:], slot[:, k, qq * 256 + 128: qq * 256 + 256], xnT[:, k, ts_],
                                     start=(k == 0), stop=(k == 7))
                            s_ = sg[n % 2]
                            n += 1
                            P.act(s_, banks[gbk][:], AF.Silu)
                            P.tt(hidT[:, fl, ts_], s_, banks[ubk][:], ALU.mult)
                nb = None
                if ps_ == 2 and next_g is not None:
                    nb = norm_begin(next_g)
                for i in range(NT):
                    for hf in range(2):
                        bk = rot("wo", [2, 3, 4, 5, 6, 7])
                        for fl in range(nf):
                            P.mm(banks[bk][:], hidT[:, fl, i * 128:(i + 1) * 128], WoF[:, fl, hf * 512:(hf + 1) * 512],
                                 start=(fl == 0), stop=(fl == nf - 1))
                        hs = h[:, i, hf * 512:(hf + 1) * 512]
                        P.tt(hs, hs, banks[bk][:], ALU.add)
                    if is_last and ps_ == 2:
                        P.dma(out_d.rearrange("(n p) d -> p n d", p=128)[:, i, :], h[:, i, :], eng="sp")
                    if nb is not None:
                        norm_push(nb, i)
                if nb is not None:
                    norm_flush(nb)

        stop_after = [s_ for s_ in dbg if s_.startswith("stop:")]
        stop_after = stop_after[0][5:] if stop_after else None

        def finish_early():
            for i in range(NT):
                P.dma(out_d.rearrange("(n p) d -> p n d", p=128)[:, i, :], h[:, i, :], eng="sp")

        done = False
        rmsnorm(gmix_d[0])
        for l in range(depth):
            dump(f"xnT{l}", xnT[:], "fm")
            mixer_A(l)
            dump(f"ya{l}", ysT[:, 0:2, :], "fm")
            mixer_B(l)
            dump(f"yb{l}", ysT[:, 2:4, :], "fm")
            mixer_C(l)
            dump(f"yc{l}", ysT[:, 4:6, :], "fm")
            mixer_D(l)
            dump(f"yd{l}", ysT[:, 6:8, :], "fm")
            merge_out(l)
            if stop_after == f"M{l}":
                finish_early(); done = True; break
            ffn(l, is_last=(l == depth - 1), next_g=(gmix_d[l + 1] if l + 1 < depth else None))
            dump(f"h{l}", h[:], "tm")
        if not done and depth < DEPTH:
            finish_early()
        P.emit(st)
        nc._prog_stats = (len(P.ops), P.n_sems)
    return nc


def make_in_maps(inputs):
    f = lambda a: np.ascontiguousarray(np.asarray(a, dtype=np.float32))
    shared = {
        "gmix": f(inputs["norm_mix_g"]),
        "w_in": f(inputs["w_in"]),
        "wconvT": f(np.transpose(np.asarray(inputs["w_conv"]), (0, 2, 1))),
        "w_sp": f(inputs["w_spatial"]),
        "b_spT": f(np.transpose(np.asarray(inputs["b_spatial"]), (0, 2, 1))),
        "lng": f(inputs["gmlp_ln_g"]),
        "lnb": f(inputs["gmlp_ln_b"]),
        "gq": f(inputs["fox_q_norm_g"]),
        "gk": f(inputs["fox_k_norm_g"]),
        "bfg": f(inputs["fox_forget_b"]),
        "w_br": f(inputs["w_branch"]),
        "w_out": f(inputs["w_out"]),
        "gffn": f(inputs["norm_ffn_g"]),
        "w_fi": f(inputs["w_ffn_in"]),
        "w_fo": f(inputs["w_ffn_out"]),
    }
    x = np.asarray(inputs["x"], dtype=np.float32)
    return [dict(shared, x=np.ascontiguousarray(x[b])) for b in range(8)]


_NC_CACHE = {}


def kernel(**inputs):
    if "nc" not in _NC_CACHE:
        _NC_CACHE["nc"] = build_nc()
    nc = _NC_CACHE["nc"]
    in_maps = make_in_maps(inputs)
    res = run_bass_kernel_spmd(nc, in_maps, core_ids=list(range(8)))
    out = np.stack([np.asarray(r["out"], dtype=np.float32) for r in res.results], axis=0)
    return out
```

```python
import numpy as np
from contextlib import ExitStack
from concourse.bass_utils import run_bass_kernel_spmd

import concourse.bass as bass
import concourse.mybir as mybir

F32 = mybir.dt.float32
BF16 = mybir.dt.bfloat16
AF = mybir.ActivationFunctionType
ALU = mybir.AluOpType
AX = mybir.AxisListType

ENGINES = ["pe", "act", "dve", "pool", "sp"]
SEM_MAX = 30000
DMA_SEMS_PER_Q = 8


def _esize(dt):
    return mybir.dt.size(dt)


def _region(ap):
    sp = str(ap.space)
    if "SB" not in sp and "PSUM" not in sp:
        return None
    pat = ap.ap
    es = _esize(ap.dtype)
    pstride = pat[0][0]
    npart = pat[0][1]
    off = ap.offset
    if pstride > 0:
        p0 = off // pstride
        f0 = off % pstride
    else:
        p0 = 0
        f0 = off
    ext = 1
    for st, cnt in pat[1:]:
        ext += abs(st) * (cnt - 1)
    if "PSUM" in sp:
        return (ap.tensor.name, 0, 128, 0, 1 << 30)
    return (ap.tensor.name, p0, p0 + npart, f0 * es, (f0 + ext) * es)


class Op:
    __slots__ = ("idx", "eng", "fn", "reads", "writes", "is_dma", "deps",
                 "signaled", "sem", "val", "name", "small")


class Prog:
    def __init__(self, nc, same_engine_sync=False):
        self.nc = nc
        self.ops = []
        self.recs = {}
        self.same_engine_sync = same_engine_sync

    def add(self, eng, fn, reads=(), writes=(), is_dma=False, name=""):
        op = Op()
        op.idx = len(self.ops)
        op.eng = eng
        op.fn = fn
        op.is_dma = is_dma
        op.name = name
        op.signaled = False
        op.sem = None
        op.val = 0
        op.small = False
        for a in writes:
            n_el = 1
            for st_, cnt_ in a.ap[1:]:
                n_el *= cnt_
            if n_el < 128:
                op.small = True
        rr = [r for r in (_region(a) for a in reads) if r is not None]
        ww = [r for r in (_region(a) for a in writes) if r is not None]
        ww = ww + [r for r in rr if r[4] == (1 << 30)]
        rr = [r for r in rr if r[4] != (1 << 30)]
        deps = set()
        for (tn, plo, phi, blo, bhi) in rr:
            for rec in self.recs.get(tn, ()):
                if rec[5] and rec[0] < phi and plo < rec[1] and rec[2] < bhi and blo < rec[3]:
                    deps.add(rec[4])
        for (tn, plo, phi, blo, bhi) in ww:
            for rec in self.recs.get(tn, ()):
                if rec[0] < phi and plo < rec[1] and rec[2] < bhi and blo < rec[3]:
                    deps.add(rec[4])
        for (tn, plo, phi, blo, bhi) in ww:
            lst = self.recs.setdefault(tn, [])
            lst[:] = [rec for rec in lst if not (plo <= rec[0] and rec[1] <= phi and blo <= rec[2] and rec[3] <= bhi)]
            lst.append([plo, phi, blo, bhi, op.idx, True])
        for (tn, plo, phi, blo, bhi) in rr:
            lst = self.recs.setdefault(tn, [])
            if not is_dma:
                lst[:] = [rec for rec in lst if not ((not rec[5]) and (rec[4] == op.idx or (
                                                     self.ops[rec[4]].eng == eng and not self.ops[rec[4]].is_dma))
                                                     and plo <= rec[0] and rec[1] <= phi
                                                     and blo <= rec[2] and rec[3] <= bhi)]
            lst.append([plo, phi, blo, bhi, op.idx, False])
        deps.discard(op.idx)
        need = []
        for d in deps:
            dop = self.ops[d]
            if dop.eng == eng and not dop.is_dma and not is_dma and not self.same_engine_sync \
                    and (eng == "pe" or (eng != "pool" and not dop.small)):
                continue
            need.append(d)
            dop.signaled = True
        op.deps = need
        if is_dma:
            op.signaled = True
        self.ops.append(op)
        return op

    def mm(self, out, lhsT, rhs, start=True, stop=True, **kw):
        reads = [lhsT, rhs] + ([] if start else [out])
        return self.add("pe", lambda e: e.matmul(out, lhsT, rhs, start=start, stop=stop, **kw),
                        reads, [out], name="mm")

    def transpose(self, out, in_, ident):
        return self.add("pe", lambda e: e.transpose(out, in_, ident), [in_, ident], [out], name="tr")

    def act(self, out, in_, func, bias=None, scale=None, accum_out=None):
        kw = {}
        reads = [in_]
        writes = [out]
        if bias is not None:
            kw["bias"] = bias
            if not isinstance(bias, (int, float)):
                reads.append(bias)
        if scale is not None:
            kw["scale"] = scale
            if not isinstance(scale, (int, float)):
                reads.append(scale)
        if accum_out is not None:
            kw["accum_out"] = accum_out
            writes.append(accum_out)
        return self.add("act", lambda e: e.activation(out, in_, func, **kw), reads, writes, name="act")

    def tt(self, out, in0, in1, op, eng="dve"):
        return self.add(eng, lambda e: e.tensor_tensor(out, in0, in1, op), [in0, in1], [out], name="tt")

    def ts(self, out, in0, s1, s2, op0, op1=None, eng="dve"):
        reads = [in0]
        for s in (s1, s2):
            if s is not None and not isinstance(s, (int, float)):
                reads.append(s)
        if op1 is None:
            return self.add(eng, lambda e: e.tensor_scalar(out, in0, s1, None, op0), reads, [out], name="ts")
        return self.add(eng, lambda e: e.tensor_scalar(out, in0, s1, s2, op0, op1), reads, [out], name="ts")

    def stt(self, out, in0, scalar, in1, op0, op1):
        reads = [in0, in1]
        if not isinstance(scalar, (int, float)):
            reads.append(scalar)
        return self.add("dve", lambda e: e.scalar_tensor_tensor(out, in0, scalar, in1, op0, op1),
                        reads, [out], name="stt")

    def copy(self, out, in_, eng="dve"):
        if eng == "act":
            return self.add("act", lambda e: e.copy(out, in_), [in_], [out], name="copy")
        return self.add(eng, lambda e: e.tensor_copy(out, in_), [in_], [out], name="copy")

    def memset(self, ap, val, eng="dve"):
        return self.add(eng, lambda e: e.memset(ap, val), [], [ap], name="memset")

    def dma(self, out, in_, eng="sp", **kw):
        return self.add(eng, lambda e: e.dma_start(out=out, in_=in_, **kw), [in_], [out],
                        is_dma=True, name="dma")

    def emit(self, stack):
        nc = self.nc
        counters = {e: 0 for e in ENGINES}
        eng_sems = {e: [] for e in ENGINES}
        dma_sems = {e: [] for e in ENGINES}
        dma_cnt = {e: 0 for e in ENGINES}
        dma_semval = {}
        dma_hist = {e: [] for e in ENGINES}
        for op in self.ops:
            if op.is_dma:
                q = op.eng
                j = dma_cnt[q] % DMA_SEMS_PER_Q
                if len(dma_sems[q]) <= j:
                    dma_sems[q].append(stack.enter_context(nc.semaphore(f"d_{q}_{j}")))
                sem = dma_sems[q][j]
                v = dma_semval.get((q, j), 0) + 16
                dma_semval[(q, j)] = v
                op.sem = sem
                op.val = v
                if dma_cnt[q] >= DMA_SEMS_PER_Q:
                    prev = dma_hist[q][dma_cnt[q] - DMA_SEMS_PER_Q]
                    if prev.idx not in op.deps:
                        op.deps.append(prev.idx)
                dma_hist[q].append(op)
                dma_cnt[q] += 1
            elif op.signaled:
                e = op.eng
                c = counters[e]
                k = c // SEM_MAX
                if len(eng_sems[e]) <= k:
                    eng_sems[e].append(stack.enter_context(nc.semaphore(f"c_{e}_{k}")))
                op.sem = eng_sems[e][k]
                op.val = c % SEM_MAX + 1
                counters[e] = c + 1
        self.n_sems = sum(len(v) for v in eng_sems.values()) + sum(len(v) for v in dma_sems.values())
        block = stack.enter_context(nc.Block())
        ops = self.ops

        def run(engname):
            def body(e):
                waited = {}
                last = None
                for op in ops:
                    if op.eng != engname:
                        continue
                    for d in sorted(op.deps):
                        dop = ops[d]
                        key = id(dop.sem)
                        if waited.get(key, 0) >= dop.val:
                            continue
                        e.wait_ge(dop.sem, dop.val)
                        waited[key] = dop.val
                    inst = op.fn(e)
                    if op.sem is not None:
                        inst.then_inc(op.sem, 16 if op.is_dma else 1)
                    last = op
                for j, sem in enumerate(dma_sems[engname]):
                    v = dma_semval.get((engname, j), 0)
                    if v:
                        e.wait_ge(sem, v)
            return body

        block.tensor(run("pe"))
        block.scalar(run("act"))
        block.vector(run("dve"))
        block.gpsimd(run("pool"))
        block.sync(run("sp"))


S = 2048
D = 1024
NT = 16
NIN = 6916
FF = 2816
DEPTH = 2
EPS = 1e-6
NEG = -30000.0
R_BYTES = 57 * 1024


class Carver:
    def __init__(self, t):
        self.t = t
        self.off = 0

    def reset(self):
        self.off = 0

    def take(self, shape, dt):
        n = 1
        for s_ in shape:
            n *= s_
        nb = n * mybir.dt.size(dt)
        nb_al = (nb + 31) // 32 * 32
        assert self.off + nb_al <= R_BYTES, (self.off, nb_al)
        a = self.t[:, self.off // 4:(self.off + nb_al) // 4]
        self.off += nb_al
        if dt != F32:
            a = a.bitcast(dt)
        a = a[:, 0:n]
        if len(shape) == 2:
            a = a.rearrange("p (a b) -> p a b", a=shape[0])
        elif len(shape) == 3:
            a = a.rearrange("p (a b c) -> p a b c", a=shape[0], b=shape[1])
        return a


def build_nc(depth=DEPTH, dbg=()):
    nc = bass.Bass("TRN2", target_bir_lowering=False)

    def din(name, shape):
        return nc.dram_tensor(name, shape, F32, kind="ExternalInput").ap()

    x_d = din("x", [S, D])
    gmix_d = din("gmix", [DEPTH, D])
    w_in_d = din("w_in", [DEPTH, D, NIN])
    wconvT_d = din("wconvT", [DEPTH, 256, 3])
    w_sp_d = din("w_sp", [DEPTH, 4, 128, 128])
    b_spT_d = din("b_spT", [DEPTH, 128, 4])
    lng_d = din("lng", [DEPTH, 256])
    lnb_d = din("lnb", [DEPTH, 256])
    gq_d = din("gq", [DEPTH, 64])
    gk_d = din("gk", [DEPTH, 64])
    bf_d = din("bfg", [DEPTH, 4])
    w_br_d = din("w_br", [DEPTH, 4, 256, D])
    w_out_d = din("w_out", [DEPTH, D, D])
    gffn_d = din("gffn", [DEPTH, D])
    w_fi_d = din("w_fi", [DEPTH, D, 2 * FF])
    w_fo_d = din("w_fo", [DEPTH, FF, D])
    out_d = nc.dram_tensor("out", [S, D], F32, kind="ExternalOutput").ap()
    dbg_d = {}
    for name in dbg:
        if name.startswith("cpos"):
            dbg_d[name] = nc.dram_tensor("dbg_" + name, [128, 64], F32, kind="ExternalOutput").ap()
        elif name.startswith("y") or name.startswith("merged") or name.startswith("xnT") or name.startswith("fq") or name.startswith("fk"):
            shp = [128, (8 if (name.startswith("merged") or name.startswith("xnT")) else 2) * S]
            dbg_d[name] = nc.dram_tensor("dbg_" + name, shp, F32, kind="ExternalOutput").ap()
        else:
            dbg_d[name] = nc.dram_tensor("dbg_" + name, [S, D], F32, kind="ExternalOutput").ap()

    with ExitStack() as st:
        def sb(name, shape, dt):
            return st.enter_context(nc.sbuf_tensor(name, shape, dt))

        banks = [st.enter_context(nc.psum_tensor(f"bank{i}", [128, 512], F32)) for i in range(8)]
        h = sb("h", [128, NT, D], F32)
        xnT = sb("xnT", [128, 8, S], BF16)
        ysT = sb("ysT", [128, 8, S], BF16)
        wslots = [sb(f"wslot{i}", [128, 8, 512], BF16) for i in range(2)]
        Rt = sb("R", [128, R_BYTES // 4], F32)
        R = Carver(Rt)
        ident = sb("ident", [128, 128], BF16)
        negones = sb("negones", [128, 128], BF16)
        ones_bf = sb("ones_bf", [128, 128], BF16)
        uneg = sb("uneg", [128, 128], BF16)
        tri_f = sb("tri_f", [128, 128], F32)
        ones_f = sb("ones_f", [128, 128], F32)
        zeros_bf = sb("zeros_bf", [128, 128], BF16)
        sbmask = sb("sbmask", [128, 128], BF16)
        foxmask = sb("foxmask", [128, 128], BF16)
        ss = sb("ss", [128, NT], F32)
        rs = sb("rs", [128, NT], F32)
        wc = sb("wc", [128, 2, 3], F32)
        bsp = sb("bsp", [128, 4], F32)
        lng_b = sb("lng_b", [128, 256], F32)
        lnb_b = sb("lnb_b", [128, 256], F32)
        gq_b = sb("gq_b", [128, 64], F32)
        gk_b = sb("gk_b", [128, 64], F32)
        bf_b = sb("bf_b", [128, 4], F32)
        small = sb("small", [128, 256], F32)

        P = Prog(nc)
        rot_state = {}

        def rot(name, lst):
            i = rot_state.get(name, 0)
            rot_state[name] = i + 1
            return lst[i % len(lst)]

        P.memset(ones_bf[:], 1.0, eng="pool")
        P.memset(negones[:], -1.0, eng="pool")
        P.memset(zeros_bf[:], 0.0, eng="pool")
        P.memset(ones_f[:], 1.0, eng="pool")

        def asel(out, in_, pattern, op, fill, base, cm):
            P.add("pool", lambda e: e.affine_select(out, in_, pattern, op, fill, base=base, channel_multiplier=cm),
                  [in_], [out], name="asel")

        asel(ident[:], ones_bf[:], [[-1, 128]], ALU.is_equal, 0.0, 0, 1)
        asel(uneg[:], negones[:], [[-1, 128]], ALU.is_ge, 0.0, 0, 1)
        asel(tri_f[:], ones_f[:], [[1, 128]], ALU.is_ge, 0.0, 0, -1)
        asel(sbmask[:], zeros_bf[:], [[1, 128]], ALU.is_gt, NEG, 0, -1)
        asel(foxmask[:], zeros_bf[:], [[1, 128]], ALU.is_ge, NEG, 0, -1)

        xr = x_d.rearrange("(n p) d -> p n d", p=128)
        for i0 in range(0, NT, 2):
            P.dma(h[:, i0:i0 + 2, :], xr[:, i0:i0 + 2, :], eng=("sp" if (i0 // 2) % 2 == 0 else "act"))

        chunks = []
        loaded = [0]
        slot_of = {}

        def wsrc(ap2d):
            return ap2d.rearrange("(k p) n -> p k n", p=128)

        def plan_layer(l):
            w = w_in_d[l]
            for j in range(2):
                chunks.append((("A", l, j), [(w[:, g * 256 + j * 128: g * 256 + j * 128 + 128], g * 128) for g in range(3)]))
            chunks.append((("B", l), [(w[:, 768:1280], 0)]))
            chunks.append((("Cqk", l), [(w[:, 1280:1792], 0)]))
            chunks.append((("Cv", l), [(w[:, 1792:2048], 0)]))
            chunks.append((("Dqk", l), [(w[:, 2048:2560], 0)]))
            chunks.append((("Dvf", l), [(w[:, 2560:2820], 0)]))
            for dc in range(8):
                chunks.append((("G", l, dc), [(w[:, 2820 + i * 1024 + dc * 128: 2820 + i * 1024 + dc * 128 + 128], i * 128) for i in range(4)]))
            for hf in range(2):
                chunks.append((("O", l, hf), [(w_out_d[l][:, hf * 512:(hf + 1) * 512], 0)]))
            wf = w_fi_d[l]
            for ps_ in range(3):
                fcs = FFN_PASSES[ps_]
                for q in range(0, len(fcs), 2):
                    srcs = []
                    for qq, fc in enumerate(fcs[q:q + 2]):
                        srcs.append((wf[:, fc * 128:(fc + 1) * 128], qq * 256))
                        srcs.append((wf[:, FF + fc * 128: FF + (fc + 1) * 128], qq * 256 + 128))
                    chunks.append((("F", l, ps_, q // 2), srcs))

        FFN_PASSES = [list(range(0, 8)), list(range(8, 15)), list(range(15, 22))]
        for l in range(depth):
            plan_layer(l)

        def ensure_loaded(upto):
            while loaded[0] <= min(upto, len(chunks) - 1):
                ci = loaded[0]
                key, srcs = chunks[ci]
                slot = wslots[ci % len(wslots)]
                for (src, off) in srcs:
                    n = src.shape[1]
                    P.dma(slot[:, :, off:off + n], wsrc(src), eng="pool")
                slot_of[key] = (ci, slot)
                loaded[0] += 1

        def wget(key, ahead=1):
            ci = None
            for i_, (k_, _) in enumerate(chunks):
                if k_ == key:
                    ci = i_
                    break
            assert ci is not None, key
            ensure_loaded(ci + ahead)
            cj, slot = slot_of[key]
            assert cj == ci
            return slot

        def dump(name, src_ap, kind):
            if name not in dbg_d:
                return
            R2 = dbgbuf
            if kind == "fm":
                n = src_ap.shape[1]
                for c in range(n):
                    for G in range(4):
                        P.copy(R2[:, 0:512], src_ap[:, c, G * 512:(G + 1) * 512], eng="dve")
                        P.dma(dbg_d[name][:, c * S + G * 512: c * S + (G + 1) * 512], R2[:, 0:512], eng="sp")
            else:
                for i in range(NT):
                    P.dma(dbg_d[name].rearrange("(n p) d -> p n d", p=128)[:, i, :], src_ap[:, i, :], eng="sp")

        dbgbuf = sb("dbgbuf", [128, 512], F32) if dbg else None

        def norm_begin(g_row):
            nb = {}
            nb["gb"] = R.take([D], F32)
            nb["sq"] = R.take([D], BF16)
            nb["xn"] = [R.take([D], BF16) for _ in range(2)]
            P.dma(nb["gb"], g_row.partition_broadcast(128), eng="sp")
            return nb

        def norm_stage(nb, si, i):
            r_ = rs[:, i:i + 1]
            if si == 0:
                P.act(nb["sq"], h[:, i, :], AF.Square, accum_out=ss[:, i:i + 1])
            elif si == 1:
                P.ts(r_, ss[:, i:i + 1], 1.0 / D, EPS, ALU.mult, ALU.add)
            elif si == 2:
                P.act(r_, r_, AF.Sqrt)
            elif si == 3:
                P.add("dve", lambda e, o=r_: e.reciprocal(o, o), [r_], [r_], name="recip")
            elif si == 4:
                P.stt(nb["xn"][i % 2], h[:, i, :], r_, nb["gb"], ALU.mult, ALU.mult)
            elif si == 5:
                xn = nb["xn"][i % 2]
                bk = rot("tr", [0, 1])
                nb[("bk", i)] = bk
                pb = banks[bk][:].bitcast(BF16)
                for k in range(8):
                    P.transpose(pb[:, k * 128:(k + 1) * 128], xn[:, k * 128:(k + 1) * 128], ident[:])
            elif si == 6:
                pb = banks[nb[("bk", i)]][:].bitcast(BF16)
                P.copy(xnT[:, :, i * 128:(i + 1) * 128], pb.rearrange("p (k t) -> p k t", k=8), eng="act")

        NST = 7

        def norm_push(nb, t):
            for si in range(NST):
                j = t - si
                if 0 <= j < NT:
                    norm_stage(nb, si, j)

        def norm_flush(nb):
            for t in range(NT, NT + NST - 1):
                norm_push(nb, t)

        def rmsnorm(g_row):
            R.reset()
            nb = norm_begin(g_row)
            for i in range(NT):
                norm_push(nb, i)
            norm_flush(nb)

        def mixer_A(l):
            R.reset()
            P.dma(wc[:], wconvT_d[l].rearrange("(j p) k -> p j k", p=128), eng="sp")
            xcbs = [R.take([S + 2], BF16) for _ in range(2)]
            cxs = [R.take([512], F32) for _ in range(2)]
            cbs = [R.take([512], F32) for _ in range(2)]
            dg = R.take([2, 3, 128], BF16)
            for j in range(2):
                for k in range(3):
                    P.ts(dg[:, j, k, :], ident[:], wc[:, j, k:k + 1], None, ALU.mult)
            for xb_ in xcbs:
                P.memset(xb_[:, 0:2], 0.0, eng="dve")
            units = [(j, G) for j in range(2) for G in range(4)]
            ust = {}

            def front(n):
                j, G = units[n]
                if G == 0:
                    ust[("slot", j)] = wget(("A", l, j))
                slot = ust[("slot", j)]
                bset = rot("convset", [[0, 1, 2], [3, 4, 5]])
                ts_ = slice(G * 512, (G + 1) * 512)
                for gi in range(3):
                    for k in range(8):
                        P.mm(banks[bset[gi]][:], slot[:, k, gi * 128:(gi + 1) * 128], xnT[:, k, ts_],
                             start=(k == 0), stop=(k == 7))
                a = n % 2
                P.copy(cxs[a], banks[bset[2]][:], eng="act")
                P.tt(xcbs[j][:, 2 + G * 512: 2 + (G + 1) * 512], banks[bset[1]][:], cxs[a], ALU.mult)
                P.copy(cbs[a], banks[bset[0]][:], eng="act")

            def back(n):
                j, G = units[n]
                a = n % 2
                ts_ = slice(G * 512, (G + 1) * 512)
                by = rot("convy", [6, 7])
                for k in range(3):
                    P.mm(banks[by][:], dg[:, j, k, :], xcbs[j][:, G * 512 + k: G * 512 + k + 512],
                         start=(k == 0), stop=(k == 2))
                P.tt(ysT[:, 0 + j, ts_], banks[by][:], cbs[a], ALU.mult)

            for n in range(len(units) + 1):
                if n < len(units):
                    front(n)
                if n >= 1:
                    back(n - 1)

        def pipeline(n_items, stages):
            ns_ = len(stages)
            for t_ in range(n_items + ns_ - 1):
                for si_ in range(ns_):
                    j_ = t_ - si_
                    if 0 <= j_ < n_items:
                        stages[si_](j_)

        def mixer_B(l):
            R.reset()
            P.dma(bsp[:], b_spT_d[l], eng="sp")
            P.dma(lng_b[:], lng_d[l].partition_broadcast(128), eng="sp")
            P.dma(lnb_b[:], lnb_d[l].partition_broadcast(128), eng="sp")
            wsf = R.take([4, 128], F32)
            wsb = R.take([4, 128], BF16)
            wsT = R.take([4, 128], BF16)
            P.dma(wsf, w_sp_d[l].rearrange("g t s -> t g s"), eng="sp")
            for g in range(4):
                asel(wsb[:, g, :], wsf[:, g, :], [[-1, 128]], ALU.is_ge, 0.0, 0, 1)
            bk = rot("tr", [0, 1])
            pb = banks[bk][:].bitcast(BF16)
            for g in range(4):
                P.transpose(pb[:, g * 128:(g + 1) * 128], wsb[:, g, :], ident[:])
            P.copy(wsT, pb[:, 0:512].rearrange("p (g t) -> p g t", g=4), eng="dve")
            NB = 6
            guv = [R.take([512], F32) for _ in range(NB)]
            vn = [R.take([256], F32) for _ in range(NB)]
            vnb = [R.take([256], BF16) for _ in range(NB)]
            ybt = [R.take([256], BF16) for _ in range(NB)]
            slot = wget(("B", l))
            stt_ = {}

            def sm(i, lo, hi):
                b = i % NB
                return small[:, b * 16 + lo: b * 16 + hi]

            def s0(i):
                buv = rot("uv", [0, 1])
                stt_[("uv", i)] = buv
                for k in range(8):
                    P.mm(banks[buv][:], xnT[:, k, i * 128:(i + 1) * 128], slot[:, k, :], start=(k == 0), stop=(k == 7))

            def s1(i):
                P.act(guv[i % NB], banks[stt_[("uv", i)]][:], AF.Gelu_apprx_tanh)

            def s2(i):
                v = guv[i % NB][:, 256:512]
                st6, mv, rstd, dd_, m2_ = sm(i, 0, 6), sm(i, 8, 10), sm(i, 10, 11), sm(i, 11, 12), sm(i, 12, 13)
                P.add("dve", lambda e, o=st6, i_=v: e.bn_stats(o, i_), [v], [st6], name="bnstats")
                P.tt(dd_, st6[:, 1:2], st6[:, 4:5], ALU.subtract)
                P.tt(mv[:, 0:1], st6[:, 1:2], st6[:, 4:5], ALU.add)
                P.tt(m2_, st6[:, 2:3], st6[:, 5:6], ALU.add)
                P.ts(mv[:, 0:1], mv[:, 0:1], 0.5, None, ALU.mult)
                P.ts(m2_, m2_, 1.0 / 256, EPS, ALU.mult, ALU.add)
                P.tt(dd_, dd_, dd_, ALU.mult)
                P.stt(rstd, dd_, 0.25, m2_, ALU.mult, ALU.add)

            def s3(i):
                rstd = sm(i, 10, 11)
                P.act(rstd, rstd, AF.Sqrt)

            def s4(i):
                b = i % NB
                v = guv[b][:, 256:512]
                mv, rstd = sm(i, 8, 10), sm(i, 10, 11)
                P.add("dve", lambda e, o=rstd: e.reciprocal(o, o), [rstd], [rstd], name="recip")
                P.ts(vn[b], v, mv[:, 0:1], rstd, ALU.subtract, ALU.mult)
                P.tt(vn[b], vn[b], lng_b[:], ALU.mult)
                P.tt(vnb[b], vn[b], lnb_b[:], ALU.add)

            def s5(i):
                b = i % NB
                bmx = rot("mx", [2, 3])
                stt_[("mx", i)] = bmx
                for g in range(4):
                    P.mm(banks[bmx][:, g * 64:(g + 1) * 64], wsT[:, g, :], vnb[b][:, g * 64:(g + 1) * 64],
                         start=True, stop=True)

            def s6(i):
                b = i % NB
                bmx = stt_[("mx", i)]
                for g in range(4):
                    P.stt(ybt[b][:, g * 64:(g + 1) * 64], banks[bmx][:, g * 64:(g + 1) * 64], bsp[:, g:g + 1],
                          guv[b][:, g * 64:(g + 1) * 64], ALU.add, ALU.mult)

            def s7(i):
                b = i % NB
                btr = rot("trb", [4, 5])
                stt_[("tr", i)] = btr
                pb2 = banks[btr][:].bitcast(BF16)
                for c in range(2):
                    P.transpose(pb2[:, c * 128:(c + 1) * 128], ybt[b][:, c * 128:(c + 1) * 128], ident[:])

            def s8(i):
                pb2 = banks[stt_[("tr", i)]][:].bitcast(BF16)
                P.copy(ysT[:, 2:4, i * 128:(i + 1) * 128], pb2[:, 0:256].rearrange("p (c t) -> p c t", c=2), eng="act")

            pipeline(NT, [s0, s1, s2, s3, s4, s5, s6, s7, s8])

        def mixer_C(l):
            R.reset()
            qpad = R.take([4, S], BF16)
            P.memset(qpad, 0.0, eng="pool")
            kT = R.take([2, S], BF16)
            vtm = R.take([NT, 256], BF16)
            e_t = [R.take([512], F32) for _ in range(3)]
            sp_t = [R.take([512], BF16) for _ in range(3)]
            w_t = [R.take([512], BF16) for _ in range(3)]
            Sb = [R.take([512], BF16) for _ in range(2)]
            slot = wget(("Cqk", l))
            for qk in range(2):
                for c in range(2):
                    for G in range(4):
                        bk = rot("proj", [0, 1, 2, 3, 4, 5])
                        ts_ = slice(G * 512, (G + 1) * 512)
                        for k in range(8):
                            P.mm(banks[bk][:], slot[:, k, qk * 256 + c * 128: qk * 256 + (c + 1) * 128], xnT[:, k, ts_],
                                 start=(k == 0), stop=(k == 7))
                        if qk == 0:
                            P.act(qpad[0:64, 2 * c, ts_], banks[bk][0:64, :], AF.Copy, scale=0.125)
                            P.act(qpad[64:128, 2 * c + 1, ts_], banks[bk][64:128, :], AF.Copy, scale=0.125)
                        else:
                            P.copy(kT[:, c, ts_], banks[bk][:], eng="dve")
            slot = wget(("Cv", l))
            for i in range(NT):
                bk = rot("proj", [0, 1, 2, 3, 4, 5])
                for k in range(8):
                    P.mm(banks[bk][:, 0:256], xnT[:, k, i * 128:(i + 1) * 128], slot[:, k, 0:256],
                         start=(k == 0), stop=(k == 7))
                P.copy(vtm[:, i, :], banks[bk][:, 0:256], eng=("act" if i % 2 else "dve"))
            for hp in range(2):
                for G in range(4):
                    nkb = 4 * G + 4
                    jobs = []
                    for kb in range(nkb - 1, -1, -1):
                        for hh in range(2):
                            jobs.append((hh, kb))
                    OB = [6, 7]
                    for hh in range(2):
                        P.memset(Sb[hh], 0.0, eng="dve")
                    state = {}

                    def stage(si, job, jn):
                        hh, kb = job
                        po = hh * 64
                        r = kb - 4 * G
                        c0 = 128 * r if r >= 0 else 0
                        cs = slice(c0, 512)
                        qs = slice(G * 512 + c0, (G + 1) * 512)
                        ksl = slice(kb * 128, (kb + 1) * 128)
                        first = (kb == nkb - 1)
                        last = (kb == 0)
                        if si == 0:
                            zb = rot("z", [0, 1, 2, 3, 4, 5])
                            state[(job, "zb")] = zb
                            P.mm(banks[zb][:, cs], kT[:, hp, ksl], qpad[:, hp * 2 + hh, qs],
                                 start=True, stop=False)
                            if r >= 0:
                                P.mm(banks[zb][:, c0:c0 + 128], ident[:], sbmask[:], start=False, stop=False)
                        elif si == 1:
                            zb = state[(job, "zb")]
                            a = jn % 3
                            P.act(e_t[a][:, cs], banks[zb][:, cs], AF.Exp)
                            P.act(sp_t[a][:, cs], e_t[a][:, cs], AF.Ln, bias=1.0)
                        elif si == 2:
                            a = jn % 3
                            zb = state[(job, "zb")]
                            P.mm(banks[zb][:, cs], uneg[:], sp_t[a][:, cs], start=False, stop=first,
                                 skip_group_check=True)
                            if not first:
                                P.mm(banks[zb][:, cs], negones[:], Sb[hh][:, cs], start=False, stop=True,
                                     skip_group_check=True)
                            if not last:
                                P.tt(Sb[hh][:, cs], Sb[hh][:, cs], sp_t[a][:, cs], ALU.add)
                        elif si == 3:
                            a = jn % 3
                            zb = state[(job, "zb")]
                            P.act(w_t[a][:, cs], banks[zb][:, cs], AF.Exp)
                        elif si == 4:
                            a = jn % 3
                            P.mm(banks[OB[hh]][:, cs], vtm[:, kb, hp * 128:(hp + 1) * 128], w_t[a][:, cs],
                                 start=first, stop=last)
                            if last:
                                P.copy(ysT[po:po + 64, 4 + hp, G * 512:(G + 1) * 512], banks[OB[hh]][po:po + 64, :],
                                       eng="dve")

                    nst = 5
                    for t in range(len(jobs) + nst - 1):
                        for si in range(nst):
                            jn = t - si
                            if 0 <= jn < len(jobs):
                                stage(si, jobs[jn], jn)

        def mixer_D(l):
            R.reset()
            P.dma(gq_b[:], gq_d[l].partition_broadcast(128), eng="sp")
            P.dma(gk_b[:], gk_d[l].partition_broadcast(128), eng="sp")
            P.dma(bf_b[:], bf_d[l].partition_broadcast(128), eng="sp")
            P.ts(gq_b[:], gq_b[:], 0.125, None, ALU.mult)
            qpad = R.take([4, S], BF16)
            kpad = R.take([4, S], BF16)
            P.memset(qpad, 0.0, eng="pool")
            P.memset(kpad, 0.0, eng="pool")
            qpv = qpad.rearrange("p (c two) s -> p c two s", two=2)
            kpv = kpad.rearrange("p (c two) s -> p c two s", two=2)
            vtm = R.take([NT, 2, 192], BF16)
            P.memset(vtm[:, :, :, 64:128], 1.0, eng="pool")
            LF = R.take([NT, 4], F32)
            cpos = R.take([NT, 4], F32)
            carry = R.take([NT, 4], F32)
            r1 = carry
            r2 = LF
            cbf = R.take([NT, 4], BF16)
            off_shared = R.off
            ND = 4
            sq = [R.take([512], BF16) for _ in range(2)]
            qkc = [R.take([512], F32) for _ in range(ND)]
            qkn = [R.take([512], BF16) for _ in range(2)]
            R.off = off_shared
            TEq = R.take([NT, 4, 8], BF16)
            TEk = R.take([NT, 4, 8], BF16)
            p_t = [R.take([512], BF16) for _ in range(3)]
            rec_one = R.take([512], F32)
            rec = [rec_one, rec_one]
            slot_qk = wget(("Dqk", l))
            slot_vf = wget(("Dvf", l), ahead=0)
            stt_ = {}

            def smf(i, lo, hi):
                b = i % ND
                return small[:, 128 + b * 16 + lo: 128 + b * 16 + hi]

            def s0(i):
                tsl = slice(i * 128, (i + 1) * 128)
                bq = rot("fq", [0, 1])
                bv = rot("fv", [2, 3])
                stt_[("bq", i)] = bq
                stt_[("bv", i)] = bv
                for k in range(8):
                    P.mm(banks[bq][:], xnT[:, k, tsl], slot_qk[:, k, :], start=(k == 0), stop=(k == 7))
                for k in range(8):
                    P.mm(banks[bv][:, 0:260], xnT[:, k, tsl], slot_vf[:, k, 0:260], start=(k == 0), stop=(k == 7))

            def s1(i):
                bq, bv = stt_[("bq", i)], stt_[("bv", i)]
                P.act(sq[i % 2], banks[bq][:], AF.Square)
                P.copy(qkc[i % ND], banks[bq][:], eng="act")
                vsrc = banks[bv][:, 0:256].rearrange("p (c two d) -> p c two d", c=2, two=2)
                P.copy(vtm[:, i, :, 0:64], vsrc[:, :, 0, :], eng="act")
                P.copy(vtm[:, i, :, 128:192], vsrc[:, :, 1, :], eng="act")
                P.tt(smf(i, 0, 4), banks[bv][:, 256:260], bf_b[:], ALU.add)

            def s2(i):
                ssq = smf(i, 8, 16)
                P.add("dve", lambda e, o=ssq, i_=sq[i % 2]: e.tensor_reduce(o, i_.rearrange("p (j d) -> p j d", j=8), AX.X, ALU.add),
                      [sq[i % 2]], [ssq], name="tred")
                P.ts(ssq, ssq, 1.0 / 64, EPS, ALU.mult, ALU.add)

            def s3(i):
                fb = smf(i, 0, 4)
                ssq = smf(i, 8, 16)
                P.act(fb, fb, AF.Exp, scale=-1.0)
                P.act(LF[:, i, :], fb, AF.Ln, bias=1.0)
                P.act(ssq, ssq, AF.Sqrt)

            def s4(i):
                ssq = smf(i, 8, 16)
                P.add("dve", lambda e, o=ssq: e.reciprocal(o, o), [ssq], [ssq], name="recip")
                for j in range(8):
                    gbt = gq_b if j < 4 else gk_b
                    P.stt(qkn[i % 2][:, j * 64:(j + 1) * 64], qkc[i % ND][:, j * 64:(j + 1) * 64], ssq[:, j:j + 1], gbt[:],
                          ALU.mult, ALU.mult)

            def s5(i):
                btr = rot("ftr", [4, 5])
                stt_[("tr", i)] = btr
                pb = banks[btr][:].bitcast(BF16)
                for c in range(4):
                    P.transpose(pb[:, c * 128:(c + 1) * 128], qkn[i % 2][:, c * 128:(c + 1) * 128], ident[:])

            def s6(i):
                tsl = slice(i * 128, (i + 1) * 128)
                pb = banks[stt_[("tr", i)]][:].bitcast(BF16)
                P.copy(qpv[0:64, :, 0, tsl], pb[0:64, 0:256].rearrange("p (c t) -> p c t", c=2), eng="act")
                P.copy(qpv[64:128, :, 1, tsl], pb[64:128, 0:256].rearrange("p (c t) -> p c t", c=2), eng="act")
                P.copy(kpv[0:64, :, 0, tsl], pb[0:64, 256:512].rearrange("p (c t) -> p c t", c=2), eng="dve")
                P.copy(kpv[64:128, :, 1, tsl], pb[64:128, 256:512].rearrange("p (c t) -> p c t", c=2), eng="dve")

            pipeline(NT, [s0, s1, s2, s3, s4, s5, s6])
            wget(("G", l, 0), ahead=1)
            LF2 = LF.rearrange("p i h -> p (i h)")
            P.mm(banks[6][:, 0:64], tri_f[:], LF2, start=True, stop=True)
            P.mm(banks[7][:, 0:64], ones_f[:], LF2, start=True, stop=True)
            P.memset(carry[:, 0, :], 0.0, eng="dve")
            for i in range(1, NT):
                P.tt(carry[:, i, :], carry[:, i - 1, :], banks[7][:, (i - 1) * 4: i * 4], ALU.add)
            cp2 = cpos.rearrange("p i h -> p (i h)")
            P.tt(cp2, banks[6][:, 0:64], carry.rearrange("p i h -> p (i h)"), ALU.add)
            P.memset(TEq, 1.0, eng="dve")
            P.memset(TEk, 1.0, eng="dve")
            cur = cpos
            for t_, nxt in enumerate((r1, r2, None)):
                P.copy(cbf, cur, eng="dve")
                P.copy(TEk[:, :, :, 3 + t_], cbf, eng="dve")
                P.ts(TEq[:, :, :, t_], cbf, -1.0, None, ALU.mult)
                if nxt is not None:
                    P.tt(nxt, cur, cbf, ALU.subtract)
                    cur = nxt
            if f"cpos{l}" in dbg_d:
                P.dma(dbg_d[f"cpos{l}"], cp2, eng="sp")
            for (TE, XP) in ((TEq, qpad), (TEk, kpad)):
                for hd in range(4):
                    opo = 64 - (hd % 2) * 64
                    for G in range(4):
                        btr = rot("lb", [0, 1, 2, 3])
                        pb = banks[btr][:].bitcast(BF16)
                        for ii in range(4):
                            P.transpose(pb[0:8, ii * 128:(ii + 1) * 128], TE[:, G * 4 + ii, hd, :], ident[:])
                        P.copy(XP[opo:opo + 6, hd, G * 512:(G + 1) * 512], pb[0:6, 0:512], eng="dve")
            for hp in range(2):
                for G in range(4):
                    nkb = 4 * G + 4
                    jobs = []
                    for kb in range(nkb - 1, -1, -1):
                        for hh in range(2):
                            jobs.append((hh, kb))
                    OBn = [4, 5] if (hp * 4 + G) % 2 == 0 else [6, 7]
                    state = {}

                    def stage(si, job, jn):
                        hh, kb = job
                        po = hh * 64
                        r = kb - 4 * G
                        c0 = 128 * r if r >= 0 else 0
                        cs = slice(c0, 512)
                        qs = slice(G * 512 + c0, (G + 1) * 512)
                        ksl = slice(kb * 128, (kb + 1) * 128)
                        first = (kb == nkb - 1)
                        last = (kb == 0)
                        a = jn % 3
                        if si == 0:
                            lb = rot("lb", [0, 1, 2, 3])
                            state[(job, "lb")] = lb
                            P.mm(banks[lb][:, cs], kpad[:, hp * 2 + hh, ksl], qpad[:, hp * 2 + hh, qs],
                                 start=True, stop=(r < 0))
                            if r >= 0:
                                P.mm(banks[lb][:, c0:c0 + 128], ident[:], foxmask[:], start=False, stop=True)
                        elif si == 1:
                            lb = state[(job, "lb")]
                            P.act(p_t[a][:, cs], banks[lb][:, cs], AF.Exp)
                        elif si == 3:
                            opo = 64 - po
                            P.mm(banks[OBn[hh]][:, cs], vtm[:, kb, hp, hh * 64: hh * 64 + 128], p_t[a][:, cs],
                                 start=first, stop=last)
                            if last:
                                rc = rec[hh]
                                P.add("dve", lambda e, o=rc[opo:opo + 64, :], i_=banks[OBn[hh]][opo:opo + 64, :]: e.reciprocal(o, i_),
                                      [banks[OBn[hh]][opo:opo + 64, :]], [rc[opo:opo + 64, :]], name="recip")
                                P.tt(ysT[po:po + 64, 6 + hp, G * 512:(G + 1) * 512], banks[OBn[hh]][po:po + 64, :],
                                     rc[opo:opo + 64, :], ALU.mult)

                    nst = 4
                    for t in range(len(jobs) + nst - 1):
                        for si in range(nst):
                            jn = t - si
                            if 0 <= jn < len(jobs):
                                stage(si, jobs[jn], jn)

        def merge_out(l):
            R.reset()
            mT = R.take([8, S], BF16)
            off_after_mT = R.off
            wb = [R.take([4, 2, 128], BF16) for _ in range(2)]
            sg = [R.take([512], F32) for _ in range(2)]
            macc = [R.take([512], F32) for _ in range(2)]
            tmp = [R.take([512], F32) for _ in range(2)]
            n = 0
            def load_wb(dc_):
                for i_ in range(4):
                    P.dma(wb[dc_ % 2][:, i_, :, :],
                          w_br_d[l][i_, :, dc_ * 128:(dc_ + 1) * 128].rearrange("(j p) c -> p j c", p=128), eng="pool")

            load_wb(0)
            for dc in range(8):
                w_b = wb[dc % 2]
                slot = wget(("G", l, dc))
                if dc + 1 < 8:
                    load_wb(dc + 1)
                for G in range(4):
                    ts_ = slice(G * 512, (G + 1) * 512)
                    mc = macc[(dc * 4 + G) % 2]
                    for i in range(4):
                        gbk = rot("gate", [0, 1, 2, 3])
                        bbk = rot("br", [4, 5, 6, 7])
                        for k in range(8):
                            P.mm(banks[gbk][:], slot[:, k, i * 128:(i + 1) * 128], xnT[:, k, ts_],
                                 start=(k == 0), stop=(k == 7))
                        for j in range(2):
                            P.mm(banks[bbk][:], w_b[:, i, j, :], ysT[:, i * 2 + j, ts_], start=(j == 0), stop=(j == 1))
                        s_ = sg[n % 2]
                        t_ = tmp[n % 2]
                        n += 1
                        P.act(s_, banks[gbk][:], AF.Sigmoid)
                        if i == 0:
                            P.tt(mc, s_, banks[bbk][:], ALU.mult)
                        else:
                            P.tt(t_, s_, banks[bbk][:], ALU.mult)
                            if i < 3:
                                P.tt(mc, mc, t_, ALU.add)
                            else:
                                P.tt(mT[:, dc, ts_], mc, t_, ALU.add)
            dump(f"merged{l}", mT, "fm")
            R.off = off_after_mT
            nb = norm_begin(gffn_d[l])
            for hf in range(2):
                slot = wget(("O", l, hf))
                for i in range(NT):
                    bk = rot("wo", [2, 3, 4, 5, 6, 7])
                    for k in range(8):
                        P.mm(banks[bk][:], mT[:, k, i * 128:(i + 1) * 128], slot[:, k, :], start=(k == 0), stop=(k == 7))
                    hs = h[:, i, hf * 512:(hf + 1) * 512]
                    P.tt(hs, hs, banks[bk][:], ALU.add)
                    if hf == 1:
                        norm_push(nb, i)
            norm_flush(nb)
            dump(f"hmix{l}", h[:], "tm")

        def ffn(l, is_last, next_g=None):
            R.reset()
            hidT = ysT
            WoF2 = [R.take([8, D], BF16) for _ in range(2)]
            sg = [R.take([512], F32) for _ in range(2)]
            n = 0
            for ps_ in range(3):
                fcs = FFN_PASSES[ps_]
                nf = len(fcs)
                WoF = WoF2[ps_ % 2]
                for hf in range(2):
                    P.dma(WoF[:, 0:nf, hf * 512:(hf + 1) * 512],
                          w_fo_d[l][fcs[0] * 128:(fcs[-1] + 1) * 128, hf * 512:(hf + 1) * 512].rearrange("(f p) c -> p f c", p=128),
                          eng="pool")
                for q in range(0, nf, 2):
                    slot = wget(("F", l, ps_, q // 2))
                    for qq, fc in enumerate(fcs[q:q + 2]):
                        fl = q + qq
                        for G in range(4):
                            ts_ = slice(G * 512, (G + 1) * 512)
                            gbk = rot("gate", [0, 1, 2, 3])
                            ubk = rot("br", [4, 5, 6, 7])
                            for k in range(8):
                                P.mm(banks[gbk][:], slot[:, k, qq * 256: qq * 256 + 128], xnT[:, k, ts_],
                                     start=(k == 0), stop=(k == 7))
                            for k in range(8):
                                P.mm(banks[ubk][:], slot[:, k, qq * 256 + 128: qq * 256 + 256], xnT[:, k, ts_],
                                     start=(k == 0), stop=(k == 7))
                            s_ = sg[n % 2]
                            n += 1
                            P.act(s_, banks[gbk][:], AF.Silu)
                            P.tt(hidT[:, fl, ts_], s_, banks[ubk][:], ALU.mult)
                nb = None
                if ps_ == 2 and next_g is not None:
                    nb = norm_begin(next_g)
                for i in range(NT):
                    for hf in range(2):
                        bk = rot("wo", [2, 3, 4, 5, 6, 7])
                        for fl in range(nf):
                            P.mm(banks[bk][:], hidT[:, fl, i * 128:(i + 1) * 128], WoF[:, fl, hf * 512:(hf + 1) * 512],
                                 start=(fl == 0), stop=(fl == nf - 1))
                        hs = h[:, i, hf * 512:(hf + 1) * 512]
                        P.tt(hs, hs, banks[bk][:], ALU.add)
                    if is_last and ps_ == 2:
                        P.dma(out_d.rearrange("(n p) d -> p n d", p=128)[:, i, :], h[:, i, :], eng="sp")
                    if nb is not None:
                        norm_push(nb, i)
                if nb is not None:
                    norm_flush(nb)

        stop_after = [s_ for s_ in dbg if s_.startswith("stop:")]
        stop_after = stop_after[0][5:] if stop_after else None

        def finish_early():
            for i in range(NT):
                P.dma(out_d.rearrange("(n p) d -> p n d", p=128)[:, i, :], h[:, i, :], eng="sp")

        done = False
        rmsnorm(gmix_d[0])
        for l in range(depth):
            dump(f"xnT{l}", xnT[:], "fm")
            mixer_A(l)
            dump(f"ya{l}", ysT[:, 0:2, :], "fm")
            mixer_B(l)
            dump(f"yb{l}", ysT[:, 2:4, :], "fm")
            mixer_C(l)
            dump(f"yc{l}", ysT[:, 4:6, :], "fm")
            mixer_D(l)
            dump(f"yd{l}", ysT[:, 6:8, :], "fm")
            merge_out(l)
            if stop_after == f"M{l}":
                finish_early(); done = True; break
            ffn(l, is_last=(l == depth - 1), next_g=(gmix_d[l + 1] if l + 1 < depth else None))
            dump(f"h{l}", h[:], "tm")
        if not done and depth < DEPTH:
            finish_early()
        P.emit(st)
        nc._prog_stats = (len(P.ops), P.n_sems)
    return nc


def make_in_maps(inputs):
    f = lambda a: np.ascontiguousarray(np.asarray(a, dtype=np.float32))
    shared = {
        "gmix": f(inputs["norm_mix_g"]),
        "w_in": f(inputs["w_in"]),
        "wconvT": f(np.transpose(np.asarray(inputs["w_conv"]), (0, 2, 1))),
        "w_sp": f(inputs["w_spatial"]),
        "b_spT": f(np.transpose(np.asarray(inputs["b_spatial"]), (0, 2, 1))),
        "lng": f(inputs["gmlp_ln_g"]),
        "lnb": f(inputs["gmlp_ln_b"]),
        "gq": f(inputs["fox_q_norm_g"]),
        "gk": f(inputs["fox_k_norm_g"]),
        "bfg": f(inputs["fox_forget_b"]),
        "w_br": f(inputs["w_branch"]),
        "w_out": f(inputs["w_out"]),
        "gffn": f(inputs["norm_ffn_g"]),
        "w_fi": f(inputs["w_ffn_in"]),
        "w_fo": f(inputs["w_ffn_out"]),
    }
    x = np.asarray(inputs["x"], dtype=np.float32)
    return [dict(shared, x=np.ascontiguousarray(x[b])) for b in range(8)]


_NC_CACHE = {}


def kernel(**inputs):
    if "nc" not in _NC_CACHE:
        _NC_CACHE["nc"] = build_nc()
    nc = _NC_CACHE["nc"]
    in_maps = make_in_maps(inputs)
    res = run_bass_kernel_spmd(nc, in_maps, core_ids=list(range(8)))
    out = np.stack([np.asarray(r["out"], dtype=np.float32) for r in res.results], axis=0)
    return out
```

```python
import numpy as np
from contextlib import ExitStack
from concourse.bass_utils import run_bass_kernel_spmd

import concourse.bass as bass
import concourse.mybir as mybir

F32 = mybir.dt.float32
BF16 = mybir.dt.bfloat16
AF = mybir.ActivationFunctionType
ALU = mybir.AluOpType
AX = mybir.AxisListType

ENGINES = ["pe", "act", "dve", "pool", "sp"]
SEM_MAX = 30000
DMA_SEMS_PER_Q = 8


def _esize(dt):
    return mybir.dt.size(dt)


def _region(ap):
    sp = str(ap.space)
    if "SB" not in sp and "PSUM" not in sp:
        return None
    pat = ap.ap
    es = _esize(ap.dtype)
    pstride = pat[0][0]
    npart = pat[0][1]
    off = ap.offset
    if pstride > 0:
        p0 = off // pstride
        f0 = off % pstride
    else:
        p0 = 0
        f0 = off
    ext = 1
    for st, cnt in pat[1:]:
        ext += abs(st) * (cnt - 1)
    if "PSUM" in sp:
        return (ap.tensor.name, 0, 128, 0, 1 << 30)
    return (ap.tensor.name, p0, p0 + npart, f0 * es, (f0 + ext) * es)


class Op:
    __slots__ = ("idx", "eng", "fn", "reads", "writes", "is_dma", "deps",
                 "signaled", "sem", "val", "name", "small")


class Prog:
    def __init__(self, nc, same_engine_sync=False):
        self.nc = nc
        self.ops = []
        self.recs = {}
        self.same_engine_sync = same_engine_sync

    def add(self, eng, fn, reads=(), writes=(), is_dma=False, name=""):
        op = Op()
        op.idx = len(self.ops)
        op.eng = eng
        op.fn = fn
        op.is_dma = is_dma
        op.name = name
        op.signaled = False
        op.sem = None
        op.val = 0
        op.small = False
        for a in writes:
            n_el = 1
            for st_, cnt_ in a.ap[1:]:
                n_el *= cnt_
            if n_el < 128:
                op.small = True
        rr = [r for r in (_region(a) for a in reads) if r is not None]
        ww = [r for r in (_region(a) for a in writes) if r is not None]
        ww = ww + [r for r in rr if r[4] == (1 << 30)]
        rr = [r for r in rr if r[4] != (1 << 30)]
        deps = set()
        for (tn, plo, phi, blo, bhi) in rr:
            for rec in self.recs.get(tn, ()):
                if rec[5] and rec[0] < phi and plo < rec[1] and rec[2] < bhi and blo < rec[3]:
                    deps.add(rec[4])
        for (tn, plo, phi, blo, bhi) in ww:
            for rec in self.recs.get(tn, ()):
                if rec[0] < phi and plo < rec[1] and rec[2] < bhi and blo < rec[3]:
                    deps.add(rec[4])
        for (tn, plo, phi, blo, bhi) in ww:
            lst = self.recs.setdefault(tn, [])
            lst[:] = [rec for rec in lst if not (plo <= rec[0] and rec[1] <= phi and blo <= rec[2] and rec[3] <= bhi)]
            lst.append([plo, phi, blo, bhi, op.idx, True])
        for (tn, plo, phi, blo, bhi) in rr:
            lst = self.recs.setdefault(tn, [])
            if not is_dma:
                lst[:] = [rec for rec in lst if not ((not rec[5]) and (rec[4] == op.idx or (
                                                     self.ops[rec[4]].eng == eng and not self.ops[rec[4]].is_dma))
                                                     and plo <= rec[0] and rec[1] <= phi
                                                     and blo <= rec[2] and rec[3] <= bhi)]
            lst.append([plo, phi, blo, bhi, op.idx, False])
        deps.discard(op.idx)
        need = []
        for d in deps:
            dop = self.ops[d]
            if dop.eng == eng and not dop.is_dma and not is_dma and not self.same_engine_sync \
                    and (eng == "pe" or (eng != "pool" and not dop.small)):
                continue
            need.append(d)
            dop.signaled = True
        op.deps = need
        if is_dma:
            op.signaled = True
        self.ops.append(op)
        return op

    def mm(self, out, lhsT, rhs, start=True, stop=True, **kw):
        reads = [lhsT, rhs] + ([] if start else [out])
        return self.add("pe", lambda e: e.matmul(out, lhsT, rhs, start=start, stop=stop, **kw),
                        reads, [out], name="mm")

    def transpose(self, out, in_, ident):
        return self.add("pe", lambda e: e.transpose(out, in_, ident), [in_, ident], [out], name="tr")

    def act(self, out, in_, func, bias=None, scale=None, accum_out=None):
        kw = {}
        reads = [in_]
        writes = [out]
        if bias is not None:
            kw["bias"] = bias
            if not isinstance(bias, (int, float)):
                reads.append(bias)
        if scale is not None:
            kw["scale"] = scale
            if not isinstance(scale, (int, float)):
                reads.append(scale)
        if accum_out is not None:
            kw["accum_out"] = accum_out
            writes.append(accum_out)
        return self.add("act", lambda e: e.activation(out, in_, func, **kw), reads, writes, name="act")

    def tt(self, out, in0, in1, op, eng="dve"):
        return self.add(eng, lambda e: e.tensor_tensor(out, in0, in1, op), [in0, in1], [out], name="tt")

    def ts(self, out, in0, s1, s2, op0, op1=None, eng="dve"):
        reads = [in0]
        for s in (s1, s2):
            if s is not None and not isinstance(s, (int, float)):
                reads.append(s)
        if op1 is None:
            return self.add(eng, lambda e: e.tensor_scalar(out, in0, s1, None, op0), reads, [out], name="ts")
        return self.add(eng, lambda e: e.tensor_scalar(out, in0, s1, s2, op0, op1), reads, [out], name="ts")

    def stt(self, out, in0, scalar, in1, op0, op1):
        reads = [in0, in1]
        if not isinstance(scalar, (int, float)):
            reads.append(scalar)
        return self.add("dve", lambda e: e.scalar_tensor_tensor(out, in0, scalar, in1, op0, op1),
                        reads, [out], name="stt")

    def copy(self, out, in_, eng="dve"):
        if eng == "act":
            return self.add("act", lambda e: e.copy(out, in_), [in_], [out], name="copy")
        return self.add(eng, lambda e: e.tensor_copy(out, in_), [in_], [out], name="copy")

    def memset(self, ap, val, eng="dve"):
        return self.add(eng, lambda e: e.memset(ap, val), [], [ap], name="memset")

    def dma(self, out, in_, eng="sp", **kw):
        return self.add(eng, lambda e: e.dma_start(out=out, in_=in_, **kw), [in_], [out],
                        is_dma=True, name="dma")

    def emit(self, stack):
        nc = self.nc
        counters = {e: 0 for e in ENGINES}
        eng_sems = {e: [] for e in ENGINES}
        dma_sems = {e: [] for e in ENGINES}
        dma_cnt = {e: 0 for e in ENGINES}
        dma_semval = {}
        dma_hist = {e: [] for e in ENGINES}
        for op in self.ops:
            if op.is_dma:
                q = op.eng
                j = dma_cnt[q] % DMA_SEMS_PER_Q
                if len(dma_sems[q]) <= j:
                    dma_sems[q].append(stack.enter_context(nc.semaphore(f"d_{q}_{j}")))
                sem = dma_sems[q][j]
                v = dma_semval.get((q, j), 0) + 16
                dma_semval[(q, j)] = v
                op.sem = sem
                op.val = v
                if dma_cnt[q] >= DMA_SEMS_PER_Q:
                    prev = dma_hist[q][dma_cnt[q] - DMA_SEMS_PER_Q]
                    if prev.idx not in op.deps:
                        op.deps.append(prev.idx)
                dma_hist[q].append(op)
                dma_cnt[q] += 1
            elif op.signaled:
                e = op.eng
                c = counters[e]
                k = c // SEM_MAX
                if len(eng_sems[e]) <= k:
                    eng_sems[e].append(stack.enter_context(nc.semaphore(f"c_{e}_{k}")))
                op.sem = eng_sems[e][k]
                op.val = c % SEM_MAX + 1
                counters[e] = c + 1
        self.n_sems = sum(len(v) for v in eng_sems.values()) + sum(len(v) for v in dma_sems.values())
        block = stack.enter_context(nc.Block())
        ops = self.ops

        def run(engname):
            def body(e):
                waited = {}
                last = None
                for op in ops:
                    if op.eng != engname:
                        continue
                    for d in sorted(op.deps):
                        dop = ops[d]
                        key = id(dop.sem)
                        if waited.get(key, 0) >= dop.val:
                            continue
                        e.wait_ge(dop.sem, dop.val)
                        waited[key] = dop.val
                    inst = op.fn(e)
                    if op.sem is not None:
                        inst.then_inc(op.sem, 16 if op.is_dma else 1)
                    last = op
                for j, sem in enumerate(dma_sems[engname]):
                    v = dma_semval.get((engname, j), 0)
                    if v:
                        e.wait_ge(sem, v)
            return body

        block.tensor(run("pe"))
        block.scalar(run("act"))
        block.vector(run("dve"))
        block.gpsimd(run("pool"))
        block.sync(run("sp"))


S = 2048
D = 1024
NT = 16
NIN = 6916
FF = 2816
DEPTH = 2
EPS = 1e-6
NEG = -30000.0
R_BYTES = 57 * 1024


class Carver:
    def __init__(self, t):
        self.t = t
        self.off = 0

    def reset(self):
        self.off = 0

    def take(self, shape, dt):
        n = 1
        for s_ in shape:
            n *= s_
        nb = n * mybir.dt.size(dt)
        nb_al = (nb + 31) // 32 * 32
        assert self.off + nb_al <= R_BYTES, (self.off, nb_al)
        a = self.t[:, self.off // 4:(self.off + nb_al) // 4]
        self.off += nb_al
        if dt != F32:
            a = a.bitcast(dt)
        a = a[:, 0:n]
        if len(shape) == 2:
            a = a.rearrange("p (a b) -> p a b", a=shape[0])
        elif len(shape) == 3:
            a = a.rearrange("p (a b c) -> p a b c", a=shape[0], b=shape[1])
        return a


def build_nc(depth=DEPTH, dbg=()):
    nc = bass.Bass("TRN2", target_bir_lowering=False)

    def din(name, shape):
        return nc.dram_tensor(name, shape, F32, kind="ExternalInput").ap()

    x_d = din("x", [S, D])
    gmix_d = din("gmix", [DEPTH, D])
    w_in_d = din("w_in", [DEPTH, D, NIN])
    wconvT_d = din("wconvT", [DEPTH, 256, 3])
    w_sp_d = din("w_sp", [DEPTH, 4, 128, 128])
    b_spT_d = din("b_spT", [DEPTH, 128, 4])
    lng_d = din("lng", [DEPTH, 256])
    lnb_d = din("lnb", [DEPTH, 256])
    gq_d = din("gq", [DEPTH, 64])
    gk_d = din("gk", [DEPTH, 64])
    bf_d = din("bfg", [DEPTH, 4])
    w_br_d = din("w_br", [DEPTH, 4, 256, D])
    w_out_d = din("w_out", [DEPTH, D, D])
    gffn_d = din("gffn", [DEPTH, D])
    w_fi_d = din("w_fi", [DEPTH, D, 2 * FF])
    w_fo_d = din("w_fo", [DEPTH, FF, D])
    out_d = nc.dram_tensor("out", [S, D], F32, kind="ExternalOutput").ap()
    dbg_d = {}
    for name in dbg:
        if name.startswith("cpos"):
            dbg_d[name] = nc.dram_tensor("dbg_" + name, [128, 64], F32, kind="ExternalOutput").ap()
        elif name.startswith("y") or name.startswith("merged") or name.startswith("xnT") or name.startswith("fq") or name.startswith("fk"):
            shp = [128, (8 if (name.startswith("merged") or name.startswith("xnT")) else 2) * S]
            dbg_d[name] = nc.dram_tensor("dbg_" + name, shp, F32, kind="ExternalOutput").ap()
        else:
            dbg_d[name] = nc.dram_tensor("dbg_" + name, [S, D], F32, kind="ExternalOutput").ap()

    with ExitStack() as st:
        def sb(name, shape, dt):
            return st.enter_context(nc.sbuf_tensor(name, shape, dt))

        banks = [st.enter_context(nc.psum_tensor(f"bank{i}", [128, 512], F32)) for i in range(8)]
        h = sb("h", [128, NT, D], F32)
        xnT = sb("xnT", [128, 8, S], BF16)
        ysT = sb("ysT", [128, 8, S], BF16)
        wslots = [sb(f"wslot{i}", [128, 8, 512], BF16) for i in range(2)]
        Rt = sb("R", [128, R_BYTES // 4], F32)
        R = Carver(Rt)
        ident = sb("ident", [128, 128], BF16)
        negones = sb("negones", [128, 128], BF16)
        ones_bf = sb("ones_bf", [128, 128], BF16)
        uneg = sb("uneg", [128, 128], BF16)
        tri_f = sb("tri_f", [128, 128], F32)
        ones_f = sb("ones_f", [128, 128], F32)
        zeros_bf = sb("zeros_bf", [128, 128], BF16)
        sbmask = sb("sbmask", [128, 128], BF16)
        foxmask = sb("foxmask", [128, 128], BF16)
        ss = sb("ss", [128, NT], F32)
        rs = sb("rs", [128, NT], F32)
        wc = sb("wc", [128, 2, 3], F32)
        bsp = sb("bsp", [128, 4], F32)
        lng_b = sb("lng_b", [128, 256], F32)
        lnb_b = sb("lnb_b", [128, 256], F32)
        gq_b = sb("gq_b", [128, 64], F32)
        gk_b = sb("gk_b", [128, 64], F32)
        bf_b = sb("bf_b", [128, 4], F32)
        small = sb("small", [128, 256], F32)

        P = Prog(nc)
        rot_state = {}

        def rot(name, lst):
            i = rot_state.get(name, 0)
            rot_state[name] = i + 1
            return lst[i % len(lst)]

        P.memset(ones_bf[:], 1.0, eng="pool")
        P.memset(negones[:], -1.0, eng="pool")
        P.memset(zeros_bf[:], 0.0, eng="pool")
        P.memset(ones_f[:], 1.0, eng="pool")

        def asel(out, in_, pattern, op, fill, base, cm):
            P.add("pool", lambda e: e.affine_select(out, in_, pattern, op, fill, base=base, channel_multiplier=cm),
                  [in_], [out], name="asel")

        asel(ident[:], ones_bf[:], [[-1, 128]], ALU.is_equal, 0.0, 0, 1)
        asel(uneg[:], negones[:], [[-1, 128]], ALU.is_ge, 0.0, 0, 1)
        asel(tri_f[:], ones_f[:], [[1, 128]], ALU.is_ge, 0.0, 0, -1)
        asel(sbmask[:], zeros_bf[:], [[1, 128]], ALU.is_gt, NEG, 0, -1)
        asel(foxmask[:], zeros_bf[:], [[1, 128]], ALU.is_ge, NEG, 0, -1)

        xr = x_d.rearrange("(n p) d -> p n d", p=128)
        for i0 in range(0, NT, 2):
            P.dma(h[:, i0:i0 + 2, :], xr[:, i0:i0 + 2, :], eng=("sp" if (i0 // 2) % 2 == 0 else "act"))

        chunks = []
        loaded = [0]
        slot_of = {}

        def wsrc(ap2d):
            return ap2d.rearrange("(k p) n -> p k n", p=128)

        def plan_layer(l):
            w = w_in_d[l]
            for j in range(2):
                chunks.append((("A", l, j), [(w[:, g * 256 + j * 128: g * 256 + j * 128 + 128], g * 128) for g in range(3)]))
            chunks.append((("B", l), [(w[:, 768:1280], 0)]))
            chunks.append((("Cqk", l), [(w[:, 1280:1792], 0)]))
            chunks.append((("Cv", l), [(w[:, 1792:2048], 0)]))
            chunks.append((("Dqk", l), [(w[:, 2048:2560], 0)]))
            chunks.append((("Dvf", l), [(w[:, 2560:2820], 0)]))
            for dc in range(8):
                chunks.append((("G", l, dc), [(w[:, 2820 + i * 1024 + dc * 128: 2820 + i * 1024 + dc * 128 + 128], i * 128) for i in range(4)]))
            for hf in range(2):
                chunks.append((("O", l, hf), [(w_out_d[l][:, hf * 512:(hf + 1) * 512], 0)]))
            wf = w_fi_d[l]
            for ps_ in range(3):
                fcs = FFN_PASSES[ps_]
                for q in range(0, len(fcs), 2):
                    srcs = []
                    for qq, fc in enumerate(fcs[q:q + 2]):
                        srcs.append((wf[:, fc * 128:(fc + 1) * 128], qq * 256))
                        srcs.append((wf[:, FF + fc * 128: FF + (fc + 1) * 128], qq * 256 + 128))
                    chunks.append((("F", l, ps_, q // 2), srcs))

        FFN_PASSES = [list(range(0, 8)), list(range(8, 15)), list(range(15, 22))]
        for l in range(depth):
            plan_layer(l)

        def ensure_loaded(upto):
            while loaded[0] <= min(upto, len(chunks) - 1):
                ci = loaded[0]
                key, srcs = chunks[ci]
                slot = wslots[ci % len(wslots)]
                for (src, off) in srcs:
                    n = src.shape[1]
                    P.dma(slot[:, :, off:off + n], wsrc(src), eng="pool")
                slot_of[key] = (ci, slot)
                loaded[0] += 1

        def wget(key, ahead=1):
            ci = None
            for i_, (k_, _) in enumerate(chunks):
                if k_ == key:
                    ci = i_
                    break
            assert ci is not None, key
            ensure_loaded(ci + ahead)
            cj, slot = slot_of[key]
            assert cj == ci
            return slot

        def dump(name, src_ap, kind):
            if name not in dbg_d:
                return
            R2 = dbgbuf
            if kind == "fm":
                n = src_ap.shape[1]
                for c in range(n):
                    for G in range(4):
                        P.copy(R2[:, 0:512], src_ap[:, c, G * 512:(G + 1) * 512], eng="dve")
                        P.dma(dbg_d[name][:, c * S + G * 512: c * S + (G + 1) * 512], R2[:, 0:512], eng="sp")
            else:
                for i in range(NT):
                    P.dma(dbg_d[name].rearrange("(n p) d -> p n d", p=128)[:, i, :], src_ap[:, i, :], eng="sp")

        dbgbuf = sb("dbgbuf", [128, 512], F32) if dbg else None

        def norm_begin(g_row):
            nb = {}
            nb["gb"] = R.take([D], F32)
            nb["sq"] = R.take([D], BF16)
            nb["xn"] = [R.take([D], BF16) for _ in range(2)]
            P.dma(nb["gb"], g_row.partition_broadcast(128), eng="sp")
            return nb

        def norm_stage(nb, si, i):
            r_ = rs[:, i:i + 1]
            if si == 0:
                P.act(nb["sq"], h[:, i, :], AF.Square, accum_out=ss[:, i:i + 1])
            elif si == 1:
                P.ts(r_, ss[:, i:i + 1], 1.0 / D, EPS, ALU.mult, ALU.add)
            elif si == 2:
                P.act(r_, r_, AF.Sqrt)
            elif si == 3:
                P.add("dve", lambda e, o=r_: e.reciprocal(o, o), [r_], [r_], name="recip")
            elif si == 4:
                P.stt(nb["xn"][i % 2], h[:, i, :], r_, nb["gb"], ALU.mult, ALU.mult)
            elif si == 5:
                xn = nb["xn"][i % 2]
                bk = rot("tr", [0, 1])
                nb[("bk", i)] = bk
                pb = banks[bk][:].bitcast(BF16)
                for k in range(8):
                    P.transpose(pb[:, k * 128:(k + 1) * 128], xn[:, k * 128:(k + 1) * 128], ident[:])
            elif si == 6:
                pb = banks[nb[("bk", i)]][:].bitcast(BF16)
                P.copy(xnT[:, :, i * 128:(i + 1) * 128], pb.rearrange("p (k t) -> p k t", k=8), eng="act")

        NST = 7

        def norm_push(nb, t):
            for si in range(NST):
                j = t - si
                if 0 <= j < NT:
                    norm_stage(nb, si, j)

        def norm_flush(nb):
            for t in range(NT, NT + NST - 1):
                norm_push(nb, t)

        def rmsnorm(g_row):
            R.reset()
            nb = norm_begin(g_row)
            for i in range(NT):
                norm_push(nb, i)
            norm_flush(nb)

        def mixer_A(l):
            R.reset()
            P.dma(wc[:], wconvT_d[l].rearrange("(j p) k -> p j k", p=128), eng="sp")
            xcbs = [R.take([S + 2], BF16) for _ in range(2)]
            cxs = [R.take([512], F32) for _ in range(2)]
            cbs = [R.take([512], F32) for _ in range(2)]
            dg = R.take([2, 3, 128], BF16)
            for j in range(2):
                for k in range(3):
                    P.ts(dg[:, j, k, :], ident[:], wc[:, j, k:k + 1], None, ALU.mult)
            for xb_ in xcbs:
                P.memset(xb_[:, 0:2], 0.0, eng="dve")
            units = [(j, G) for j in range(2) for G in range(4)]
            ust = {}

            def front(n):
                j, G = units[n]
                if G == 0:
                    ust[("slot", j)] = wget(("A", l, j))
                slot = ust[("slot", j)]
                bset = rot("convset", [[0, 1, 2], [3, 4, 5]])
                ts_ = slice(G * 512, (G + 1) * 512)
                for gi in range(3):
                    for k in range(8):
                        P.mm(banks[bset[gi]][:], slot[:, k, gi * 128:(gi + 1) * 128], xnT[:, k, ts_],
                             start=(k == 0), stop=(k == 7))
                a = n % 2
                P.copy(cxs[a], banks[bset[2]][:], eng="act")
                P.tt(xcbs[j][:, 2 + G * 512: 2 + (G + 1) * 512], banks[bset[1]][:], cxs[a], ALU.mult)
                P.copy(cbs[a], banks[bset[0]][:], eng="act")

            def back(n):
                j, G = units[n]
                a = n % 2
                ts_ = slice(G * 512, (G + 1) * 512)
                by = rot("convy", [6, 7])
                for k in range(3):
                    P.mm(banks[by][:], dg[:, j, k, :], xcbs[j][:, G * 512 + k: G * 512 + k + 512],
                         start=(k == 0), stop=(k == 2))
                P.tt(ysT[:, 0 + j, ts_], banks[by][:], cbs[a], ALU.mult)

            for n in range(len(units) + 1):
                if n < len(units):
                    front(n)
                if n >= 1:
                    back(n - 1)

        def pipeline(n_items, stages):
            ns_ = len(stages)
            for t_ in range(n_items + ns_ - 1):
                for si_ in range(ns_):
                    j_ = t_ - si_
                    if 0 <= j_ < n_items:
                        stages[si_](j_)

        def mixer_B(l):
            R.reset()
            P.dma(bsp[:], b_spT_d[l], eng="sp")
            P.dma(lng_b[:], lng_d[l].partition_broadcast(128), eng="sp")
            P.dma(lnb_b[:], lnb_d[l].partition_broadcast(128), eng="sp")
            wsf = R.take([4, 128], F32)
            wsb = R.take([4, 128], BF16)
            wsT = R.take([4, 128], BF16)
            P.dma(wsf, w_sp_d[l].rearrange("g t s -> t g s"), eng="sp")
            for g in range(4):
                asel(wsb[:, g, :], wsf[:, g, :], [[-1, 128]], ALU.is_ge, 0.0, 0, 1)
            bk = rot("tr", [0, 1])
            pb = banks[bk][:].bitcast(BF16)
            for g in range(4):
                P.transpose(pb[:, g * 128:(g + 1) * 128], wsb[:, g, :], ident[:])
            P.copy(wsT, pb[:, 0:512].rearrange("p (g t) -> p g t", g=4), eng="dve")
            NB = 6
            guv = [R.take([512], F32) for _ in range(NB)]
            vn = [R.take([256], F32) for _ in range(NB)]
            vnb = [R.take([256], BF16) for _ in range(NB)]
            ybt = [R.take([256], BF16) for _ in range(NB)]
            slot = wget(("B", l))
            stt_ = {}

            def sm(i, lo, hi):
                b = i % NB
                return small[:, b * 16 + lo: b * 16 + hi]

            def s0(i):
                buv = rot("uv", [0, 1])
                stt_[("uv", i)] = buv
                for k in range(8):
                    P.mm(banks[buv][:], xnT[:, k, i * 128:(i + 1) * 128], slot[:, k, :], start=(k == 0), stop=(k == 7))

            def s1(i):
                P.act(guv[i % NB], banks[stt_[("uv", i)]][:], AF.Gelu_apprx_tanh)

            def s2(i):
                v = guv[i % NB][:, 256:512]
                st6, mv, rstd, dd_, m2_ = sm(i, 0, 6), sm(i, 8, 10), sm(i, 10, 11), sm(i, 11, 12), sm(i, 12, 13)
                P.add("dve", lambda e, o=st6, i_=v: e.bn_stats(o, i_), [v], [st6], name="bnstats")
                P.tt(dd_, st6[:, 1:2], st6[:, 4:5], ALU.subtract)
                P.tt(mv[:, 0:1], st6[:, 1:2], st6[:, 4:5], ALU.add)
                P.tt(m2_, st6[:, 2:3], st6[:, 5:6], ALU.add)
                P.ts(mv[:, 0:1], mv[:, 0:1], 0.5, None, ALU.mult)
                P.ts(m2_, m2_, 1.0 / 256, EPS, ALU.mult, ALU.add)
                P.tt(dd_, dd_, dd_, ALU.mult)
                P.stt(rstd, dd_, 0.25, m2_, ALU.mult, ALU.add)

            def s3(i):
                rstd = sm(i, 10, 11)
                P.act(rstd, rstd, AF.Sqrt)

            def s4(i):
                b = i % NB
                v = guv[b][:, 256:512]
                mv, rstd = sm(i, 8, 10), sm(i, 10, 11)
                P.add("dve", lambda e, o=rstd: e.reciprocal(o, o), [rstd], [rstd], name="recip")
                P.ts(vn[b], v, mv[:, 0:1], rstd, ALU.subtract, ALU.mult)
                P.tt(vn[b], vn[b], lng_b[:], ALU.mult)
                P.tt(vnb[b], vn[b], lnb_b[:], ALU.add)

            def s5(i):
                b = i % NB
                bmx = rot("mx", [2, 3])
                stt_[("mx", i)] = bmx
                for g in range(4):
                    P.mm(banks[bmx][:, g * 64:(g + 1) * 64], wsT[:, g, :], vnb[b][:, g * 64:(g + 1) * 64],
                         start=True, stop=True)

            def s6(i):
                b = i % NB
                bmx = stt_[("mx", i)]
                for g in range(4):
                    P.stt(ybt[b][:, g * 64:(g + 1) * 64], banks[bmx][:, g * 64:(g + 1) * 64], bsp[:, g:g + 1],
                          guv[b][:, g * 64:(g + 1) * 64], ALU.add, ALU.mult)

            def s7(i):
                b = i % NB
                btr = rot("trb", [4, 5])
                stt_[("tr", i)] = btr
                pb2 = banks[btr][:].bitcast(BF16)
                for c in range(2):
                    P.transpose(pb2[:, c * 128:(c + 1) * 128], ybt[b][:, c * 128:(c + 1) * 128], ident[:])

            def s8(i):
                pb2 = banks[stt_[("tr", i)]][:].bitcast(BF16)
                P.copy(ysT[:, 2:4, i * 128:(i + 1) * 128], pb2[:, 0:256].rearrange("p (c t) -> p c t", c=2), eng="act")

            pipeline(NT, [s0, s1, s2, s3, s4, s5, s6, s7, s8])

        def mixer_C(l):
            R.reset()
            qpad = R.take([4, S], BF16)
            P.memset(qpad, 0.0, eng="pool")
            kT = R.take([2, S], BF16)
            vtm = R.take([NT, 256], BF16)
            e_t = [R.take([512], F32) for _ in range(3)]
            sp_t = [R.take([512], BF16) for _ in range(3)]
            w_t = [R.take([512], BF16) for _ in range(3)]
            Sb = [R.take([512], BF16) for _ in range(2)]
            slot = wget(("Cqk", l))
            for qk in range(2):
                for c in range(2):
                    for G in range(4):
                        bk = rot("proj", [0, 1, 2, 3, 4, 5])
                        ts_ = slice(G * 512, (G + 1) * 512)
                        for k in range(8):
                            P.mm(banks[bk][:], slot[:, k, qk * 256 + c * 128: qk * 256 + (c + 1) * 128], xnT[:, k, ts_],
                                 start=(k == 0), stop=(k == 7))
                        if qk == 0:
                            P.act(qpad[0:64, 2 * c, ts_], banks[bk][0:64, :], AF.Copy, scale=0.125)
                            P.act(qpad[64:128, 2 * c + 1, ts_], banks[bk][64:128, :], AF.Copy, scale=0.125)
                        else:
                            P.copy(kT[:, c, ts_], banks[bk][:], eng="dve")
            slot = wget(("Cv", l))
            for i in range(NT):
                bk = rot("proj", [0, 1, 2, 3, 4, 5])
                for k in range(8):
                    P.mm(banks[bk][:, 0:256], xnT[:, k, i * 128:(i + 1) * 128], slot[:, k, 0:256],
                         start=(k == 0), stop=(k == 7))
                P.copy(vtm[:, i, :], banks[bk][:, 0:256], eng=("act" if i % 2 else "dve"))
            jobs = []
            for hp in range(2):
                for G in range(4):
                    for kb in range(4 * G + 3, -1, -1):
                        for hh in range(2):
                            jobs.append((hp, G, hh, kb))
            state = {}

            def stage(si, job, jn):
                hp, G, hh, kb = job
                nkb = 4 * G + 4
                OB = [4, 5] if (hp * 4 + G) % 2 == 0 else [6, 7]
                po = hh * 64
                r = kb - 4 * G
                c0 = 128 * r if r >= 0 else 0
                cs = slice(c0, 512)
                qs = slice(G * 512 + c0, (G + 1) * 512)
                ksl = slice(kb * 128, (kb + 1) * 128)
                first = (kb == nkb - 1)
                last = (kb == 0)
                a = jn % 3
                if si == 0:
                    zb = rot("z", [0, 1, 2, 3])
                    state[(job, "zb")] = zb
                    P.mm(banks[zb][:, cs], kT[:, hp, ksl], qpad[:, hp * 2 + hh, qs],
                         start=True, stop=False)
                    if r >= 0:
                        P.mm(banks[zb][:, c0:c0 + 128], ident[:], sbmask[:], start=False, stop=False)
                elif si == 1:
                    zb = state[(job, "zb")]
                    P.act(e_t[a][:, cs], banks[zb][:, cs], AF.Exp)
                    P.act(sp_t[a][:, cs], e_t[a][:, cs], AF.Ln, bias=1.0)
                elif si == 2:
                    zb = state[(job, "zb")]
                    P.mm(banks[zb][:, cs], uneg[:], sp_t[a][:, cs], start=False, stop=first,
                         skip_group_check=True)
                    if not first:
                        P.mm(banks[zb][:, cs], negones[:], Sb[hh][:, cs], start=False, stop=True,
                             skip_group_check=True)
                    if first:
                        if c0 > 0:
                            P.memset(Sb[hh][:, 0:c0], 0.0, eng="dve")
                        P.copy(Sb[hh][:, cs], sp_t[a][:, cs], eng="dve")
                    elif not last:
                        P.tt(Sb[hh][:, cs], Sb[hh][:, cs], sp_t[a][:, cs], ALU.add)
                elif si == 3:
                    zb = state[(job, "zb")]
                    P.act(w_t[a][:, cs], banks[zb][:, cs], AF.Exp)
                elif si == 4:
                    P.mm(banks[OB[hh]][:, cs], vtm[:, kb, hp * 128:(hp + 1) * 128], w_t[a][:, cs],
                         start=first, stop=last)
                    if last:
                        P.copy(ysT[po:po + 64, 4 + hp, G * 512:(G + 1) * 512], banks[OB[hh]][po:po + 64, :],
                               eng="dve")

            nst = 5
            for t in range(len(jobs) + nst - 1):
                for si in range(nst):
                    jn = t - si
                    if 0 <= jn < len(jobs):
                        stage(si, jobs[jn], jn)

        def mixer_D(l):
            R.reset()
            P.dma(gq_b[:], gq_d[l].partition_broadcast(128), eng="sp")
            P.dma(gk_b[:], gk_d[l].partition_broadcast(128), eng="sp")
            P.dma(bf_b[:], bf_d[l].partition_broadcast(128), eng="sp")
            P.ts(gq_b[:], gq_b[:], 0.125, None, ALU.mult)
            qpad = R.take([4, S], BF16)
            kpad = R.take([4, S], BF16)
            P.memset(qpad, 0.0, eng="pool")
            P.memset(kpad, 0.0, eng="pool")
            qpv = qpad.rearrange("p (c two) s -> p c two s", two=2)
            kpv = kpad.rearrange("p (c two) s -> p c two s", two=2)
            vtm = R.take([NT, 2, 192], BF16)
            P.memset(vtm[:, :, :, 64:128], 1.0, eng="pool")
            LF = R.take([NT, 4], F32)
            cpos = R.take([NT, 4], F32)
            carry = R.take([NT, 4], F32)
            r1 = carry
            r2 = LF
            cbf = R.take([NT, 4], BF16)
            off_shared = R.off
            ND = 4
            sq = [R.take([512], BF16) for _ in range(2)]
            qkc = [R.take([512], F32) for _ in range(ND)]
            qkn = [R.take([512], BF16) for _ in range(2)]
            R.off = off_shared
            TEq = R.take([NT, 4, 8], BF16)
            TEk = R.take([NT, 4, 8], BF16)
            p_t = [R.take([512], BF16) for _ in range(3)]
            rec_one = R.take([512], F32)
            rec = [rec_one, rec_one]
            slot_qk = wget(("Dqk", l))
            slot_vf = wget(("Dvf", l), ahead=0)
            stt_ = {}

            def smf(i, lo, hi):
                b = i % ND
                return small[:, 128 + b * 16 + lo: 128 + b * 16 + hi]

            def s0(i):
                tsl = slice(i * 128, (i + 1) * 128)
                bq = rot("fq", [0, 1])
                bv = rot("fv", [2, 3])
                stt_[("bq", i)] = bq
                stt_[("bv", i)] = bv
                for k in range(8):
                    P.mm(banks[bq][:], xnT[:, k, tsl], slot_qk[:, k, :], start=(k == 0), stop=(k == 7))
                for k in range(8):
                    P.mm(banks[bv][:, 0:260], xnT[:, k, tsl], slot_vf[:, k, 0:260], start=(k == 0), stop=(k == 7))

            def s1(i):
                bq, bv = stt_[("bq", i)], stt_[("bv", i)]
                P.act(sq[i % 2], banks[bq][:], AF.Square)
                P.copy(qkc[i % ND], banks[bq][:], eng="act")
                vsrc = banks[bv][:, 0:256].rearrange("p (c two d) -> p c two d", c=2, two=2)
                P.copy(vtm[:, i, :, 0:64], vsrc[:, :, 0, :], eng="act")
                P.copy(vtm[:, i, :, 128:192], vsrc[:, :, 1, :], eng="act")
                P.tt(smf(i, 0, 4), banks[bv][:, 256:260], bf_b[:], ALU.add)

            def s2(i):
                ssq = smf(i, 8, 16)
                P.add("dve", lambda e, o=ssq, i_=sq[i % 2]: e.tensor_reduce(o, i_.rearrange("p (j d) -> p j d", j=8), AX.X, ALU.add),
                      [sq[i % 2]], [ssq], name="tred")
                P.ts(ssq, ssq, 1.0 / 64, EPS, ALU.mult, ALU.add)

            def s3(i):
                fb = smf(i, 0, 4)
                ssq = smf(i, 8, 16)
                P.act(fb, fb, AF.Exp, scale=-1.0)
                P.act(LF[:, i, :], fb, AF.Ln, bias=1.0)
                P.act(ssq, ssq, AF.Sqrt)

            def s4(i):
                ssq = smf(i, 8, 16)
                P.add("dve", lambda e, o=ssq: e.reciprocal(o, o), [ssq], [ssq], name="recip")
                for j in range(8):
                    gbt = gq_b if j < 4 else gk_b
                    P.stt(qkn[i % 2][:, j * 64:(j + 1) * 64], qkc[i % ND][:, j * 64:(j + 1) * 64], ssq[:, j:j + 1], gbt[:],
                          ALU.mult, ALU.mult)

            def s5(i):
                btr = rot("ftr", [4, 5])
                stt_[("tr", i)] = btr
                pb = banks[btr][:].bitcast(BF16)
                for c in range(4):
                    P.transpose(pb[:, c * 128:(c + 1) * 128], qkn[i % 2][:, c * 128:(c + 1) * 128], ident[:])

            def s6(i):
                tsl = slice(i * 128, (i + 1) * 128)
                pb = banks[stt_[("tr", i)]][:].bitcast(BF16)
                P.copy(qpv[0:64, :, 0, tsl], pb[0:64, 0:256].rearrange("p (c t) -> p c t", c=2), eng="act")
                P.copy(qpv[64:128, :, 1, tsl], pb[64:128, 0:256].rearrange("p (c t) -> p c t", c=2), eng="act")
                P.copy(kpv[0:64, :, 0, tsl], pb[0:64, 256:512].rearrange("p (c t) -> p c t", c=2), eng="dve")
                P.copy(kpv[64:128, :, 1, tsl], pb[64:128, 256:512].rearrange("p (c t) -> p c t", c=2), eng="dve")

            pipeline(NT, [s0, s1, s2, s3, s4, s5, s6])
            wget(("G", l, 0), ahead=1)
            LF2 = LF.rearrange("p i h -> p (i h)")
            P.mm(banks[6][:, 0:64], tri_f[:], LF2, start=True, stop=True)
            P.mm(banks[7][:, 0:64], ones_f[:], LF2, start=True, stop=True)
            P.memset(carry[:, 0, :], 0.0, eng="dve")
            for i in range(1, NT):
                P.tt(carry[:, i, :], carry[:, i - 1, :], banks[7][:, (i - 1) * 4: i * 4], ALU.add)
            cp2 = cpos.rearrange("p i h -> p (i h)")
            P.tt(cp2, banks[6][:, 0:64], carry.rearrange("p i h -> p (i h)"), ALU.add)
            P.memset(TEq, 1.0, eng="dve")
            P.memset(TEk, 1.0, eng="dve")
            cur = cpos
            for t_, nxt in enumerate((r1, r2, None)):
                P.copy(cbf, cur, eng="dve")
                P.copy(TEk[:, :, :, 3 + t_], cbf, eng="dve")
                P.ts(TEq[:, :, :, t_], cbf, -1.0, None, ALU.mult)
                if nxt is not None:
                    P.tt(nxt, cur, cbf, ALU.subtract)
                    cur = nxt
            if f"cpos{l}" in dbg_d:
                P.dma(dbg_d[f"cpos{l}"], cp2, eng="sp")
            for (TE, XP) in ((TEq, qpad), (TEk, kpad)):
                for hd in range(4):
                    opo = 64 - (hd % 2) * 64
                    for G in range(4):
                        btr = rot("lb", [0, 1, 2, 3])
                        pb = banks[btr][:].bitcast(BF16)
                        for ii in range(4):
                            P.transpose(pb[0:8, ii * 128:(ii + 1) * 128], TE[:, G * 4 + ii, hd, :], ident[:])
                        P.copy(XP[opo:opo + 6, hd, G * 512:(G + 1) * 512], pb[0:6, 0:512], eng="dve")
            for hp in range(2):
                for G in range(4):
                    nkb = 4 * G + 4
                    jobs = []
                    for kb in range(nkb - 1, -1, -1):
                        for hh in range(2):
                            jobs.append((hh, kb))
                    OBn = [4, 5] if (hp * 4 + G) % 2 == 0 else [6, 7]
                    state = {}

                    def stage(si, job, jn):
                        hh, kb = job
                        po = hh * 64
                        r = kb - 4 * G
                        c0 = 128 * r if r >= 0 else 0
                        cs = slice(c0, 512)
                        qs = slice(G * 512 + c0, (G + 1) * 512)
                        ksl = slice(kb * 128, (kb + 1) * 128)
                        first = (kb == nkb - 1)
                        last = (kb == 0)
                        a = jn % 3
                        if si == 0:
                            lb = rot("lb", [0, 1, 2, 3])
                            state[(job, "lb")] = lb
                            P.mm(banks[lb][:, cs], kpad[:, hp * 2 + hh, ksl], qpad[:, hp * 2 + hh, qs],
                                 start=True, stop=(r < 0))
                            if r >= 0:
                                P.mm(banks[lb][:, c0:c0 + 128], ident[:], foxmask[:], start=False, stop=True)
                        elif si == 1:
                            lb = state[(job, "lb")]
                            P.act(p_t[a][:, cs], banks[lb][:, cs], AF.Exp)
                        elif si == 3:
                            opo = 64 - po
                            P.mm(banks[OBn[hh]][:, cs], vtm[:, kb, hp, hh * 64: hh * 64 + 128], p_t[a][:, cs],
                                 start=first, stop=last)
                            if last:
                                rc = rec[hh]
                                P.add("dve", lambda e, o=rc[opo:opo + 64, :], i_=banks[OBn[hh]][opo:opo + 64, :]: e.reciprocal(o, i_),
                                      [banks[OBn[hh]][opo:opo + 64, :]], [rc[opo:opo + 64, :]], name="recip")
                                P.tt(ysT[po:po + 64, 6 + hp, G * 512:(G + 1) * 512], banks[OBn[hh]][po:po + 64, :],
                                     rc[opo:opo + 64, :], ALU.mult)

                    nst = 4
                    for t in range(len(jobs) + nst - 1):
                        for si in range(nst):
                            jn = t - si
                            if 0 <= jn < len(jobs):
                                stage(si, jobs[jn], jn)

        def merge_out(l):
            R.reset()
            mT = R.take([8, S], BF16)
            off_after_mT = R.off
            wb = [R.take([4, 2, 128], BF16) for _ in range(2)]
            sg = [R.take([512], F32) for _ in range(2)]
            macc = [R.take([512], F32) for _ in range(2)]
            tmp = [R.take([512], F32) for _ in range(2)]
            n = 0
            for dc in range(8):
                w_b = wb[dc % 2]
                for i in range(4):
                    P.dma(w_b[:, i, :, :], w_br_d[l][i, :, dc * 128:(dc + 1) * 128].rearrange("(j p) c -> p j c", p=128),
                          eng="pool")
                slot = wget(("G", l, dc))
                for G in range(4):
                    ts_ = slice(G * 512, (G + 1) * 512)
                    mc = macc[(dc * 4 + G) % 2]
                    for i in range(4):
                        gbk = rot("gate", [0, 1, 2, 3])
                        bbk = rot("br", [4, 5, 6, 7])
                        for k in range(8):
                            P.mm(banks[gbk][:], slot[:, k, i * 128:(i + 1) * 128], xnT[:, k, ts_],
                                 start=(k == 0), stop=(k == 7))
                        for j in range(2):
                            P.mm(banks[bbk][:], w_b[:, i, j, :], ysT[:, i * 2 + j, ts_], start=(j == 0), stop=(j == 1))
                        s_ = sg[n % 2]
                        t_ = tmp[n % 2]
                        n += 1
                        P.act(s_, banks[gbk][:], AF.Sigmoid)
                        if i == 0:
                            P.tt(mc, s_, banks[bbk][:], ALU.mult)
                        else:
                            P.tt(t_, s_, banks[bbk][:], ALU.mult)
                            if i < 3:
                                P.tt(mc, mc, t_, ALU.add)
                            else:
                                P.tt(mT[:, dc, ts_], mc, t_, ALU.add)
            dump(f"merged{l}", mT, "fm")
            R.off = off_after_mT
            nb = norm_begin(gffn_d[l])
            for hf in range(2):
                slot = wget(("O", l, hf))
                for i in range(NT):
                    bk = rot("wo", [2, 3, 4, 5, 6, 7])
                    for k in range(8):
                        P.mm(banks[bk][:], mT[:, k, i * 128:(i + 1) * 128], slot[:, k, :], start=(k == 0), stop=(k == 7))
                    hs = h[:, i, hf * 512:(hf + 1) * 512]
                    P.tt(hs, hs, banks[bk][:], ALU.add)
                    if hf == 1:
                        norm_push(nb, i)
            norm_flush(nb)
            dump(f"hmix{l}", h[:], "tm")

        def ffn(l, is_last, next_g=None):
            R.reset()
            hidT = ysT
            WoF2 = [R.take([8, D], BF16) for _ in range(2)]
            sg = [R.take([512], F32) for _ in range(2)]
            n = 0
            for ps_ in range(3):
                fcs = FFN_PASSES[ps_]
                nf = len(fcs)
                WoF = WoF2[ps_ % 2]
                for hf in range(2):
                    P.dma(WoF[:, 0:nf, hf * 512:(hf + 1) * 512],
                          w_fo_d[l][fcs[0] * 128:(fcs[-1] + 1) * 128, hf * 512:(hf + 1) * 512].rearrange("(f p) c -> p f c", p=128),
                          eng="pool")
                for q in range(0, nf, 2):
                    slot = wget(("F", l, ps_, q // 2))
                    for qq, fc in enumerate(fcs[q:q + 2]):
                        fl = q + qq
                        for G in range(4):
                            ts_ = slice(G * 512, (G + 1) * 512)
                            gbk = rot("gate", [0, 1, 2, 3])
                            ubk = rot("br", [4, 5, 6, 7])
                            for k in range(8):
                                P.mm(banks[gbk][:], slot[:, k, qq * 256: qq * 256 + 128], xnT[:, k, ts_],
                                     start=(k == 0), stop=(k == 7))
                            for k in range(8):
                                P.mm(banks[ubk][:], slot[:, k, qq * 256 + 128: qq * 256 + 256], xnT[:, k, ts_],
                                     start=(k == 0), stop=(k == 7))
                            s_ = sg[n % 2]
                            n += 1
                            P.act(s_, banks[gbk][:], AF.Silu)
                            P.tt(hidT[:, fl, ts_], s_, banks[ubk][:], ALU.mult)
                nb = None
                if ps_ == 2 and next_g is not None:
                    nb = norm_begin(next_g)
                for i in range(NT):
                    for hf in range(2):
                        bk = rot("wo", [2, 3, 4, 5, 6, 7])
                        for fl in range(nf):
                            P.mm(banks[bk][:], hidT[:, fl, i * 128:(i + 1) * 128], WoF[:, fl, hf * 512:(hf + 1) * 512],
                                 start=(fl == 0), stop=(fl == nf - 1))
                        hs = h[:, i, hf * 512:(hf + 1) * 512]
                        P.tt(hs, hs, banks[bk][:], ALU.add)
                    if is_last and ps_ == 2:
                        P.dma(out_d.rearrange("(n p) d -> p n d", p=128)[:, i, :], h[:, i, :], eng="sp")
                    if nb is not None:
                        norm_push(nb, i)
                if nb is not None:
                    norm_flush(nb)

        stop_after = [s_ for s_ in dbg if s_.startswith("stop:")]
        stop_after = stop_after[0][5:] if stop_after else None

        def finish_early():
            for i in range(NT):
                P.dma(out_d.rearrange("(n p) d -> p n d", p=128)[:, i, :], h[:, i, :], eng="sp")

        done = False
        rmsnorm(gmix_d[0])
        for l in range(depth):
            dump(f"xnT{l}", xnT[:], "fm")
            mixer_A(l)
            dump(f"ya{l}", ysT[:, 0:2, :], "fm")
            mixer_B(l)
            dump(f"yb{l}", ysT[:, 2:4, :], "fm")
            mixer_C(l)
            dump(f"yc{l}", ysT[:, 4:6, :], "fm")
            mixer_D(l)
            dump(f"yd{l}", ysT[:, 6:8, :], "fm")
            merge_out(l)
            if stop_after == f"M{l}":
                finish_early(); done = True; break
            ffn(l, is_last=(l == depth - 1), next_g=(gmix_d[l + 1] if l + 1 < depth else None))
            dump(f"h{l}", h[:], "tm")
        if not done and depth < DEPTH:
            finish_early()
        P.emit(st)
        nc._prog_stats = (len(P.ops), P.n_sems)
    return nc


def make_in_maps(inputs):
    f = lambda a: np.ascontiguousarray(np.asarray(a, dtype=np.float32))
    shared = {
        "gmix": f(inputs["norm_mix_g"]),
        "w_in": f(inputs["w_in"]),
        "wconvT": f(np.transpose(np.asarray(inputs["w_conv"]), (0, 2, 1))),
        "w_sp": f(inputs["w_spatial"]),
        "b_spT": f(np.transpose(np.asarray(inputs["b_spatial"]), (0, 2, 1))),
        "lng": f(inputs["gmlp_ln_g"]),
        "lnb": f(inputs["gmlp_ln_b"]),
        "gq": f(inputs["fox_q_norm_g"]),
        "gk": f(inputs["fox_k_norm_g"]),
        "bfg": f(inputs["fox_forget_b"]),
        "w_br": f(inputs["w_branch"]),
        "w_out": f(inputs["w_out"]),
        "gffn": f(inputs["norm_ffn_g"]),
        "w_fi": f(inputs["w_ffn_in"]),
        "w_fo": f(inputs["w_ffn_out"]),
    }
    x = np.asarray(inputs["x"], dtype=np.float32)
    return [dict(shared, x=np.ascontiguousarray(x[b])) for b in range(8)]


_NC_CACHE = {}


def kernel(**inputs):
    if "nc" not in _NC_CACHE:
        _NC_CACHE["nc"] = build_nc()
    nc = _NC_CACHE["nc"]
    in_maps = make_in_maps(inputs)
    res = run_bass_kernel_spmd(nc, in_maps, core_ids=list(range(8)))
    out = np.stack([np.asarray(r["out"], dtype=np.float32) for r in res.results], axis=0)
    return out
```

```python
import numpy as np
from contextlib import ExitStack
from concourse.bass_utils import run_bass_kernel_spmd

import concourse.bass as bass
import concourse.mybir as mybir

F32 = mybir.dt.float32
BF16 = mybir.dt.bfloat16
AF = mybir.ActivationFunctionType
ALU = mybir.AluOpType
AX = mybir.AxisListType

ENGINES = ["pe", "act", "dve", "pool", "sp"]
SEM_MAX = 30000
DMA_SEMS_PER_Q = 8


def _esize(dt):
    return mybir.dt.size(dt)


def _region(ap):
    sp = str(ap.space)
    if "SB" not in sp and "PSUM" not in sp:
        return None
    pat = ap.ap
    es = _esize(ap.dtype)
    pstride = pat[0][0]
    npart = pat[0][1]
    off = ap.offset
    if pstride > 0:
        p0 = off // pstride
        f0 = off % pstride
    else:
        p0 = 0
        f0 = off
    ext = 1
    for st, cnt in pat[1:]:
        ext += abs(st) * (cnt - 1)
    if "PSUM" in sp:
        return (ap.tensor.name, 0, 128, 0, 1 << 30)
    return (ap.tensor.name, p0, p0 + npart, f0 * es, (f0 + ext) * es)


class Op:
    __slots__ = ("idx", "eng", "fn", "reads", "writes", "is_dma", "deps",
                 "signaled", "sem", "val", "name", "small")


class Prog:
    def __init__(self, nc, same_engine_sync=False):
        self.nc = nc
        self.ops = []
        self.recs = {}
        self.same_engine_sync = same_engine_sync

    def add(self, eng, fn, reads=(), writes=(), is_dma=False, name=""):
        op = Op()
        op.idx = len(self.ops)
        op.eng = eng
        op.fn = fn
        op.is_dma = is_dma
        op.name = name
        op.signaled = False
        op.sem = None
        op.val = 0
        op.small = False
        for a in writes:
            n_el = 1
            for st_, cnt_ in a.ap[1:]:
                n_el *= cnt_
            if n_el < 128:
                op.small = True
        rr = [r for r in (_region(a) for a in reads) if r is not None]
        ww = [r for r in (_region(a) for a in writes) if r is not None]
        ww = ww + [r for r in rr if r[4] == (1 << 30)]
        rr = [r for r in rr if r[4] != (1 << 30)]
        deps = set()
        for (tn, plo, phi, blo, bhi) in rr:
            for rec in self.recs.get(tn, ()):
                if rec[5] and rec[0] < phi and plo < rec[1] and rec[2] < bhi and blo < rec[3]:
                    deps.add(rec[4])
        for (tn, plo, phi, blo, bhi) in ww:
            for rec in self.recs.get(tn, ()):
                if rec[0] < phi and plo < rec[1] and rec[2] < bhi and blo < rec[3]:
                    deps.add(rec[4])
        for (tn, plo, phi, blo, bhi) in ww:
            lst = self.recs.setdefault(tn, [])
            lst[:] = [rec for rec in lst if not (plo <= rec[0] and rec[1] <= phi and blo <= rec[2] and rec[3] <= bhi)]
            lst.append([plo, phi, blo, bhi, op.idx, True])
        for (tn, plo, phi, blo, bhi) in rr:
            lst = self.recs.setdefault(tn, [])
            if not is_dma:
                lst[:] = [rec for rec in lst if not ((not rec[5]) and (rec[4] == op.idx or (
                                                     self.ops[rec[4]].eng == eng and not self.ops[rec[4]].is_dma))
                                                     and plo <= rec[0] and rec[1] <= phi
                                                     and blo <= rec[2] and rec[3] <= bhi)]
            lst.append([plo, phi, blo, bhi, op.idx, False])
        deps.discard(op.idx)
        need = []
        for d in deps:
            dop = self.ops[d]
            if dop.eng == eng and not dop.is_dma and not is_dma and not self.same_engine_sync \
                    and (eng == "pe" or (eng != "pool" and not dop.small)):
                continue
            need.append(d)
            dop.signaled = True
        op.deps = need
        if is_dma:
            op.signaled = True
        self.ops.append(op)
        return op

    def mm(self, out, lhsT, rhs, start=True, stop=True, **kw):
        reads = [lhsT, rhs] + ([] if start else [out])
        return self.add("pe", lambda e: e.matmul(out, lhsT, rhs, start=start, stop=stop, **kw),
                        reads, [out], name="mm")

    def transpose(self, out, in_, ident):
        return self.add("pe", lambda e: e.transpose(out, in_, ident), [in_, ident], [out], name="tr")

    def act(self, out, in_, func, bias=None, scale=None, accum_out=None):
        kw = {}
        reads = [in_]
        writes = [out]
        if bias is not None:
            kw["bias"] = bias
            if not isinstance(bias, (int, float)):
                reads.append(bias)
        if scale is not None:
            kw["scale"] = scale
            if not isinstance(scale, (int, float)):
                reads.append(scale)
        if accum_out is not None:
            kw["accum_out"] = accum_out
            writes.append(accum_out)
        return self.add("act", lambda e: e.activation(out, in_, func, **kw), reads, writes, name="act")

    def tt(self, out, in0, in1, op, eng="dve"):
        return self.add(eng, lambda e: e.tensor_tensor(out, in0, in1, op), [in0, in1], [out], name="tt")

    def ts(self, out, in0, s1, s2, op0, op1=None, eng="dve"):
        reads = [in0]
        for s in (s1, s2):
            if s is not None and not isinstance(s, (int, float)):
                reads.append(s)
        if op1 is None:
            return self.add(eng, lambda e: e.tensor_scalar(out, in0, s1, None, op0), reads, [out], name="ts")
        return self.add(eng, lambda e: e.tensor_scalar(out, in0, s1, s2, op0, op1), reads, [out], name="ts")

    def stt(self, out, in0, scalar, in1, op0, op1):
        reads = [in0, in1]
        if not isinstance(scalar, (int, float)):
            reads.append(scalar)
        return self.add("dve", lambda e: e.scalar_tensor_tensor(out, in0, scalar, in1, op0, op1),
                        reads, [out], name="stt")

    def copy(self, out, in_, eng="dve"):
        if eng == "act":
            return self.add("act", lambda e: e.copy(out, in_), [in_], [out], name="copy")
        return self.add(eng, lambda e: e.tensor_copy(out, in_), [in_], [out], name="copy")

    def memset(self, ap, val, eng="dve"):
        return self.add(eng, lambda e: e.memset(ap, val), [], [ap], name="memset")

    def dma(self, out, in_, eng="sp", **kw):
        return self.add(eng, lambda e: e.dma_start(out=out, in_=in_, **kw), [in_], [out],
                        is_dma=True, name="dma")

    def emit(self, stack):
        nc = self.nc
        counters = {e: 0 for e in ENGINES}
        eng_sems = {e: [] for e in ENGINES}
        dma_sems = {e: [] for e in ENGINES}
        dma_cnt = {e: 0 for e in ENGINES}
        dma_semval = {}
        dma_hist = {e: [] for e in ENGINES}
        for op in self.ops:
            if op.is_dma:
                q = op.eng
                j = dma_cnt[q] % DMA_SEMS_PER_Q
                if len(dma_sems[q]) <= j:
                    dma_sems[q].append(stack.enter_context(nc.semaphore(f"d_{q}_{j}")))
                sem = dma_sems[q][j]
                v = dma_semval.get((q, j), 0) + 16
                dma_semval[(q, j)] = v
                op.sem = sem
                op.val = v
                if dma_cnt[q] >= DMA_SEMS_PER_Q:
                    prev = dma_hist[q][dma_cnt[q] - DMA_SEMS_PER_Q]
                    if prev.idx not in op.deps:
                        op.deps.append(prev.idx)
                dma_hist[q].append(op)
                dma_cnt[q] += 1
            elif op.signaled:
                e = op.eng
                c = counters[e]
                k = c // SEM_MAX
                if len(eng_sems[e]) <= k:
                    eng_sems[e].append(stack.enter_context(nc.semaphore(f"c_{e}_{k}")))
                op.sem = eng_sems[e][k]
                op.val = c % SEM_MAX + 1
                counters[e] = c + 1
        self.n_sems = sum(len(v) for v in eng_sems.values()) + sum(len(v) for v in dma_sems.values())
        block = stack.enter_context(nc.Block())
        ops = self.ops

        def run(engname):
            def body(e):
                waited = {}
                last = None
                for op in ops:
                    if op.eng != engname:
                        continue
                    for d in sorted(op.deps):
                        dop = ops[d]
                        key = id(dop.sem)
                        if waited.get(key, 0) >= dop.val:
                            continue
                        e.wait_ge(dop.sem, dop.val)
                        waited[key] = dop.val
                    inst = op.fn(e)
                    if op.sem is not None:
                        inst.then_inc(op.sem, 16 if op.is_dma else 1)
                    last = op
                for j, sem in enumerate(dma_sems[engname]):
                    v = dma_semval.get((engname, j), 0)
                    if v:
                        e.wait_ge(sem, v)
            return body

        block.tensor(run("pe"))
        block.scalar(run("act"))
        block.vector(run("dve"))
        block.gpsimd(run("pool"))
        block.sync(run("sp"))


S = 2048
D = 1024
NT = 16
NIN = 6916
FF = 2816
DEPTH = 2
EPS = 1e-6
NEG = -30000.0
R_BYTES = 57 * 1024


class Carver:
    def __init__(self, t):
        self.t = t
        self.off = 0

    def reset(self):
        self.off = 0

    def take(self, shape, dt):
        n = 1
        for s_ in shape:
            n *= s_
        nb = n * mybir.dt.size(dt)
        nb_al = (nb + 31) // 32 * 32
        assert self.off + nb_al <= R_BYTES, (self.off, nb_al)
        a = self.t[:, self.off // 4:(self.off + nb_al) // 4]
        self.off += nb_al
        if dt != F32:
            a = a.bitcast(dt)
        a = a[:, 0:n]
        if len(shape) == 2:
            a = a.rearrange("p (a b) -> p a b", a=shape[0])
        elif len(shape) == 3:
            a = a.rearrange("p (a b c) -> p a b c", a=shape[0], b=shape[1])
        return a


def build_nc(depth=DEPTH, dbg=()):
    nc = bass.Bass("TRN2", target_bir_lowering=False)

    def din(name, shape):
        return nc.dram_tensor(name, shape, F32, kind="ExternalInput").ap()

    x_d = din("x", [S, D])
    gmix_d = din("gmix", [DEPTH, D])
    w_in_d = din("w_in", [DEPTH, D, NIN])
    wconvT_d = din("wconvT", [DEPTH, 256, 3])
    w_sp_d = din("w_sp", [DEPTH, 4, 128, 128])
    b_spT_d = din("b_spT", [DEPTH, 128, 4])
    lng_d = din("lng", [DEPTH, 256])
    lnb_d = din("lnb", [DEPTH, 256])
    gq_d = din("gq", [DEPTH, 64])
    gk_d = din("gk", [DEPTH, 64])
    bf_d = din("bfg", [DEPTH, 4])
    w_br_d = din("w_br", [DEPTH, 4, 256, D])
    w_out_d = din("w_out", [DEPTH, D, D])
    gffn_d = din("gffn", [DEPTH, D])
    w_fi_d = din("w_fi", [DEPTH, D, 2 * FF])
    w_fo_d = din("w_fo", [DEPTH, FF, D])
    out_d = nc.dram_tensor("out", [S, D], F32, kind="ExternalOutput").ap()
    dbg_d = {}
    for name in dbg:
        if name.startswith("cpos"):
            dbg_d[name] = nc.dram_tensor("dbg_" + name, [128, 64], F32, kind="ExternalOutput").ap()
        elif name.startswith("y") or name.startswith("merged") or name.startswith("xnT") or name.startswith("fq") or name.startswith("fk"):
            shp = [128, (8 if (name.startswith("merged") or name.startswith("xnT")) else 2) * S]
            dbg_d[name] = nc.dram_tensor("dbg_" + name, shp, F32, kind="ExternalOutput").ap()
        else:
            dbg_d[name] = nc.dram_tensor("dbg_" + name, [S, D], F32, kind="ExternalOutput").ap()

    with ExitStack() as st:
        def sb(name, shape, dt):
            return st.enter_context(nc.sbuf_tensor(name, shape, dt))

        banks = [st.enter_context(nc.psum_tensor(f"bank{i}", [128, 512], F32)) for i in range(8)]
        h = sb("h", [128, NT, D], F32)
        xnT = sb("xnT", [128, 8, S], BF16)
        ysT = sb("ysT", [128, 8, S], BF16)
        wslots = [sb(f"wslot{i}", [128, 8, 512], BF16) for i in range(2)]
        Rt = sb("R", [128, R_BYTES // 4], F32)
        R = Carver(Rt)
        ident = sb("ident", [128, 128], BF16)
        negones = sb("negones", [128, 128], BF16)
        ones_bf = sb("ones_bf", [128, 128], BF16)
        uneg = sb("uneg", [128, 128], BF16)
        tri_f = sb("tri_f", [128, 128], F32)
        ones_f = sb("ones_f", [128, 128], F32)
        zeros_bf = sb("zeros_bf", [128, 128], BF16)
        sbmask = sb("sbmask", [128, 128], BF16)
        foxmask = sb("foxmask", [128, 128], BF16)
        ss = sb("ss", [128, NT], F32)
        rs = sb("rs", [128, NT], F32)
        wc = sb("wc", [128, 2, 3], F32)
        bsp = sb("bsp", [128, 4], F32)
        lng_b = sb("lng_b", [128, 256], F32)
        lnb_b = sb("lnb_b", [128, 256], F32)
        gq_b = sb("gq_b", [128, 64], F32)
        gk_b = sb("gk_b", [128, 64], F32)
        bf_b = sb("bf_b", [128, 4], F32)
        small = sb("small", [128, 256], F32)

        P = Prog(nc)
        rot_state = {}

        def rot(name, lst):
            i = rot_state.get(name, 0)
            rot_state[name] = i + 1
            return lst[i % len(lst)]

        P.memset(ones_bf[:], 1.0, eng="pool")
        P.memset(negones[:], -1.0, eng="pool")
        P.memset(zeros_bf[:], 0.0, eng="pool")
        P.memset(ones_f[:], 1.0, eng="pool")

        def asel(out, in_, pattern, op, fill, base, cm):
            P.add("pool", lambda e: e.affine_select(out, in_, pattern, op, fill, base=base, channel_multiplier=cm),
                  [in_], [out], name="asel")

        asel(ident[:], ones_bf[:], [[-1, 128]], ALU.is_equal, 0.0, 0, 1)
        asel(uneg[:], negones[:], [[-1, 128]], ALU.is_ge, 0.0, 0, 1)
        asel(tri_f[:], ones_f[:], [[1, 128]], ALU.is_ge, 0.0, 0, -1)
        asel(sbmask[:], zeros_bf[:], [[1, 128]], ALU.is_gt, NEG, 0, -1)
        asel(foxmask[:], zeros_bf[:], [[1, 128]], ALU.is_ge, NEG, 0, -1)

        xr = x_d.rearrange("(n p) d -> p n d", p=128)
        for i0 in range(0, NT, 2):
            P.dma(h[:, i0:i0 + 2, :], xr[:, i0:i0 + 2, :], eng=("sp" if (i0 // 2) % 2 == 0 else "act"))

        chunks = []
        loaded = [0]
        slot_of = {}

        def wsrc(ap2d):
            return ap2d.rearrange("(k p) n -> p k n", p=128)

        def plan_layer(l):
            w = w_in_d[l]
            for j in range(2):
                chunks.append((("A", l, j), [(w[:, g * 256 + j * 128: g * 256 + j * 128 + 128], g * 128) for g in range(3)]))
            chunks.append((("B", l), [(w[:, 768:1280], 0)]))
            chunks.append((("Cqk", l), [(w[:, 1280:1792], 0)]))
            chunks.append((("Cv", l), [(w[:, 1792:2048], 0)]))
            chunks.append((("Dqk", l), [(w[:, 2048:2560], 0)]))
            chunks.append((("Dvf", l), [(w[:, 2560:2820], 0)]))
            for dc in range(8):
                chunks.append((("G", l, dc), [(w[:, 2820 + i * 1024 + dc * 128: 2820 + i * 1024 + dc * 128 + 128], i * 128) for i in range(4)]))
            for hf in range(2):
                chunks.append((("O", l, hf), [(w_out_d[l][:, hf * 512:(hf + 1) * 512], 0)]))
            wf = w_fi_d[l]
            for ps_ in range(3):
                fcs = FFN_PASSES[ps_]
                for q in range(0, len(fcs), 2):
                    srcs = []
                    for qq, fc in enumerate(fcs[q:q + 2]):
                        srcs.append((wf[:, fc * 128:(fc + 1) * 128], qq * 256))
                        srcs.append((wf[:, FF + fc * 128: FF + (fc + 1) * 128], qq * 256 + 128))
                    chunks.append((("F", l, ps_, q // 2), srcs))

        FFN_PASSES = [list(range(0, 8)), list(range(8, 15)), list(range(15, 22))]
        for l in range(depth):
            plan_layer(l)

        def ensure_loaded(upto):
            while loaded[0] <= min(upto, len(chunks) - 1):
                ci = loaded[0]
                key, srcs = chunks[ci]
                slot = wslots[ci % len(wslots)]
                for (src, off) in srcs:
                    n = src.shape[1]
                    P.dma(slot[:, :, off:off + n], wsrc(src), eng="pool")
                slot_of[key] = (ci, slot)
                loaded[0] += 1

        def wget(key, ahead=1):
            ci = None
            for i_, (k_, _) in enumerate(chunks):
                if k_ == key:
                    ci = i_
                    break
            assert ci is not None, key
            ensure_loaded(ci + ahead)
            cj, slot = slot_of[key]
            assert cj == ci
            return slot

        def dump(name, src_ap, kind):
            if name not in dbg_d:
                return
            R2 = dbgbuf
            if kind == "fm":
                n = src_ap.shape[1]
                for c in range(n):
                    for G in range(4):
                        P.copy(R2[:, 0:512], src_ap[:, c, G * 512:(G + 1) * 512], eng="dve")
                        P.dma(dbg_d[name][:, c * S + G * 512: c * S + (G + 1) * 512], R2[:, 0:512], eng="sp")
            else:
                for i in range(NT):
                    P.dma(dbg_d[name].rearrange("(n p) d -> p n d", p=128)[:, i, :], src_ap[:, i, :], eng="sp")

        dbgbuf = sb("dbgbuf", [128, 512], F32) if dbg else None

        def norm_begin(g_row):
            nb = {}
            nb["gb"] = R.take([D], F32)
            nb["sq"] = R.take([D], BF16)
            nb["xn"] = [R.take([D], BF16) for _ in range(2)]
            P.dma(nb["gb"], g_row.partition_broadcast(128), eng="sp")
            return nb

        def norm_stage(nb, si, i):
            r_ = rs[:, i:i + 1]
            if si == 0:
                P.act(nb["sq"], h[:, i, :], AF.Square, accum_out=ss[:, i:i + 1])
            elif si == 1:
                P.ts(r_, ss[:, i:i + 1], 1.0 / D, EPS, ALU.mult, ALU.add)
            elif si == 2:
                P.act(r_, r_, AF.Sqrt)
            elif si == 3:
                P.add("dve", lambda e, o=r_: e.reciprocal(o, o), [r_], [r_], name="recip")
            elif si == 4:
                P.stt(nb["xn"][i % 2], h[:, i, :], r_, nb["gb"], ALU.mult, ALU.mult)
            elif si == 5:
                xn = nb["xn"][i % 2]
                bk = rot("tr", [0, 1])
                nb[("bk", i)] = bk
                pb = banks[bk][:].bitcast(BF16)
                for k in range(8):
                    P.transpose(pb[:, k * 128:(k + 1) * 128], xn[:, k * 128:(k + 1) * 128], ident[:])
            elif si == 6:
                pb = banks[nb[("bk", i)]][:].bitcast(BF16)
                P.copy(xnT[:, :, i * 128:(i + 1) * 128], pb.rearrange("p (k t) -> p k t", k=8), eng="act")

        NST = 7

        def norm_push(nb, t):
            for si in range(NST):
                j = t - si
                if 0 <= j < NT:
                    norm_stage(nb, si, j)

        def norm_flush(nb):
            for t in range(NT, NT + NST - 1):
                norm_push(nb, t)

        def rmsnorm(g_row):
            R.reset()
            nb = norm_begin(g_row)
            for i in range(NT):
                norm_push(nb, i)
            norm_flush(nb)

        def mixer_A(l):
            R.reset()
            P.dma(wc[:], wconvT_d[l].rearrange("(j p) k -> p j k", p=128), eng="sp")
            xcbs = [R.take([S + 2], BF16) for _ in range(2)]
            cxs = [R.take([512], F32) for _ in range(2)]
            cbs = [R.take([512], F32) for _ in range(2)]
            dg = R.take([2, 3, 128], BF16)
            for j in range(2):
                for k in range(3):
                    P.ts(dg[:, j, k, :], ident[:], wc[:, j, k:k + 1], None, ALU.mult)
            for xb_ in xcbs:
                P.memset(xb_[:, 0:2], 0.0, eng="dve")
            units = [(j, G) for j in range(2) for G in range(4)]
            ust = {}

            def front(n):
                j, G = units[n]
                if G == 0:
                    ust[("slot", j)] = wget(("A", l, j))
                slot = ust[("slot", j)]
                bset = rot("convset", [[0, 1, 2], [3, 4, 5]])
                ts_ = slice(G * 512, (G + 1) * 512)
                for gi in range(3):
                    for k in range(8):
                        P.mm(banks[bset[gi]][:], slot[:, k, gi * 128:(gi + 1) * 128], xnT[:, k, ts_],
                             start=(k == 0), stop=(k == 7))
                a = n % 2
                P.copy(cxs[a], banks[bset[2]][:], eng="act")
                P.tt(xcbs[j][:, 2 + G * 512: 2 + (G + 1) * 512], banks[bset[1]][:], cxs[a], ALU.mult)
                P.copy(cbs[a], banks[bset[0]][:], eng="act")

            def back(n):
                j, G = units[n]
                a = n % 2
                ts_ = slice(G * 512, (G + 1) * 512)
                by = rot("convy", [6, 7])
                for k in range(3):
                    P.mm(banks[by][:], dg[:, j, k, :], xcbs[j][:, G * 512 + k: G * 512 + k + 512],
                         start=(k == 0), stop=(k == 2))
                P.tt(ysT[:, 0 + j, ts_], banks[by][:], cbs[a], ALU.mult)

            for n in range(len(units) + 1):
                if n < len(units):
                    front(n)
                if n >= 1:
                    back(n - 1)

        def pipeline(n_items, stages):
            ns_ = len(stages)
            for t_ in range(n_items + ns_ - 1):
                for si_ in range(ns_):
                    j_ = t_ - si_
                    if 0 <= j_ < n_items:
                        stages[si_](j_)

        def mixer_B(l):
            R.reset()
            P.dma(bsp[:], b_spT_d[l], eng="sp")
            P.dma(lng_b[:], lng_d[l].partition_broadcast(128), eng="sp")
            P.dma(lnb_b[:], lnb_d[l].partition_broadcast(128), eng="sp")
            wsf = R.take([4, 128], F32)
            wsb = R.take([4, 128], BF16)
            wsT = R.take([4, 128], BF16)
            P.dma(wsf, w_sp_d[l].rearrange("g t s -> t g s"), eng="sp")
            for g in range(4):
                asel(wsb[:, g, :], wsf[:, g, :], [[-1, 128]], ALU.is_ge, 0.0, 0, 1)
            bk = rot("tr", [0, 1])
            pb = banks[bk][:].bitcast(BF16)
            for g in range(4):
                P.transpose(pb[:, g * 128:(g + 1) * 128], wsb[:, g, :], ident[:])
            P.copy(wsT, pb[:, 0:512].rearrange("p (g t) -> p g t", g=4), eng="dve")
            NB = 6
            guv = [R.take([512], F32) for _ in range(NB)]
            vn = [R.take([256], F32) for _ in range(NB)]
            vnb = [R.take([256], BF16) for _ in range(NB)]
            ybt = [R.take([256], BF16) for _ in range(NB)]
            slot = wget(("B", l))
            stt_ = {}

            def sm(i, lo, hi):
                b = i % NB
                return small[:, b * 16 + lo: b * 16 + hi]

            def s0(i):
                buv = rot("uv", [0, 1])
                stt_[("uv", i)] = buv
                for k in range(8):
                    P.mm(banks[buv][:], xnT[:, k, i * 128:(i + 1) * 128], slot[:, k, :], start=(k == 0), stop=(k == 7))

            def s1(i):
                P.act(guv[i % NB], banks[stt_[("uv", i)]][:], AF.Gelu_apprx_tanh)

            def s2(i):
                v = guv[i % NB][:, 256:512]
                st6, mv, rstd, dd_, m2_ = sm(i, 0, 6), sm(i, 8, 10), sm(i, 10, 11), sm(i, 11, 12), sm(i, 12, 13)
                P.add("dve", lambda e, o=st6, i_=v: e.bn_stats(o, i_), [v], [st6], name="bnstats")
                P.tt(dd_, st6[:, 1:2], st6[:, 4:5], ALU.subtract)
                P.tt(mv[:, 0:1], st6[:, 1:2], st6[:, 4:5], ALU.add)
                P.tt(m2_, st6[:, 2:3], st6[:, 5:6], ALU.add)
                P.ts(mv[:, 0:1], mv[:, 0:1], 0.5, None, ALU.mult)
                P.ts(m2_, m2_, 1.0 / 256, EPS, ALU.mult, ALU.add)
                P.tt(dd_, dd_, dd_, ALU.mult)
                P.stt(rstd, dd_, 0.25, m2_, ALU.mult, ALU.add)

            def s3(i):
                rstd = sm(i, 10, 11)
                P.act(rstd, rstd, AF.Sqrt)

            def s4(i):
                b = i % NB
                v = guv[b][:, 256:512]
                mv, rstd = sm(i, 8, 10), sm(i, 10, 11)
                P.add("dve", lambda e, o=rstd: e.reciprocal(o, o), [rstd], [rstd], name="recip")
                P.ts(vn[b], v, mv[:, 0:1], rstd, ALU.subtract, ALU.mult)
                P.tt(vn[b], vn[b], lng_b[:], ALU.mult)
                P.tt(vnb[b], vn[b], lnb_b[:], ALU.add)

            def s5(i):
                b = i % NB
                bmx = rot("mx", [2, 3])
                stt_[("mx", i)] = bmx
                for g in range(4):
                    P.mm(banks[bmx][:, g * 64:(g + 1) * 64], wsT[:, g, :], vnb[b][:, g * 64:(g + 1) * 64],
                         start=True, stop=True)

            def s6(i):
                b = i % NB
                bmx = stt_[("mx", i)]
                for g in range(4):
                    P.stt(ybt[b][:, g * 64:(g + 1) * 64], banks[bmx][:, g * 64:(g + 1) * 64], bsp[:, g:g + 1],
                          guv[b][:, g * 64:(g + 1) * 64], ALU.add, ALU.mult)

            def s7(i):
                b = i % NB
                btr = rot("trb", [4, 5])
                stt_[("tr", i)] = btr
                pb2 = banks[btr][:].bitcast(BF16)
                for c in range(2):
                    P.transpose(pb2[:, c * 128:(c + 1) * 128], ybt[b][:, c * 128:(c + 1) * 128], ident[:])

            def s8(i):
                pb2 = banks[stt_[("tr", i)]][:].bitcast(BF16)
                P.copy(ysT[:, 2:4, i * 128:(i + 1) * 128], pb2[:, 0:256].rearrange("p (c t) -> p c t", c=2), eng="act")

            pipeline(NT, [s0, s1, s2, s3, s4, s5, s6, s7, s8])

        def mixer_C(l):
            R.reset()
            qpad = R.take([4, S], BF16)
            P.memset(qpad, 0.0, eng="pool")
            kT = R.take([2, S], BF16)
            vtm = R.take([NT, 256], BF16)
            e_t = [R.take([512], F32) for _ in range(3)]
            sp_t = [R.take([512], BF16) for _ in range(3)]
            w_t = [R.take([512], BF16) for _ in range(3)]
            Sb = [R.take([512], BF16) for _ in range(2)]
            slot = wget(("Cqk", l))
            for qk in range(2):
                for c in range(2):
                    for G in range(4):
                        bk = rot("proj", [0, 1, 2, 3, 4, 5])
                        ts_ = slice(G * 512, (G + 1) * 512)
                        for k in range(8):
                            P.mm(banks[bk][:], slot[:, k, qk * 256 + c * 128: qk * 256 + (c + 1) * 128], xnT[:, k, ts_],
                                 start=(k == 0), stop=(k == 7))
                        if qk == 0:
                            P.act(qpad[0:64, 2 * c, ts_], banks[bk][0:64, :], AF.Copy, scale=0.125)
                            P.act(qpad[64:128, 2 * c + 1, ts_], banks[bk][64:128, :], AF.Copy, scale=0.125)
                        else:
                            P.copy(kT[:, c, ts_], banks[bk][:], eng="dve")
            slot = wget(("Cv", l))
            for i in range(NT):
                bk = rot("proj", [0, 1, 2, 3, 4, 5])
                for k in range(8):
                    P.mm(banks[bk][:, 0:256], xnT[:, k, i * 128:(i + 1) * 128], slot[:, k, 0:256],
                         start=(k == 0), stop=(k == 7))
                P.copy(vtm[:, i, :], banks[bk][:, 0:256], eng=("act" if i % 2 else "dve"))
            jobs = []
            for hp in range(2):
                for G in range(4):
                    for kb in range(4 * G + 3, -1, -1):
                        for hh in range(2):
                            jobs.append((hp, G, hh, kb))
            state = {}

            def stage(si, job, jn):
                hp, G, hh, kb = job
                nkb = 4 * G + 4
                OB = [4, 5] if (hp * 4 + G) % 2 == 0 else [6, 7]
                po = hh * 64
                r = kb - 4 * G
                c0 = 128 * r if r >= 0 else 0
                cs = slice(c0, 512)
                qs = slice(G * 512 + c0, (G + 1) * 512)
                ksl = slice(kb * 128, (kb + 1) * 128)
                first = (kb == nkb - 1)
                last = (kb == 0)
                a = jn % 3
                if si == 0:
                    zb = rot("z", [0, 1, 2, 3])
                    state[(job, "zb")] = zb
                    P.mm(banks[zb][:, cs], kT[:, hp, ksl], qpad[:, hp * 2 + hh, qs],
                         start=True, stop=False)
                    if r >= 0:
                        P.mm(banks[zb][:, c0:c0 + 128], ident[:], sbmask[:], start=False, stop=False)
                elif si == 1:
                    zb = state[(job, "zb")]
                    P.act(e_t[a][:, cs], banks[zb][:, cs], AF.Exp)
                    P.act(sp_t[a][:, cs], e_t[a][:, cs], AF.Ln, bias=1.0)
                elif si == 2:
                    zb = state[(job, "zb")]
                    P.mm(banks[zb][:, cs], uneg[:], sp_t[a][:, cs], start=False, stop=first,
                         skip_group_check=True)
                    if not first:
                        P.mm(banks[zb][:, cs], negones[:], Sb[hh][:, cs], start=False, stop=True,
                             skip_group_check=True)
                    if first:
                        if c0 > 0:
                            P.memset(Sb[hh][:, 0:c0], 0.0, eng="dve")
                        P.copy(Sb[hh][:, cs], sp_t[a][:, cs], eng="dve")
                    elif not last:
                        P.tt(Sb[hh][:, cs], Sb[hh][:, cs], sp_t[a][:, cs], ALU.add)
                elif si == 3:
                    zb = state[(job, "zb")]
                    P.act(w_t[a][:, cs], banks[zb][:, cs], AF.Exp)
                elif si == 4:
                    P.mm(banks[OB[hh]][:, cs], vtm[:, kb, hp * 128:(hp + 1) * 128], w_t[a][:, cs],
                         start=first, stop=last)
                    if last:
                        P.copy(ysT[po:po + 64, 4 + hp, G * 512:(G + 1) * 512], banks[OB[hh]][po:po + 64, :],
                               eng="dve")

            nst = 5
            for t in range(len(jobs) + nst - 1):
                for si in range(nst):
                    jn = t - si
                    if 0 <= jn < len(jobs):
                        stage(si, jobs[jn], jn)

        def mixer_D(l):
            R.reset()
            P.dma(gq_b[:], gq_d[l].partition_broadcast(128), eng="sp")
            P.dma(gk_b[:], gk_d[l].partition_broadcast(128), eng="sp")
            P.dma(bf_b[:], bf_d[l].partition_broadcast(128), eng="sp")
            P.ts(gq_b[:], gq_b[:], 0.125, None, ALU.mult)
            qpad = R.take([4, S], BF16)
            kpad = R.take([4, S], BF16)
            P.memset(qpad, 0.0, eng="pool")
            P.memset(kpad, 0.0, eng="pool")
            qpv = qpad.rearrange("p (c two) s -> p c two s", two=2)
            kpv = kpad.rearrange("p (c two) s -> p c two s", two=2)
            vtm = R.take([NT, 2, 192], BF16)
            P.memset(vtm[:, :, :, 64:128], 1.0, eng="pool")
            LF = R.take([NT, 4], F32)
            cpos = R.take([NT, 4], F32)
            carry = R.take([NT, 4], F32)
            r1 = carry
            r2 = LF
            cbf = R.take([NT, 4], BF16)
            off_shared = R.off
            ND = 4
            sq = [R.take([512], BF16) for _ in range(2)]
            qkc = [R.take([512], F32) for _ in range(ND)]
            qkn = [R.take([512], BF16) for _ in range(2)]
            R.off = off_shared
            TEq = R.take([NT, 4, 8], BF16)
            TEk = R.take([NT, 4, 8], BF16)
            p_t = [R.take([512], BF16) for _ in range(3)]
            rec_one = R.take([512], F32)
            rec = [rec_one, rec_one]
            slot_qk = wget(("Dqk", l))
            slot_vf = wget(("Dvf", l), ahead=0)
            stt_ = {}

            def smf(i, lo, hi):
                b = i % ND
                return small[:, 128 + b * 16 + lo: 128 + b * 16 + hi]

            def s0(i):
                tsl = slice(i * 128, (i + 1) * 128)
                bq = rot("fq", [0, 1])
                bv = rot("fv", [2, 3])
                stt_[("bq", i)] = bq
                stt_[("bv", i)] = bv
                for k in range(8):
                    P.mm(banks[bq][:], xnT[:, k, tsl], slot_qk[:, k, :], start=(k == 0), stop=(k == 7))
                for k in range(8):
                    P.mm(banks[bv][:, 0:260], xnT[:, k, tsl], slot_vf[:, k, 0:260], start=(k == 0), stop=(k == 7))

            def s1(i):
                bq, bv = stt_[("bq", i)], stt_[("bv", i)]
                P.act(sq[i % 2], banks[bq][:], AF.Square)
                P.copy(qkc[i % ND], banks[bq][:], eng="act")
                vsrc = banks[bv][:, 0:256].rearrange("p (c two d) -> p c two d", c=2, two=2)
                P.copy(vtm[:, i, :, 0:64], vsrc[:, :, 0, :], eng="act")
                P.copy(vtm[:, i, :, 128:192], vsrc[:, :, 1, :], eng="act")
                P.tt(smf(i, 0, 4), banks[bv][:, 256:260], bf_b[:], ALU.add)

            def s2(i):
                ssq = smf(i, 8, 16)
                P.add("dve", lambda e, o=ssq, i_=sq[i % 2]: e.tensor_reduce(o, i_.rearrange("p (j d) -> p j d", j=8), AX.X, ALU.add),
                      [sq[i % 2]], [ssq], name="tred")
                P.ts(ssq, ssq, 1.0 / 64, EPS, ALU.mult, ALU.add)

            def s3(i):
                fb = smf(i, 0, 4)
                ssq = smf(i, 8, 16)
                P.act(fb, fb, AF.Exp, scale=-1.0)
                P.act(LF[:, i, :], fb, AF.Ln, bias=1.0)
                P.act(ssq, ssq, AF.Sqrt)

            def s4(i):
                ssq = smf(i, 8, 16)
                P.add("dve", lambda e, o=ssq: e.reciprocal(o, o), [ssq], [ssq], name="recip")
                for j in range(8):
                    gbt = gq_b if j < 4 else gk_b
                    P.stt(qkn[i % 2][:, j * 64:(j + 1) * 64], qkc[i % ND][:, j * 64:(j + 1) * 64], ssq[:, j:j + 1], gbt[:],
                          ALU.mult, ALU.mult)

            def s5(i):
                btr = rot("ftr", [4, 5])
                stt_[("tr", i)] = btr
                pb = banks[btr][:].bitcast(BF16)
                for c in range(4):
                    P.transpose(pb[:, c * 128:(c + 1) * 128], qkn[i % 2][:, c * 128:(c + 1) * 128], ident[:])

            def s6(i):
                tsl = slice(i * 128, (i + 1) * 128)
                pb = banks[stt_[("tr", i)]][:].bitcast(BF16)
                P.copy(qpv[0:64, :, 0, tsl], pb[0:64, 0:256].rearrange("p (c t) -> p c t", c=2), eng="act")
                P.copy(qpv[64:128, :, 1, tsl], pb[64:128, 0:256].rearrange("p (c t) -> p c t", c=2), eng="act")
                P.copy(kpv[0:64, :, 0, tsl], pb[0:64, 256:512].rearrange("p (c t) -> p c t", c=2), eng="dve")
                P.copy(kpv[64:128, :, 1, tsl], pb[64:128, 256:512].rearrange("p (c t) -> p c t", c=2), eng="dve")

            pipeline(NT, [s0, s1, s2, s3, s4, s5, s6])
            wget(("G", l, 0), ahead=1)
            LF2 = LF.rearrange("p i h -> p (i h)")
            P.mm(banks[6][:, 0:64], tri_f[:], LF2, start=True, stop=True)
            P.mm(banks[7][:, 0:64], ones_f[:], LF2, start=True, stop=True)
            P.memset(carry[:, 0, :], 0.0, eng="dve")
            for i in range(1, NT):
                P.tt(carry[:, i, :], carry[:, i - 1, :], banks[7][:, (i - 1) * 4: i * 4], ALU.add)
            cp2 = cpos.rearrange("p i h -> p (i h)")
            P.tt(cp2, banks[6][:, 0:64], carry.rearrange("p i h -> p (i h)"), ALU.add)
            P.memset(TEq, 1.0, eng="dve")
            P.memset(TEk, 1.0, eng="dve")
            cur = cpos
            for t_, nxt in enumerate((r1, r2, None)):
                P.copy(cbf, cur, eng="dve")
                P.copy(TEk[:, :, :, 3 + t_], cbf, eng="dve")
                P.ts(TEq[:, :, :, t_], cbf, -1.0, None, ALU.mult)
                if nxt is not None:
                    P.tt(nxt, cur, cbf, ALU.subtract)
                    cur = nxt
            if f"cpos{l}" in dbg_d:
                P.dma(dbg_d[f"cpos{l}"], cp2, eng="sp")
            for (TE, XP) in ((TEq, qpad), (TEk, kpad)):
                for hd in range(4):
                    opo = 64 - (hd % 2) * 64
                    for G in range(4):
                        btr = rot("lb", [0, 1, 2, 3])
                        pb = banks[btr][:].bitcast(BF16)
                        for ii in range(4):
                            P.transpose(pb[0:8, ii * 128:(ii + 1) * 128], TE[:, G * 4 + ii, hd, :], ident[:])
                        P.copy(XP[opo:opo + 6, hd, G * 512:(G + 1) * 512], pb[0:6, 0:512], eng="dve")
            jobs = []
            for hp in range(2):
                for G in range(4):
                    for kb in range(4 * G + 3, -1, -1):
                        for hh in range(2):
                            jobs.append((hp, G, hh, kb))
            state = {}

            def stage(si, job, jn):
                hp, G, hh, kb = job
                nkb = 4 * G + 4
                OBn = [4, 5] if (hp * 4 + G) % 2 == 0 else [6, 7]
                po = hh * 64
                r = kb - 4 * G
                c0 = 128 * r if r >= 0 else 0
                cs = slice(c0, 512)
                qs = slice(G * 512 + c0, (G + 1) * 512)
                ksl = slice(kb * 128, (kb + 1) * 128)
                first = (kb == nkb - 1)
                last = (kb == 0)
                a = jn % 3
                if si == 0:
                    lb = rot("lb", [0, 1, 2, 3])
                    state[(job, "lb")] = lb
                    P.mm(banks[lb][:, cs], kpad[:, hp * 2 + hh, ksl], qpad[:, hp * 2 + hh, qs],
                         start=True, stop=(r < 0))
                    if r >= 0:
                        P.mm(banks[lb][:, c0:c0 + 128], ident[:], foxmask[:], start=False, stop=True)
                elif si == 1:
                    lb = state[(job, "lb")]
                    P.act(p_t[a][:, cs], banks[lb][:, cs], AF.Exp)
                elif si == 3:
                    opo = 64 - po
                    P.mm(banks[OBn[hh]][:, cs], vtm[:, kb, hp, hh * 64: hh * 64 + 128], p_t[a][:, cs],
                         start=first, stop=last)
                    if last:
                        rc = rec[hh]
                        P.add("dve", lambda e, o=rc[opo:opo + 64, :], i_=banks[OBn[hh]][opo:opo + 64, :]: e.reciprocal(o, i_),
                              [banks[OBn[hh]][opo:opo + 64, :]], [rc[opo:opo + 64, :]], name="recip")
                        P.tt(ysT[po:po + 64, 6 + hp, G * 512:(G + 1) * 512], banks[OBn[hh]][po:po + 64, :],
                             rc[opo:opo + 64, :], ALU.mult)

            nst = 4
            for t in range(len(jobs) + nst - 1):
                for si in range(nst):
                    jn = t - si
                    if 0 <= jn < len(jobs):
                        stage(si, jobs[jn], jn)

        def merge_out(l):
            R.reset()
            mT = R.take([8, S], BF16)
            off_after_mT = R.off
            wb = [R.take([4, 2, 128], BF16) for _ in range(2)]
            sg = [R.take([512], F32) for _ in range(2)]
            macc = [R.take([512], F32) for _ in range(2)]
            tmp = [R.take([512], F32) for _ in range(2)]
            n = 0
            for dc in range(8):
                w_b = wb[dc % 2]
                for i in range(4):
                    P.dma(w_b[:, i, :, :], w_br_d[l][i, :, dc * 128:(dc + 1) * 128].rearrange("(j p) c -> p j c", p=128),
                          eng="pool")
                slot = wget(("G", l, dc))
                for G in range(4):
                    ts_ = slice(G * 512, (G + 1) * 512)
                    mc = macc[(dc * 4 + G) % 2]
                    for i in range(4):
                        gbk = rot("gate", [0, 1, 2, 3])
                        bbk = rot("br", [4, 5, 6, 7])
                        for k in range(8):
                            P.mm(banks[gbk][:], slot[:, k, i * 128:(i + 1) * 128], xnT[:, k, ts_],
                                 start=(k == 0), stop=(k == 7))
                        for j in range(2):
                            P.mm(banks[bbk][:], w_b[:, i, j, :], ysT[:, i * 2 + j, ts_], start=(j == 0), stop=(j == 1))
                        s_ = sg[n % 2]
                        t_ = tmp[n % 2]
                        n += 1
                        P.act(s_, banks[gbk][:], AF.Sigmoid)
                        if i == 0:
                            P.tt(mc, s_, banks[bbk][:], ALU.mult)
                        else:
                            P.tt(t_, s_, banks[bbk][:], ALU.mult)
                            if i < 3:
                                P.tt(mc, mc, t_, ALU.add)
                            else:
                                P.tt(mT[:, dc, ts_], mc, t_, ALU.add)
            dump(f"merged{l}", mT, "fm")
            R.off = off_after_mT
            nb = norm_begin(gffn_d[l])
            for hf in range(2):
                slot = wget(("O", l, hf))
                for i in range(NT):
                    bk = rot("wo", [2, 3, 4, 5, 6, 7])
                    for k in range(8):
                        P.mm(banks[bk][:], mT[:, k, i * 128:(i + 1) * 128], slot[:, k, :], start=(k == 0), stop=(k == 7))
                    hs = h[:, i, hf * 512:(hf + 1) * 512]
                    P.tt(hs, hs, banks[bk][:], ALU.add)
                    if hf == 1:
                        norm_push(nb, i)
            norm_flush(nb)
            dump(f"hmix{l}", h[:], "tm")

        def ffn(l, is_last, next_g=None):
            R.reset()
            hidT = ysT
            WoF2 = [R.take([8, D], BF16) for _ in range(2)]
            sg = [R.take([512], F32) for _ in range(2)]
            n = 0
            for ps_ in range(3):
                fcs = FFN_PASSES[ps_]
                nf = len(fcs)
                WoF = WoF2[ps_ % 2]
                for hf in range(2):
                    P.dma(WoF[:, 0:nf, hf * 512:(hf + 1) * 512],
                          w_fo_d[l][fcs[0] * 128:(fcs[-1] + 1) * 128, hf * 512:(hf + 1) * 512].rearrange("(f p) c -> p f c", p=128),
                          eng="pool")
                for q in range(0, nf, 2):
                    slot = wget(("F", l, ps_, q // 2))
                    for qq, fc in enumerate(fcs[q:q + 2]):
                        fl = q + qq
                        for G in range(4):
                            ts_ = slice(G * 512, (G + 1) * 512)
                            gbk = rot("gate", [0, 1, 2, 3])
                            ubk = rot("br", [4, 5, 6, 7])
                            for k in range(8):
                                P.mm(banks[gbk][:], slot[:, k, qq * 256: qq * 256 + 128], xnT[:, k, ts_],
                                     start=(k == 0), stop=(k == 7))
                            for k in range(8):
                                P.mm(banks[ubk][:], slot[:, k, qq * 256 + 128: qq * 256 + 256], xnT[:, k, ts_],
                                     start=(k == 0), stop=(k == 7))
                            s_ = sg[n % 2]
                            n += 1
                            P.act(s_, banks[gbk][:], AF.Silu)
                            P.tt(hidT[:, fl, ts_], s_, banks[ubk][:], ALU.mult)
                nb = None
                if ps_ == 2 and next_g is not None:
                    nb = norm_begin(next_g)
                for i in range(NT):
                    for hf in range(2):
                        bk = rot("wo", [2, 3, 4, 5, 6, 7])
                        for fl in range(nf):
                            P.mm(banks[bk][:], hidT[:, fl, i * 128:(i + 1) * 128], WoF[:, fl, hf * 512:(hf + 1) * 512],
                                 start=(fl == 0), stop=(fl == nf - 1))
                        hs = h[:, i, hf * 512:(hf + 1) * 512]
                        P.tt(hs, hs, banks[bk][:], ALU.add)
                    if is_last and ps_ == 2:
                        P.dma(out_d.rearrange("(n p) d -> p n d", p=128)[:, i, :], h[:, i, :], eng="sp")
                    if nb is not None:
                        norm_push(nb, i)
                if nb is not None:
                    norm_flush(nb)

        stop_after = [s_ for s_ in dbg if s_.startswith("stop:")]
        stop_after = stop_after[0][5:] if stop_after else None

        def finish_early():
            for i in range(NT):
                P.dma(out_d.rearrange("(n p) d -> p n d", p=128)[:, i, :], h[:, i, :], eng="sp")

        done = False
        rmsnorm(gmix_d[0])
        for l in range(depth):
            dump(f"xnT{l}", xnT[:], "fm")
            mixer_A(l)
            dump(f"ya{l}", ysT[:, 0:2, :], "fm")
            mixer_B(l)
            dump(f"yb{l}", ysT[:, 2:4, :], "fm")
            mixer_C(l)
            dump(f"yc{l}", ysT[:, 4:6, :], "fm")
            mixer_D(l)
            dump(f"yd{l}", ysT[:, 6:8, :], "fm")
            merge_out(l)
            if stop_after == f"M{l}":
                finish_early(); done = True; break
            ffn(l, is_last=(l == depth - 1), next_g=(gmix_d[l + 1] if l + 1 < depth else None))
            dump(f"h{l}", h[:], "tm")
        if not done and depth < DEPTH:
            finish_early()
        P.emit(st)
        nc._prog_stats = (len(P.ops), P.n_sems)
    return nc


def make_in_maps(inputs):
    f = lambda a: np.ascontiguousarray(np.asarray(a, dtype=np.float32))
    shared = {
        "gmix": f(inputs["norm_mix_g"]),
        "w_in": f(inputs["w_in"]),
        "wconvT": f(np.transpose(np.asarray(inputs["w_conv"]), (0, 2, 1))),
        "w_sp": f(inputs["w_spatial"]),
        "b_spT": f(np.transpose(np.asarray(inputs["b_spatial"]), (0, 2, 1))),
        "lng": f(inputs["gmlp_ln_g"]),
        "lnb": f(inputs["gmlp_ln_b"]),
        "gq": f(inputs["fox_q_norm_g"]),
        "gk": f(inputs["fox_k_norm_g"]),
        "bfg": f(inputs["fox_forget_b"]),
        "w_br": f(inputs["w_branch"]),
        "w_out": f(inputs["w_out"]),
        "gffn": f(inputs["norm_ffn_g"]),
        "w_fi": f(inputs["w_ffn_in"]),
        "w_fo": f(inputs["w_ffn_out"]),
    }
    x = np.asarray(inputs["x"], dtype=np.float32)
    return [dict(shared, x=np.ascontiguousarray(x[b])) for b in range(8)]


_NC_CACHE = {}


def kernel(**inputs):
    if "nc" not in _NC_CACHE:
        _NC_CACHE["nc"] = build_nc()
    nc = _NC_CACHE["nc"]
    in_maps = make_in_maps(inputs)
    res = run_bass_kernel_spmd(nc, in_maps, core_ids=list(range(8)))
    out = np.stack([np.asarray(r["out"], dtype=np.float32) for r in res.results], axis=0)
    return out
```

```python
import numpy as np
from contextlib import ExitStack
from concourse.bass_utils import run_bass_kernel_spmd

import concourse.bass as bass
import concourse.mybir as mybir

F32 = mybir.dt.float32
BF16 = mybir.dt.bfloat16
AF = mybir.ActivationFunctionType
ALU = mybir.AluOpType
AX = mybir.AxisListType

ENGINES = ["pe", "act", "dve", "pool", "sp"]
SEM_MAX = 30000
DMA_SEMS_PER_Q = 8


def _esize(dt):
    return mybir.dt.size(dt)


def _region(ap):
    sp = str(ap.space)
    if "SB" not in sp and "PSUM" not in sp:
        return None
    pat = ap.ap
    es = _esize(ap.dtype)
    pstride = pat[0][0]
    npart = pat[0][1]
    off = ap.offset
    if pstride > 0:
        p0 = off // pstride
        f0 = off % pstride
    else:
        p0 = 0
        f0 = off
    ext = 1
    for st, cnt in pat[1:]:
        ext += abs(st) * (cnt - 1)
    if "PSUM" in sp:
        return (ap.tensor.name, 0, 128, 0, 1 << 30)
    return (ap.tensor.name, p0, p0 + npart, f0 * es, (f0 + ext) * es)


class Op:
    __slots__ = ("idx", "eng", "fn", "reads", "writes", "is_dma", "deps",
                 "signaled", "sem", "val", "name", "small")


class Prog:
    def __init__(self, nc, same_engine_sync=False):
        self.nc = nc
        self.ops = []
        self.recs = {}
        self.same_engine_sync = same_engine_sync

    def add(self, eng, fn, reads=(), writes=(), is_dma=False, name=""):
        op = Op()
        op.idx = len(self.ops)
        op.eng = eng
        op.fn = fn
        op.is_dma = is_dma
        op.name = name
        op.signaled = False
        op.sem = None
        op.val = 0
        op.small = False
        for a in writes:
            n_el = 1
            for st_, cnt_ in a.ap[1:]:
                n_el *= cnt_
            if n_el < 128:
                op.small = True
        rr = [r for r in (_region(a) for a in reads) if r is not None]
        ww = [r for r in (_region(a) for a in writes) if r is not None]
        ww = ww + [r for r in rr if r[4] == (1 << 30)]
        rr = [r for r in rr if r[4] != (1 << 30)]
        deps = set()
        for (tn, plo, phi, blo, bhi) in rr:
            for rec in self.recs.get(tn, ()):
                if rec[5] and rec[0] < phi and plo < rec[1] and rec[2] < bhi and blo < rec[3]:
                    deps.add(rec[4])
        for (tn, plo, phi, blo, bhi) in ww:
            for rec in self.recs.get(tn, ()):
                if rec[0] < phi and plo < rec[1] and rec[2] < bhi and blo < rec[3]:
                    deps.add(rec[4])
        for (tn, plo, phi, blo, bhi) in ww:
            lst = self.recs.setdefault(tn, [])
            lst[:] = [rec for rec in lst if not (plo <= rec[0] and rec[1] <= phi and blo <= rec[2] and rec[3] <= bhi)]
            lst.append([plo, phi, blo, bhi, op.idx, True])
        for (tn, plo, phi, blo, bhi) in rr:
            lst = self.recs.setdefault(tn, [])
            if not is_dma:
                lst[:] = [rec for rec in lst if not ((not rec[5]) and (rec[4] == op.idx or (
                                                     self.ops[rec[4]].eng == eng and not self.ops[rec[4]].is_dma))
                                                     and plo <= rec[0] and rec[1] <= phi
                                                     and blo <= rec[2] and rec[3] <= bhi)]
            lst.append([plo, phi, blo, bhi, op.idx, False])
        deps.discard(op.idx)
        need = []
        for d in deps:
            dop = self.ops[d]
            if dop.eng == eng and not dop.is_dma and not is_dma and not self.same_engine_sync \
                    and (eng == "pe" or (eng != "pool" and not dop.small)):
                continue
            need.append(d)
            dop.signaled = True
        op.deps = need
        if is_dma:
            op.signaled = True
        self.ops.append(op)
        return op

    def mm(self, out, lhsT, rhs, start=True, stop=True, **kw):
        reads = [lhsT, rhs] + ([] if start else [out])
        return self.add("pe", lambda e: e.matmul(out, lhsT, rhs, start=start, stop=stop, **kw),
                        reads, [out], name="mm")

    def transpose(self, out, in_, ident):
        return self.add("pe", lambda e: e.transpose(out, in_, ident), [in_, ident], [out], name="tr")

    def act(self, out, in_, func, bias=None, scale=None, accum_out=None):
        kw = {}
        reads = [in_]
        writes = [out]
        if bias is not None:
            kw["bias"] = bias
            if not isinstance(bias, (int, float)):
                reads.append(bias)
        if scale is not None:
            kw["scale"] = scale
            if not isinstance(scale, (int, float)):
                reads.append(scale)
        if accum_out is not None:
            kw["accum_out"] = accum_out
            writes.append(accum_out)
        return self.add("act", lambda e: e.activation(out, in_, func, **kw), reads, writes, name="act")

    def tt(self, out, in0, in1, op, eng="dve"):
        return self.add(eng, lambda e: e.tensor_tensor(out, in0, in1, op), [in0, in1], [out], name="tt")

    def ts(self, out, in0, s1, s2, op0, op1=None, eng="dve"):
        reads = [in0]
        for s in (s1, s2):
            if s is not None and not isinstance(s, (int, float)):
                reads.append(s)
        if op1 is None:
            return self.add(eng, lambda e: e.tensor_scalar(out, in0, s1, None, op0), reads, [out], name="ts")
        return self.add(eng, lambda e: e.tensor_scalar(out, in0, s1, s2, op0, op1), reads, [out], name="ts")

    def stt(self, out, in0, scalar, in1, op0, op1):
        reads = [in0, in1]
        if not isinstance(scalar, (int, float)):
            reads.append(scalar)
        return self.add("dve", lambda e: e.scalar_tensor_tensor(out, in0, scalar, in1, op0, op1),
                        reads, [out], name="stt")

    def copy(self, out, in_, eng="dve"):
        if eng == "act":
            return self.add("act", lambda e: e.copy(out, in_), [in_], [out], name="copy")
        return self.add(eng, lambda e: e.tensor_copy(out, in_), [in_], [out], name="copy")

    def memset(self, ap, val, eng="dve"):
        return self.add(eng, lambda e: e.memset(ap, val), [], [ap], name="memset")

    def dma(self, out, in_, eng="sp", **kw):
        return self.add(eng, lambda e: e.dma_start(out=out, in_=in_, **kw), [in_], [out],
                        is_dma=True, name="dma")

    def emit(self, stack):
        nc = self.nc
        counters = {e: 0 for e in ENGINES}
        eng_sems = {e: [] for e in ENGINES}
        dma_sems = {e: [] for e in ENGINES}
        dma_cnt = {e: 0 for e in ENGINES}
        dma_semval = {}
        dma_hist = {e: [] for e in ENGINES}
        for op in self.ops:
            if op.is_dma:
                q = op.eng
                j = dma_cnt[q] % DMA_SEMS_PER_Q
                if len(dma_sems[q]) <= j:
                    dma_sems[q].append(stack.enter_context(nc.semaphore(f"d_{q}_{j}")))
                sem = dma_sems[q][j]
                v = dma_semval.get((q, j), 0) + 16
                dma_semval[(q, j)] = v
                op.sem = sem
                op.val = v
                if dma_cnt[q] >= DMA_SEMS_PER_Q:
                    prev = dma_hist[q][dma_cnt[q] - DMA_SEMS_PER_Q]
                    if prev.idx not in op.deps:
                        op.deps.append(prev.idx)
                dma_hist[q].append(op)
                dma_cnt[q] += 1
            elif op.signaled:
                e = op.eng
                c = counters[e]
                k = c // SEM_MAX
                if len(eng_sems[e]) <= k:
                    eng_sems[e].append(stack.enter_context(nc.semaphore(f"c_{e}_{k}")))
                op.sem = eng_sems[e][k]
                op.val = c % SEM_MAX + 1
                counters[e] = c + 1
        self.n_sems = sum(len(v) for v in eng_sems.values()) + sum(len(v) for v in dma_sems.values())
        block = stack.enter_context(nc.Block())
        ops = self.ops

        def run(engname):
            def body(e):
                waited = {}
                last = None
                for op in ops:
                    if op.eng != engname:
                        continue
                    for d in sorted(op.deps):
                        dop = ops[d]
                        key = id(dop.sem)
                        if waited.get(key, 0) >= dop.val:
                            continue
                        e.wait_ge(dop.sem, dop.val)
                        waited[key] = dop.val
                    inst = op.fn(e)
                    if op.sem is not None:
                        inst.then_inc(op.sem, 16 if op.is_dma else 1)
                    last = op
                for j, sem in enumerate(dma_sems[engname]):
                    v = dma_semval.get((engname, j), 0)
                    if v:
                        e.wait_ge(sem, v)
            return body

        block.tensor(run("pe"))
        block.scalar(run("act"))
        block.vector(run("dve"))
        block.gpsimd(run("pool"))
        block.sync(run("sp"))


S = 2048
D = 1024
NT = 16
NIN = 6916
FF = 2816
DEPTH = 2
EPS = 1e-6
NEG = -30000.0
R_BYTES = 57 * 1024


class Carver:
    def __init__(self, t):
        self.t = t
        self.off = 0

    def reset(self):
        self.off = 0

    def take(self, shape, dt):
        n = 1
        for s_ in shape:
            n *= s_
        nb = n * mybir.dt.size(dt)
        nb_al = (nb + 31) // 32 * 32
        assert self.off + nb_al <= R_BYTES, (self.off, nb_al)
        a = self.t[:, self.off // 4:(self.off + nb_al) // 4]
        self.off += nb_al
        if dt != F32:
            a = a.bitcast(dt)
        a = a[:, 0:n]
        if len(shape) == 2:
            a = a.rearrange("p (a b) -> p a b", a=shape[0])
        elif len(shape) == 3:
            a = a.rearrange("p (a b c) -> p a b c", a=shape[0], b=shape[1])
        return a


def build_nc(depth=DEPTH, dbg=()):
    nc = bass.Bass("TRN2", target_bir_lowering=False)

    def din(name, shape):
        return nc.dram_tensor(name, shape, F32, kind="ExternalInput").ap()

    x_d = din("x", [S, D])
    gmix_d = din("gmix", [DEPTH, D])
    w_in_d = din("w_in", [DEPTH, D, NIN])
    wconvT_d = din("wconvT", [DEPTH, 256, 3])
    w_sp_d = din("w_sp", [DEPTH, 4, 128, 128])
    b_spT_d = din("b_spT", [DEPTH, 128, 4])
    lng_d = din("lng", [DEPTH, 256])
    lnb_d = din("lnb", [DEPTH, 256])
    gq_d = din("gq", [DEPTH, 64])
    gk_d = din("gk", [DEPTH, 64])
    bf_d = din("bfg", [DEPTH, 4])
    w_br_d = din("w_br", [DEPTH, 4, 256, D])
    w_out_d = din("w_out", [DEPTH, D, D])
    gffn_d = din("gffn", [DEPTH, D])
    w_fi_d = din("w_fi", [DEPTH, D, 2 * FF])
    w_fo_d = din("w_fo", [DEPTH, FF, D])
    out_d = nc.dram_tensor("out", [S, D], F32, kind="ExternalOutput").ap()
    dbg_d = {}
    for name in dbg:
        if name.startswith("cpos"):
            dbg_d[name] = nc.dram_tensor("dbg_" + name, [128, 64], F32, kind="ExternalOutput").ap()
        elif name.startswith("y") or name.startswith("merged") or name.startswith("xnT") or name.startswith("fq") or name.startswith("fk"):
            shp = [128, (8 if (name.startswith("merged") or name.startswith("xnT")) else 2) * S]
            dbg_d[name] = nc.dram_tensor("dbg_" + name, shp, F32, kind="ExternalOutput").ap()
        else:
            dbg_d[name] = nc.dram_tensor("dbg_" + name, [S, D], F32, kind="ExternalOutput").ap()

    with ExitStack() as st:
        def sb(name, shape, dt):
            return st.enter_context(nc.sbuf_tensor(name, shape, dt))

        banks = [st.enter_context(nc.psum_tensor(f"bank{i}", [128, 512], F32)) for i in range(8)]
        h = sb("h", [128, NT, D], F32)
        xnT = sb("xnT", [128, 8, S], BF16)
        ysT = sb("ysT", [128, 8, S], BF16)
        wslots = [sb(f"wslot{i}", [128, 8, 512], BF16) for i in range(2)]
        Rt = sb("R", [128, R_BYTES // 4], F32)
        R = Carver(Rt)
        ident = sb("ident", [128, 128], BF16)
        negones = sb("negones", [128, 128], BF16)
        ones_bf = sb("ones_bf", [128, 128], BF16)
        uneg = sb("uneg", [128, 128], BF16)
        tri_f = sb("tri_f", [128, 128], F32)
        ones_f = sb("ones_f", [128, 128], F32)
        zeros_bf = sb("zeros_bf", [128, 128], BF16)
        sbmask = sb("sbmask", [128, 128], BF16)
        foxmask = sb("foxmask", [128, 128], BF16)
        ss = sb("ss", [128, NT], F32)
        rs = sb("rs", [128, NT], F32)
        wc = sb("wc", [128, 2, 3], F32)
        bsp = sb("bsp", [128, 4], F32)
        lng_b = sb("lng_b", [128, 256], F32)
        lnb_b = sb("lnb_b", [128, 256], F32)
        gq_b = sb("gq_b", [128, 64], F32)
        gk_b = sb("gk_b", [128, 64], F32)
        bf_b = sb("bf_b", [128, 4], F32)
        small = sb("small", [128, 256], F32)

        P = Prog(nc)
        rot_state = {}

        def rot(name, lst):
            i = rot_state.get(name, 0)
            rot_state[name] = i + 1
            return lst[i % len(lst)]

        P.memset(ones_bf[:], 1.0, eng="pool")
        P.memset(negones[:], -1.0, eng="pool")
        P.memset(zeros_bf[:], 0.0, eng="pool")
        P.memset(ones_f[:], 1.0, eng="pool")

        def asel(out, in_, pattern, op, fill, base, cm):
            P.add("pool", lambda e: e.affine_select(out, in_, pattern, op, fill, base=base, channel_multiplier=cm),
                  [in_], [out], name="asel")

        asel(ident[:], ones_bf[:], [[-1, 128]], ALU.is_equal, 0.0, 0, 1)
        asel(uneg[:], negones[:], [[-1, 128]], ALU.is_ge, 0.0, 0, 1)
        asel(tri_f[:], ones_f[:], [[1, 128]], ALU.is_ge, 0.0, 0, -1)
        asel(sbmask[:], zeros_bf[:], [[1, 128]], ALU.is_gt, NEG, 0, -1)
        asel(foxmask[:], zeros_bf[:], [[1, 128]], ALU.is_ge, NEG, 0, -1)

        xr = x_d.rearrange("(n p) d -> p n d", p=128)
        for i0 in range(0, NT, 2):
            P.dma(h[:, i0:i0 + 2, :], xr[:, i0:i0 + 2, :], eng=("sp" if (i0 // 2) % 2 == 0 else "act"))

        chunks = []
        loaded = [0]
        slot_of = {}

        def wsrc(ap2d):
            return ap2d.rearrange("(k p) n -> p k n", p=128)

        def plan_layer(l):
            w = w_in_d[l]
            for j in range(2):
                chunks.append((("A", l, j), [(w[:, g * 256 + j * 128: g * 256 + j * 128 + 128], g * 128) for g in range(3)]))
            chunks.append((("B", l), [(w[:, 768:1280], 0)]))
            chunks.append((("Cqk", l), [(w[:, 1280:1792], 0)]))
            chunks.append((("Cv", l), [(w[:, 1792:2048], 0)]))
            chunks.append((("Dqk", l), [(w[:, 2048:2560], 0)]))
            chunks.append((("Dvf", l), [(w[:, 2560:2820], 0)]))
            for dc in range(8):
                chunks.append((("G", l, dc), [(w[:, 2820 + i * 1024 + dc * 128: 2820 + i * 1024 + dc * 128 + 128], i * 128) for i in range(4)]))
            for hf in range(2):
                chunks.append((("O", l, hf), [(w_out_d[l][:, hf * 512:(hf + 1) * 512], 0)]))
            wf = w_fi_d[l]
            for ps_ in range(3):
                fcs = FFN_PASSES[ps_]
                for q in range(0, len(fcs), 2):
                    srcs = []
                    for qq, fc in enumerate(fcs[q:q + 2]):
                        srcs.append((wf[:, fc * 128:(fc + 1) * 128], qq * 256))
                        srcs.append((wf[:, FF + fc * 128: FF + (fc + 1) * 128], qq * 256 + 128))
                    chunks.append((("F", l, ps_, q // 2), srcs))

        FFN_PASSES = [list(range(0, 8)), list(range(8, 15)), list(range(15, 22))]
        for l in range(depth):
            plan_layer(l)

        def ensure_loaded(upto):
            while loaded[0] <= min(upto, len(chunks) - 1):
                ci = loaded[0]
                key, srcs = chunks[ci]
                slot = wslots[ci % len(wslots)]
                for (src, off) in srcs:
                    n = src.shape[1]
                    P.dma(slot[:, :, off:off + n], wsrc(src), eng="pool")
                slot_of[key] = (ci, slot)
                loaded[0] += 1

        def wget(key, ahead=1):
            ci = None
            for i_, (k_, _) in enumerate(chunks):
                if k_ == key:
                    ci = i_
                    break
            assert ci is not None, key
            ensure_loaded(ci + ahead)
            cj, slot = slot_of[key]
            assert cj == ci
            return slot

        def dump(name, src_ap, kind):
            if name not in dbg_d:
                return
            R2 = dbgbuf
            if kind == "fm":
                n = src_ap.shape[1]
                for c in range(n):
                    for G in range(4):
                        P.copy(R2[:, 0:512], src_ap[:, c, G * 512:(G + 1) * 512], eng="dve")
                        P.dma(dbg_d[name][:, c * S + G * 512: c * S + (G + 1) * 512], R2[:, 0:512], eng="sp")
            else:
                for i in range(NT):
                    P.dma(dbg_d[name].rearrange("(n p) d -> p n d", p=128)[:, i, :], src_ap[:, i, :], eng="sp")

        dbgbuf = sb("dbgbuf", [128, 512], F32) if dbg else None

        def norm_begin(g_row):
            nb = {}
            nb["gb"] = R.take([D], F32)
            nb["sq"] = R.take([D], BF16)
            nb["xn"] = [R.take([D], BF16) for _ in range(2)]
            P.dma(nb["gb"], g_row.partition_broadcast(128), eng="sp")
            return nb

        def norm_stage(nb, si, i):
            r_ = rs[:, i:i + 1]
            if si == 0:
                P.act(nb["sq"], h[:, i, :], AF.Square, accum_out=ss[:, i:i + 1])
            elif si == 1:
                P.ts(r_, ss[:, i:i + 1], 1.0 / D, EPS, ALU.mult, ALU.add)
            elif si == 2:
                P.act(r_, r_, AF.Sqrt)
            elif si == 3:
                P.add("dve", lambda e, o=r_: e.reciprocal(o, o), [r_], [r_], name="recip")
            elif si == 4:
                P.stt(nb["xn"][i % 2], h[:, i, :], r_, nb["gb"], ALU.mult, ALU.mult)
            elif si == 5:
                xn = nb["xn"][i % 2]
                bk = rot("tr", [0, 1])
                nb[("bk", i)] = bk
                pb = banks[bk][:].bitcast(BF16)
                for k in range(8):
                    P.transpose(pb[:, k * 128:(k + 1) * 128], xn[:, k * 128:(k + 1) * 128], ident[:])
            elif si == 6:
                pb = banks[nb[("bk", i)]][:].bitcast(BF16)
                P.copy(xnT[:, :, i * 128:(i + 1) * 128], pb.rearrange("p (k t) -> p k t", k=8), eng="act")

        NST = 7

        def norm_push(nb, t):
            for si in range(NST):
                j = t - si
                if 0 <= j < NT:
                    norm_stage(nb, si, j)

        def norm_flush(nb):
            for t in range(NT, NT + NST - 1):
                norm_push(nb, t)

        def rmsnorm(g_row):
            R.reset()
            nb = norm_begin(g_row)
            for i in range(NT):
                norm_push(nb, i)
            norm_flush(nb)

        def mixer_A(l):
            R.reset()
            P.dma(wc[:], wconvT_d[l].rearrange("(j p) k -> p j k", p=128), eng="sp")
            xcbs = [R.take([S + 2], BF16) for _ in range(2)]
            cxs = [R.take([512], F32) for _ in range(2)]
            cbs = [R.take([512], F32) for _ in range(2)]
            dg = R.take([2, 3, 128], BF16)
            for j in range(2):
                for k in range(3):
                    P.ts(dg[:, j, k, :], ident[:], wc[:, j, k:k + 1], None, ALU.mult)
            for xb_ in xcbs:
                P.memset(xb_[:, 0:2], 0.0, eng="dve")
            units = [(j, G) for j in range(2) for G in range(4)]
            ust = {}

            def front(n):
                j, G = units[n]
                if G == 0:
                    ust[("slot", j)] = wget(("A", l, j))
                slot = ust[("slot", j)]
                bset = rot("convset", [[0, 1, 2], [3, 4, 5]])
                ts_ = slice(G * 512, (G + 1) * 512)
                for gi in range(3):
                    for k in range(8):
                        P.mm(banks[bset[gi]][:], slot[:, k, gi * 128:(gi + 1) * 128], xnT[:, k, ts_],
                             start=(k == 0), stop=(k == 7))
                a = n % 2
                P.copy(cxs[a], banks[bset[2]][:], eng="act")
                P.tt(xcbs[j][:, 2 + G * 512: 2 + (G + 1) * 512], banks[bset[1]][:], cxs[a], ALU.mult)
                P.copy(cbs[a], banks[bset[0]][:], eng="act")

            def back(n):
                j, G = units[n]
                a = n % 2
                ts_ = slice(G * 512, (G + 1) * 512)
                by = rot("convy", [6, 7])
                for k in range(3):
                    P.mm(banks[by][:], dg[:, j, k, :], xcbs[j][:, G * 512 + k: G * 512 + k + 512],
                         start=(k == 0), stop=(k == 2))
                P.tt(ysT[:, 0 + j, ts_], banks[by][:], cbs[a], ALU.mult)

            for n in range(len(units) + 1):
                if n < len(units):
                    front(n)
                if n >= 1:
                    back(n - 1)

        def pipeline(n_items, stages):
            ns_ = len(stages)
            for t_ in range(n_items + ns_ - 1):
                for si_ in range(ns_):
                    j_ = t_ - si_
                    if 0 <= j_ < n_items:
                        stages[si_](j_)

        def mixer_B(l):
            R.reset()
            P.dma(bsp[:], b_spT_d[l], eng="sp")
            P.dma(lng_b[:], lng_d[l].partition_broadcast(128), eng="sp")
            P.dma(lnb_b[:], lnb_d[l].partition_broadcast(128), eng="sp")
            wsf = R.take([4, 128], F32)
            wsb = R.take([4, 128], BF16)
            wsT = R.take([4, 128], BF16)
            P.dma(wsf, w_sp_d[l].rearrange("g t s -> t g s"), eng="sp")
            for g in range(4):
                asel(wsb[:, g, :], wsf[:, g, :], [[-1, 128]], ALU.is_ge, 0.0, 0, 1)
            bk = rot("tr", [0, 1])
            pb = banks[bk][:].bitcast(BF16)
            for g in range(4):
                P.transpose(pb[:, g * 128:(g + 1) * 128], wsb[:, g, :], ident[:])
            P.copy(wsT, pb[:, 0:512].rearrange("p (g t) -> p g t", g=4), eng="dve")
            NB = 6
            guv = [R.take([512], F32) for _ in range(NB)]
            vn = [R.take([256], F32) for _ in range(NB)]
            vnb = [R.take([256], BF16) for _ in range(NB)]
            ybt = [R.take([256], BF16) for _ in range(NB)]
            slot = wget(("B", l))
            stt_ = {}

            def sm(i, lo, hi):
                b = i % NB
                return small[:, b * 16 + lo: b * 16 + hi]

            def s0(i):
                buv = rot("uv", [0, 1])
                stt_[("uv", i)] = buv
                for k in range(8):
                    P.mm(banks[buv][:], xnT[:, k, i * 128:(i + 1) * 128], slot[:, k, :], start=(k == 0), stop=(k == 7))

            def s1(i):
                P.act(guv[i % NB], banks[stt_[("uv", i)]][:], AF.Gelu_apprx_tanh)

            def s2(i):
                v = guv[i % NB][:, 256:512]
                st6, mv, rstd, dd_, m2_ = sm(i, 0, 6), sm(i, 8, 10), sm(i, 10, 11), sm(i, 11, 12), sm(i, 12, 13)
                P.add("dve", lambda e, o=st6, i_=v: e.bn_stats(o, i_), [v], [st6], name="bnstats")
                P.tt(dd_, st6[:, 1:2], st6[:, 4:5], ALU.subtract)
                P.tt(mv[:, 0:1], st6[:, 1:2], st6[:, 4:5], ALU.add)
                P.tt(m2_, st6[:, 2:3], st6[:, 5:6], ALU.add)
                P.ts(mv[:, 0:1], mv[:, 0:1], 0.5, None, ALU.mult)
                P.ts(m2_, m2_, 1.0 / 256, EPS, ALU.mult, ALU.add)
                P.tt(dd_, dd_, dd_, ALU.mult)
                P.stt(rstd, dd_, 0.25, m2_, ALU.mult, ALU.add)

            def s3(i):
                rstd = sm(i, 10, 11)
                P.act(rstd, rstd, AF.Sqrt)

            def s4(i):
                b = i % NB
                v = guv[b][:, 256:512]
                mv, rstd = sm(i, 8, 10), sm(i, 10, 11)
                P.add("dve", lambda e, o=rstd: e.reciprocal(o, o), [rstd], [rstd], name="recip")
                P.stt(vn[b], v, mv[:, 0:1], lng_b[:], ALU.subtract, ALU.mult)
                P.stt(vnb[b], vn[b], rstd, lnb_b[:], ALU.mult, ALU.add)

            def s5(i):
                b = i % NB
                bmx = rot("mx", [2, 3])
                stt_[("mx", i)] = bmx
                for g in range(4):
                    P.mm(banks[bmx][:, g * 64:(g + 1) * 64], wsT[:, g, :], vnb[b][:, g * 64:(g + 1) * 64],
                         start=True, stop=True)

            def s6(i):
                b = i % NB
                bmx = stt_[("mx", i)]
                for g in range(4):
                    P.stt(ybt[b][:, g * 64:(g + 1) * 64], banks[bmx][:, g * 64:(g + 1) * 64], bsp[:, g:g + 1],
                          guv[b][:, g * 64:(g + 1) * 64], ALU.add, ALU.mult)

            def s7(i):
                b = i % NB
                btr = rot("trb", [4, 5])
                stt_[("tr", i)] = btr
                pb2 = banks[btr][:].bitcast(BF16)
                for c in range(2):
                    P.transpose(pb2[:, c * 128:(c + 1) * 128], ybt[b][:, c * 128:(c + 1) * 128], ident[:])

            def s8(i):
                pb2 = banks[stt_[("tr", i)]][:].bitcast(BF16)
                P.copy(ysT[:, 2:4, i * 128:(i + 1) * 128], pb2[:, 0:256].rearrange("p (c t) -> p c t", c=2), eng="act")

            pipeline(NT, [s0, s1, s2, s3, s4, s5, s6, s7, s8])

        def mixer_C(l):
            R.reset()
            qpad = R.take([4, S], BF16)
            P.memset(qpad, 0.0, eng="pool")
            kT = R.take([2, S], BF16)
            vtm = R.take([NT, 256], BF16)
            e_t = [R.take([512], F32) for _ in range(3)]
            sp_t = [R.take([512], BF16) for _ in range(3)]
            w_t = [R.take([512], BF16) for _ in range(3)]
            Sb = [R.take([512], BF16) for _ in range(2)]
            slot = wget(("Cqk", l))
            for qk in range(2):
                for c in range(2):
                    for G in range(4):
                        bk = rot("proj", [0, 1, 2, 3, 4, 5])
                        ts_ = slice(G * 512, (G + 1) * 512)
                        for k in range(8):
                            P.mm(banks[bk][:], slot[:, k, qk * 256 + c * 128: qk * 256 + (c + 1) * 128], xnT[:, k, ts_],
                                 start=(k == 0), stop=(k == 7))
                        if qk == 0:
                            P.act(qpad[0:64, 2 * c, ts_], banks[bk][0:64, :], AF.Copy, scale=0.125)
                            P.act(qpad[64:128, 2 * c + 1, ts_], banks[bk][64:128, :], AF.Copy, scale=0.125)
                        else:
                            P.copy(kT[:, c, ts_], banks[bk][:], eng="dve")
            slot = wget(("Cv", l))
            for i in range(NT):
                bk = rot("proj", [0, 1, 2, 3, 4, 5])
                for k in range(8):
                    P.mm(banks[bk][:, 0:256], xnT[:, k, i * 128:(i + 1) * 128], slot[:, k, 0:256],
                         start=(k == 0), stop=(k == 7))
                P.copy(vtm[:, i, :], banks[bk][:, 0:256], eng=("act" if i % 2 else "dve"))
            jobs = []
            for hp in range(2):
                for G in range(4):
                    for kb in range(4 * G + 3, -1, -1):
                        for hh in range(2):
                            jobs.append((hp, G, hh, kb))
            state = {}

            def stage(si, job, jn):
                hp, G, hh, kb = job
                nkb = 4 * G + 4
                OB = [4, 5] if (hp * 4 + G) % 2 == 0 else [6, 7]
                po = hh * 64
                r = kb - 4 * G
                c0 = 128 * r if r >= 0 else 0
                cs = slice(c0, 512)
                qs = slice(G * 512 + c0, (G + 1) * 512)
                ksl = slice(kb * 128, (kb + 1) * 128)
                first = (kb == nkb - 1)
                last = (kb == 0)
                a = jn % 3
                if si == 0:
                    zb = rot("z", [0, 1, 2, 3])
                    state[(job, "zb")] = zb
                    P.mm(banks[zb][:, cs], kT[:, hp, ksl], qpad[:, hp * 2 + hh, qs],
                         start=True, stop=False)
                    if r >= 0:
                        P.mm(banks[zb][:, c0:c0 + 128], ident[:], sbmask[:], start=False, stop=False)
                elif si == 1:
                    zb = state[(job, "zb")]
                    P.act(e_t[a][:, cs], banks[zb][:, cs], AF.Exp)
                    P.act(sp_t[a][:, cs], e_t[a][:, cs], AF.Ln, bias=1.0)
                elif si == 2:
                    zb = state[(job, "zb")]
                    P.mm(banks[zb][:, cs], uneg[:], sp_t[a][:, cs], start=False, stop=first,
                         skip_group_check=True)
                    if not first:
                        P.mm(banks[zb][:, cs], negones[:], Sb[hh][:, cs], start=False, stop=True,
                             skip_group_check=True)
                    if first:
                        if c0 > 0:
                            P.memset(Sb[hh][:, 0:c0], 0.0, eng="dve")
                        P.copy(Sb[hh][:, cs], sp_t[a][:, cs], eng="dve")
                    elif not last:
                        P.tt(Sb[hh][:, cs], Sb[hh][:, cs], sp_t[a][:, cs], ALU.add)
                elif si == 3:
                    zb = state[(job, "zb")]
                    P.act(w_t[a][:, cs], banks[zb][:, cs], AF.Exp)
                elif si == 4:
                    P.mm(banks[OB[hh]][:, cs], vtm[:, kb, hp * 128:(hp + 1) * 128], w_t[a][:, cs],
                         start=first, stop=last)
                    if last:
                        P.copy(ysT[po:po + 64, 4 + hp, G * 512:(G + 1) * 512], banks[OB[hh]][po:po + 64, :],
                               eng="dve")

            nst = 5
            for t in range(len(jobs) + nst - 1):
                for si in range(nst):
                    jn = t - si
                    if 0 <= jn < len(jobs):
                        stage(si, jobs[jn], jn)

        def mixer_D(l):
            R.reset()
            P.dma(gq_b[:], gq_d[l].partition_broadcast(128), eng="sp")
            P.dma(gk_b[:], gk_d[l].partition_broadcast(128), eng="sp")
            P.dma(bf_b[:], bf_d[l].partition_broadcast(128), eng="sp")
            P.ts(gq_b[:], gq_b[:], 0.125, None, ALU.mult)
            qpad = R.take([4, S], BF16)
            kpad = R.take([4, S], BF16)
            P.memset(qpad, 0.0, eng="pool")
            P.memset(kpad, 0.0, eng="pool")
            qpv = qpad.rearrange("p (c two) s -> p c two s", two=2)
            kpv = kpad.rearrange("p (c two) s -> p c two s", two=2)
            vtm = R.take([NT, 2, 192], BF16)
            P.memset(vtm[:, :, :, 64:128], 1.0, eng="pool")
            LF = R.take([NT, 4], F32)
            cpos = R.take([NT, 4], F32)
            carry = R.take([NT, 4], F32)
            r1 = carry
            r2 = LF
            cbf = R.take([NT, 4], BF16)
            off_shared = R.off
            ND = 4
            sq = [R.take([512], BF16) for _ in range(2)]
            qkc = [R.take([512], F32) for _ in range(ND)]
            qkn = [R.take([512], BF16) for _ in range(2)]
            R.off = off_shared
            TEq = R.take([NT, 4, 8], BF16)
            TEk = R.take([NT, 4, 8], BF16)
            p_t = [R.take([512], BF16) for _ in range(3)]
            rec_one = R.take([512], F32)
            rec = [rec_one, rec_one]
            slot_qk = wget(("Dqk", l))
            slot_vf = wget(("Dvf", l), ahead=0)
            stt_ = {}

            def smf(i, lo, hi):
                b = i % ND
                return small[:, 128 + b * 16 + lo: 128 + b * 16 + hi]

            def s0(i):
                tsl = slice(i * 128, (i + 1) * 128)
                bq = rot("fq", [0, 1])
                bv = rot("fv", [2, 3])
                stt_[("bq", i)] = bq
                stt_[("bv", i)] = bv
                for k in range(8):
                    P.mm(banks[bq][:], xnT[:, k, tsl], slot_qk[:, k, :], start=(k == 0), stop=(k == 7))
                for k in range(8):
                    P.mm(banks[bv][:, 0:260], xnT[:, k, tsl], slot_vf[:, k, 0:260], start=(k == 0), stop=(k == 7))

            def s1(i):
                bq, bv = stt_[("bq", i)], stt_[("bv", i)]
                P.act(sq[i % 2], banks[bq][:], AF.Square)
                P.copy(qkc[i % ND], banks[bq][:], eng="act")
                vsrc = banks[bv][:, 0:256].rearrange("p (c two d) -> p c two d", c=2, two=2)
                P.copy(vtm[:, i, :, 0:64], vsrc[:, :, 0, :], eng="act")
                P.copy(vtm[:, i, :, 128:192], vsrc[:, :, 1, :], eng="act")
                P.tt(smf(i, 0, 4), banks[bv][:, 256:260], bf_b[:], ALU.add)

            def s2(i):
                ssq = smf(i, 8, 16)
                P.add("dve", lambda e, o=ssq, i_=sq[i % 2]: e.tensor_reduce(o, i_.rearrange("p (j d) -> p j d", j=8), AX.X, ALU.add),
                      [sq[i % 2]], [ssq], name="tred")
                P.ts(ssq, ssq, 1.0 / 64, EPS, ALU.mult, ALU.add)

            def s3(i):
                fb = smf(i, 0, 4)
                ssq = smf(i, 8, 16)
                P.act(fb, fb, AF.Exp, scale=-1.0)
                P.act(LF[:, i, :], fb, AF.Ln, bias=1.0)
                P.act(ssq, ssq, AF.Sqrt)

            def s4(i):
                ssq = smf(i, 8, 16)
                P.add("dve", lambda e, o=ssq: e.reciprocal(o, o), [ssq], [ssq], name="recip")
                for j in range(8):
                    gbt = gq_b if j < 4 else gk_b
                    P.stt(qkn[i % 2][:, j * 64:(j + 1) * 64], qkc[i % ND][:, j * 64:(j + 1) * 64], ssq[:, j:j + 1], gbt[:],
                          ALU.mult, ALU.mult)

            def s5(i):
                btr = rot("ftr", [4, 5])
                stt_[("tr", i)] = btr
                pb = banks[btr][:].bitcast(BF16)
                for c in range(4):
                    P.transpose(pb[:, c * 128:(c + 1) * 128], qkn[i % 2][:, c * 128:(c + 1) * 128], ident[:])

            def s6(i):
                tsl = slice(i * 128, (i + 1) * 128)
                pb = banks[stt_[("tr", i)]][:].bitcast(BF16)
                P.copy(qpv[0:64, :, 0, tsl], pb[0:64, 0:256].rearrange("p (c t) -> p c t", c=2), eng="act")
                P.copy(qpv[64:128, :, 1, tsl], pb[64:128, 0:256].rearrange("p (c t) -> p c t", c=2), eng="act")
                P.copy(kpv[0:64, :, 0, tsl], pb[0:64, 256:512].rearrange("p (c t) -> p c t", c=2), eng="dve")
                P.copy(kpv[64:128, :, 1, tsl], pb[64:128, 256:512].rearrange("p (c t) -> p c t", c=2), eng="dve")

            pipeline(NT, [s0, s1, s2, s3, s4, s5, s6])
            wget(("G", l, 0), ahead=1)
            LF2 = LF.rearrange("p i h -> p (i h)")
            P.mm(banks[6][:, 0:64], tri_f[:], LF2, start=True, stop=True)
            P.mm(banks[7][:, 0:64], ones_f[:], LF2, start=True, stop=True)
            P.memset(carry[:, 0, :], 0.0, eng="dve")
            for i in range(1, NT):
                P.tt(carry[:, i, :], carry[:, i - 1, :], banks[7][:, (i - 1) * 4: i * 4], ALU.add)
            cp2 = cpos.rearrange("p i h -> p (i h)")
            P.tt(cp2, banks[6][:, 0:64], carry.rearrange("p i h -> p (i h)"), ALU.add)
            P.memset(TEq, 1.0, eng="dve")
            P.memset(TEk, 1.0, eng="dve")
            cur = cpos
            for t_, nxt in enumerate((r1, r2, None)):
                P.copy(cbf, cur, eng="dve")
                P.copy(TEk[:, :, :, 3 + t_], cbf, eng="dve")
                P.ts(TEq[:, :, :, t_], cbf, -1.0, None, ALU.mult)
                if nxt is not None:
                    P.tt(nxt, cur, cbf, ALU.subtract)
                    cur = nxt
            if f"cpos{l}" in dbg_d:
                P.dma(dbg_d[f"cpos{l}"], cp2, eng="sp")
            for (TE, XP) in ((TEq, qpad), (TEk, kpad)):
                for hd in range(4):
                    opo = 64 - (hd % 2) * 64
                    for G in range(4):
                        btr = rot("lb", [0, 1, 2, 3])
                        pb = banks[btr][:].bitcast(BF16)
                        for ii in range(4):
                            P.transpose(pb[0:8, ii * 128:(ii + 1) * 128], TE[:, G * 4 + ii, hd, :], ident[:])
                        P.copy(XP[opo:opo + 6, hd, G * 512:(G + 1) * 512], pb[0:6, 0:512], eng="dve")
            jobs = []
            for hp in range(2):
                for G in range(4):
                    for kb in range(4 * G + 3, -1, -1):
                        for hh in range(2):
                            jobs.append((hp, G, hh, kb))
            state = {}

            def stage(si, job, jn):
                hp, G, hh, kb = job
                nkb = 4 * G + 4
                OBn = [4, 5] if (hp * 4 + G) % 2 == 0 else [6, 7]
                po = hh * 64
                r = kb - 4 * G
                c0 = 128 * r if r >= 0 else 0
                cs = slice(c0, 512)
                qs = slice(G * 512 + c0, (G + 1) * 512)
                ksl = slice(kb * 128, (kb + 1) * 128)
                first = (kb == nkb - 1)
                last = (kb == 0)
                a = jn % 3
                if si == 0:
                    lb = rot("lb", [0, 1, 2, 3])
                    state[(job, "lb")] = lb
                    P.mm(banks[lb][:, cs], kpad[:, hp * 2 + hh, ksl], qpad[:, hp * 2 + hh, qs],
                         start=True, stop=(r < 0))
                    if r >= 0:
                        P.mm(banks[lb][:, c0:c0 + 128], ident[:], foxmask[:], start=False, stop=True)
                elif si == 1:
                    lb = state[(job, "lb")]
                    P.act(p_t[a][:, cs], banks[lb][:, cs], AF.Exp)
                elif si == 3:
                    opo = 64 - po
                    P.mm(banks[OBn[hh]][:, cs], vtm[:, kb, hp, hh * 64: hh * 64 + 128], p_t[a][:, cs],
                         start=first, stop=last)
                    if last:
                        rc = rec[hh]
                        P.add("dve", lambda e, o=rc[opo:opo + 64, :], i_=banks[OBn[hh]][opo:opo + 64, :]: e.reciprocal(o, i_),
                              [banks[OBn[hh]][opo:opo + 64, :]], [rc[opo:opo + 64, :]], name="recip")
                        P.tt(ysT[po:po + 64, 6 + hp, G * 512:(G + 1) * 512], banks[OBn[hh]][po:po + 64, :],
                             rc[opo:opo + 64, :], ALU.mult)

            nst = 4
            for t in range(len(jobs) + nst - 1):
                for si in range(nst):
                    jn = t - si
                    if 0 <= jn < len(jobs):
                        stage(si, jobs[jn], jn)

        def merge_out(l):
            R.reset()
            mT = R.take([8, S], BF16)
            off_after_mT = R.off
            wb = [R.take([4, 2, 128], BF16) for _ in range(2)]
            sg = [R.take([512], F32) for _ in range(2)]
            macc = [R.take([512], F32) for _ in range(2)]
            tmp = [R.take([512], F32) for _ in range(2)]
            n = 0
            for dc in range(8):
                w_b = wb[dc % 2]
                for i in range(4):
                    P.dma(w_b[:, i, :, :], w_br_d[l][i, :, dc * 128:(dc + 1) * 128].rearrange("(j p) c -> p j c", p=128),
                          eng="pool")
                slot = wget(("G", l, dc))
                for G in range(4):
                    ts_ = slice(G * 512, (G + 1) * 512)
                    mc = macc[(dc * 4 + G) % 2]
                    for i in range(4):
                        gbk = rot("gate", [0, 1, 2, 3])
                        bbk = rot("br", [4, 5, 6, 7])
                        for k in range(8):
                            P.mm(banks[gbk][:], slot[:, k, i * 128:(i + 1) * 128], xnT[:, k, ts_],
                                 start=(k == 0), stop=(k == 7))
                        for j in range(2):
                            P.mm(banks[bbk][:], w_b[:, i, j, :], ysT[:, i * 2 + j, ts_], start=(j == 0), stop=(j == 1))
                        s_ = sg[n % 2]
                        t_ = tmp[n % 2]
                        n += 1
                        P.act(s_, banks[gbk][:], AF.Sigmoid)
                        if i == 0:
                            P.tt(mc, s_, banks[bbk][:], ALU.mult)
                        else:
                            P.tt(t_, s_, banks[bbk][:], ALU.mult)
                            if i < 3:
                                P.tt(mc, mc, t_, ALU.add)
                            else:
                                P.tt(mT[:, dc, ts_], mc, t_, ALU.add)
            dump(f"merged{l}", mT, "fm")
            R.off = off_after_mT
            nb = norm_begin(gffn_d[l])
            for hf in range(2):
                slot = wget(("O", l, hf))
                for i in range(NT):
                    bk = rot("wo", [2, 3, 4, 5, 6, 7])
                    for k in range(8):
                        P.mm(banks[bk][:], mT[:, k, i * 128:(i + 1) * 128], slot[:, k, :], start=(k == 0), stop=(k == 7))
                    hs = h[:, i, hf * 512:(hf + 1) * 512]
                    P.tt(hs, hs, banks[bk][:], ALU.add)
                    if hf == 1:
                        norm_push(nb, i)
            norm_flush(nb)
            dump(f"hmix{l}", h[:], "tm")

        def ffn(l, is_last, next_g=None):
            R.reset()
            hidT = ysT
            WoF2 = [R.take([8, D], BF16) for _ in range(2)]
            sg = [R.take([512], F32) for _ in range(2)]
            n = 0
            for ps_ in range(3):
                fcs = FFN_PASSES[ps_]
                nf = len(fcs)
                WoF = WoF2[ps_ % 2]
                for hf in range(2):
                    P.dma(WoF[:, 0:nf, hf * 512:(hf + 1) * 512],
                          w_fo_d[l][fcs[0] * 128:(fcs[-1] + 1) * 128, hf * 512:(hf + 1) * 512].rearrange("(f p) c -> p f c", p=128),
                          eng="pool")
                for q in range(0, nf, 2):
                    slot = wget(("F", l, ps_, q // 2))
                    for qq, fc in enumerate(fcs[q:q + 2]):
                        fl = q + qq
                        for G in range(4):
                            ts_ = slice(G * 512, (G + 1) * 512)
                            gbk = rot("gate", [0, 1, 2, 3])
                            ubk = rot("br", [4, 5, 6, 7])
                            for k in range(8):
                                P.mm(banks[gbk][:], slot[:, k, qq * 256: qq * 256 + 128], xnT[:, k, ts_],
                                     start=(k == 0), stop=(k == 7))
                            for k in range(8):
                                P.mm(banks[ubk][:], slot[:, k, qq * 256 + 128: qq * 256 + 256], xnT[:, k, ts_],
                                     start=(k == 0), stop=(k == 7))
                            s_ = sg[n % 2]
                            n += 1
                            P.act(s_, banks[gbk][:], AF.Silu)
                            P.tt(hidT[:, fl, ts_], s_, banks[ubk][:], ALU.mult)
                nb = None
                if ps_ == 2 and next_g is not None:
                    nb = norm_begin(next_g)
                for i in range(NT):
                    for hf in range(2):
                        bk = rot("wo", [2, 3, 4, 5, 6, 7])
                        for fl in range(nf):
                            P.mm(banks[bk][:], hidT[:, fl, i * 128:(i + 1) * 128], WoF[:, fl, hf * 512:(hf + 1) * 512],
                                 start=(fl == 0), stop=(fl == nf - 1))
                        hs = h[:, i, hf * 512:(hf + 1) * 512]
                        P.tt(hs, hs, banks[bk][:], ALU.add)
                    if is_last and ps_ == 2:
                        P.dma(out_d.rearrange("(n p) d -> p n d", p=128)[:, i, :], h[:, i, :], eng="sp")
                    if nb is not None:
                        norm_push(nb, i)
                if nb is not None:
                    norm_flush(nb)

        stop_after = [s_ for s_ in dbg if s_.startswith("stop:")]
        stop_after = stop_after[0][5:] if stop_after else None

        def finish_early():
            for i in range(NT):
                P.dma(out_d.rearrange("(n p) d -> p n d", p=128)[:, i, :], h[:, i, :], eng="sp")

        done = False
        rmsnorm(gmix_d[0])
        for l in range(depth):
            dump(f"xnT{l}", xnT[:], "fm")
            mixer_A(l)
            dump(f"ya{l}", ysT[:, 0:2, :], "fm")
            mixer_B(l)
            dump(f"yb{l}", ysT[:, 2:4, :], "fm")
            mixer_C(l)
            dump(f"yc{l}", ysT[:, 4:6, :], "fm")
            mixer_D(l)
            dump(f"yd{l}", ysT[:, 6:8, :], "fm")
            merge_out(l)
            if stop_after == f"M{l}":
                finish_early(); done = True; break
            ffn(l, is_last=(l == depth - 1), next_g=(gmix_d[l + 1] if l + 1 < depth else None))
            dump(f"h{l}", h[:], "tm")
        if not done and depth < DEPTH:
            finish_early()
        P.emit(st)
        nc._prog_stats = (len(P.ops), P.n_sems)
    return nc


def make_in_maps(inputs):
    f = lambda a: np.ascontiguousarray(np.asarray(a, dtype=np.float32))
    shared = {
        "gmix": f(inputs["norm_mix_g"]),
        "w_in": f(inputs["w_in"]),
        "wconvT": f(np.transpose(np.asarray(inputs["w_conv"]), (0, 2, 1))),
        "w_sp": f(inputs["w_spatial"]),
        "b_spT": f(np.transpose(np.asarray(inputs["b_spatial"]), (0, 2, 1))),
        "lng": f(inputs["gmlp_ln_g"]),
        "lnb": f(inputs["gmlp_ln_b"]),
        "gq": f(inputs["fox_q_norm_g"]),
        "gk": f(inputs["fox_k_norm_g"]),
        "bfg": f(inputs["fox_forget_b"]),
        "w_br": f(inputs["w_branch"]),
        "w_out": f(inputs["w_out"]),
        "gffn": f(inputs["norm_ffn_g"]),
        "w_fi": f(inputs["w_ffn_in"]),
        "w_fo": f(inputs["w_ffn_out"]),
    }
    x = np.asarray(inputs["x"], dtype=np.float32)
    return [dict(shared, x=np.ascontiguousarray(x[b])) for b in range(8)]


_NC_CACHE = {}


def kernel(**inputs):
    if "nc" not in _NC_CACHE:
        _NC_CACHE["nc"] = build_nc()
    nc = _NC_CACHE["nc"]
    in_maps = make_in_maps(inputs)
    res = run_bass_kernel_spmd(nc, in_maps, core_ids=list(range(8)))
    out = np.stack([np.asarray(r["out"], dtype=np.float32) for r in res.results], axis=0)
    return out
```
